# Optimizing a Trainium2 kernel written in Bass

```python
import math
import jax, jax.numpy as jnp
from jax import lax
import numpy as np

D_MODEL = 1024
BATCH = 4
SEQ = 4096
DEPTH = 4
DEC_BATCH = 8
DEC_SEQ = 4096
PAST_LEN = 128

GRID_W = 64
N_MIXERS = 2
N_SSM_LAYERS = (DEPTH + 1) // 2
N_ATTN_LAYERS = DEPTH // 2
D_FF = 2816
SSM_EXPAND = 2
D_INNER = SSM_EXPAND * D_MODEL
SSM_HEAD_DIM = 64
SSM_HEADS = D_INNER // SSM_HEAD_DIM
SSM_GROUPS = 8
SSM_HEADS_PER_GROUP = SSM_HEADS // SSM_GROUPS
D_STATE = 128
D_CONV = 5
CONV_DIM = D_INNER + 2 * SSM_GROUPS * D_STATE
SSM_IN_DIM = D_INNER + CONV_DIM + 2 * SSM_HEADS
SSD_CHUNK = 128
ATT_HEAD_DIM = 64
N_HEADS = D_MODEL // ATT_HEAD_DIM
N_KV_HEADS = 4
KV_REP = N_HEADS // N_KV_HEADS
QKV_DIM = (N_HEADS + 2 * N_KV_HEADS) * ATT_HEAD_DIM
AXIS_DIM = ATT_HEAD_DIM // 2
ROPE_THETA = 10000.0
Q_BLOCK = 128
EPS = 1e-6

kernel_name = 'hybrid_ssd_axial_gqa_encoder'


def rms_norm(x, g):
    xf = x.astype(jnp.float32)
    out = xf * lax.rsqrt(jnp.mean(xf * xf, axis=-1, keepdims=True) + EPS)
    return (out * g.astype(jnp.float32)).astype(x.dtype)


def swiglu(h, w_gate, w_up, w_down):
    return (jax.nn.silu(h @ w_gate) * (h @ w_up)) @ w_down


def depthwise_centred_conv(x, w, b):
    y = lax.conv_general_dilated(
        x, w[:, None, :].astype(x.dtype), window_strides=(1,),
        padding=[(D_CONV // 2, D_CONV // 2)],
        dimension_numbers=('NWC', 'WIO', 'NWC'),
        feature_group_count=x.shape[-1])
    return y + b.astype(x.dtype)


def ssd_scan(x, dt, A, Bm, Cm):
    b, T = x.shape[0], x.shape[1]
    nc = T // SSD_CHUNK

    def chunk(a):
        a = a.astype(jnp.float32)
        return jnp.moveaxis(a.reshape((b, nc, SSD_CHUNK) + a.shape[2:]), 1, 0)

    xc, dtc, Bc, Cc = chunk(x), chunk(dt), chunk(Bm), chunk(Cm)
    A = A.astype(jnp.float32)
    causal = jnp.tril(jnp.ones((SSD_CHUNK, SSD_CHUNK), dtype=bool))[None, :, :, None, None]

    def step(S, inp):
        xq, dtq, Bq, Cq = inp
        cs = jnp.cumsum(dtq * A, axis=1)
        diff = cs[:, :, None] - cs[:, None, :]
        L = jnp.exp(jnp.where(causal, diff, -jnp.inf))
        cb = jnp.einsum('bign,bjgn->bijg', Cq, Bq)
        y_in = jnp.einsum('bijg,bijgr,bjgr,bjgrp->bigrp', cb, L, dtq, xq)
        y_st = jnp.einsum('bign,bgrpn,bigr->bigrp', Cq, S, jnp.exp(cs))
        w_end = jnp.exp(cs[:, -1:] - cs) * dtq
        S_new = S * jnp.exp(cs[:, -1])[..., None, None] + jnp.einsum(
            'bjgn,bjgr,bjgrp->bgrpn', Bq, w_end, xq)
        return S_new, y_in + y_st

    S0 = jnp.zeros((b, SSM_GROUPS, SSM_HEADS_PER_GROUP, SSM_HEAD_DIM, D_STATE), jnp.float32)
    _, yc = lax.scan(step, S0, (xc, dtc, Bc, Cc))
    return jnp.moveaxis(yc, 0, 1).reshape(x.shape)


def bidirectional_ssd_mixer(h, w_in, conv_w, conv_b, dt_bias, A_log, D, norm_g, w_out):
    b, T, _ = h.shape
    proj = h @ w_in
    z = proj[..., :D_INNER]
    xbc = proj[..., D_INNER:D_INNER + CONV_DIM]
    dt_raw = proj[..., D_INNER + CONV_DIM:].reshape(b, T, 2, SSM_HEADS)
    xbc = jax.nn.silu(depthwise_centred_conv(xbc, conv_w, conv_b))
    xs = xbc[..., :D_INNER].reshape(b, T, SSM_GROUPS, SSM_HEADS_PER_GROUP, SSM_HEAD_DIM)
    Bm = xbc[..., D_INNER:D_INNER + SSM_GROUPS * D_STATE].reshape(b, T, SSM_GROUPS, D_STATE)
    Cm = xbc[..., D_INNER + SSM_GROUPS * D_STATE:].reshape(b, T, SSM_GROUPS, D_STATE)
    dt = jax.nn.softplus(dt_raw.astype(jnp.float32) + dt_bias.astype(jnp.float32))
    dt = dt.reshape(b, T, 2, SSM_GROUPS, SSM_HEADS_PER_GROUP)
    A = -jnp.exp(A_log.astype(jnp.float32)).reshape(2, SSM_GROUPS, SSM_HEADS_PER_GROUP)
    flip = lambda a: jnp.flip(a, axis=1)
    y_f = ssd_scan(xs, dt[:, :, 0], A[0], Bm, Cm)
    y_b = flip(ssd_scan(flip(xs), flip(dt[:, :, 1]), A[1], flip(Bm), flip(Cm)))
    Dg = D.astype(jnp.float32).reshape(SSM_GROUPS, SSM_HEADS_PER_GROUP)[:, :, None]
    y = y_f + y_b + Dg * xs.astype(jnp.float32)
    y = y.reshape(b, T, D_INNER) * jax.nn.silu(z.astype(jnp.float32))
    yg = y.reshape(b, T, SSM_GROUPS, D_INNER // SSM_GROUPS)
    yg = yg * lax.rsqrt(jnp.mean(yg * yg, axis=-1, keepdims=True) + EPS)
    y = yg.reshape(b, T, D_INNER) * norm_g.astype(jnp.float32)
    return y.astype(h.dtype) @ w_out


def axial_rope_tables(T):
    rows = T // GRID_W
    r_idx, c_idx = jnp.meshgrid(jnp.arange(rows), jnp.arange(GRID_W), indexing='ij')
    r_idx = r_idx.reshape(-1).astype(jnp.float32)
    c_idx = c_idx.reshape(-1).astype(jnp.float32)
    inv_freq = ROPE_THETA ** (-jnp.arange(0, AXIS_DIM, 2, dtype=jnp.float32) / AXIS_DIM)
    ang = jnp.stack([r_idx[:, None] * inv_freq, c_idx[:, None] * inv_freq], axis=1)
    return jnp.cos(ang)[:, :, None, :], jnp.sin(ang)[:, :, None, :]


def apply_axial_rope(x, cos, sin):
    xf = x.astype(jnp.float32)
    xr = xf.reshape(x.shape[:-1] + (2, 2, AXIS_DIM // 2))
    rot = jnp.stack([-xr[..., 1, :], xr[..., 0, :]], axis=-2)
    bshape = (1, x.shape[1]) + (1,) * (x.ndim - 3) + (2, 1, AXIS_DIM // 2)
    out = xr * cos.reshape(bshape) + rot * sin.reshape(bshape)
    return out.reshape(x.shape).astype(x.dtype)


def axial_gqa_attention(h, w_qkv, q_gain, k_gain, w_out):
    b, T, _ = h.shape
    qd, kd = N_HEADS * ATT_HEAD_DIM, N_KV_HEADS * ATT_HEAD_DIM
    qkv = h @ w_qkv
    q = qkv[..., :qd].reshape(b, T, N_KV_HEADS, KV_REP, ATT_HEAD_DIM)
    k = qkv[..., qd:qd + kd].reshape(b, T, N_KV_HEADS, ATT_HEAD_DIM)
    v = qkv[..., qd + kd:].reshape(b, T, N_KV_HEADS, ATT_HEAD_DIM)
    cos, sin = axial_rope_tables(T)
    q = apply_axial_rope(rms_norm(q, q_gain), cos, sin)
    k = apply_axial_rope(rms_norm(k, k_gain), cos, sin)
    n_blocks = T // Q_BLOCK
    q_blocks = jnp.moveaxis(q.reshape(b, n_blocks, Q_BLOCK, N_KV_HEADS, KV_REP, ATT_HEAD_DIM), 1, 0)
    scale = ATT_HEAD_DIM ** -0.5

    def attend(qb):
        s = jnp.einsum('bqgrd,bkgd->bgrqk', qb, k).astype(jnp.float32) * scale
        p = jax.nn.softmax(s, axis=-1).astype(v.dtype)
        return jnp.einsum('bgrqk,bkgd->bqgrd', p, v)

    o = lax.map(attend, q_blocks)
    o = jnp.moveaxis(o, 0, 1).reshape(b, T, qd)
    return o @ w_out


def trunk(x, norm_g, ffn_w_gate, ffn_w_up, ffn_w_down,
          ssm_w_in, ssm_conv_w, ssm_conv_b, ssm_dt_bias, ssm_A_log, ssm_D, ssm_norm_g, ssm_w_out,
          attn_w_qkv, attn_q_norm, attn_k_norm, attn_w_out, final_norm):
    for i in range(DEPTH):
        j = i // N_MIXERS
        x = x + 0.5 * swiglu(rms_norm(x, norm_g[i, 0]), ffn_w_gate[i, 0], ffn_w_up[i, 0], ffn_w_down[i, 0])
        h = rms_norm(x, norm_g[i, 1])
        if i % N_MIXERS == 0:
            x = x + bidirectional_ssd_mixer(h, ssm_w_in[j], ssm_conv_w[j], ssm_conv_b[j], ssm_dt_bias[j],
                                            ssm_A_log[j], ssm_D[j], ssm_norm_g[j], ssm_w_out[j])
        else:
            x = x + axial_gqa_attention(h, attn_w_qkv[j], attn_q_norm[j], attn_k_norm[j], attn_w_out[j])
        x = x + 0.5 * swiglu(rms_norm(x, norm_g[i, 2]), ffn_w_gate[i, 1], ffn_w_up[i, 1], ffn_w_down[i, 1])
    return rms_norm(x, final_norm)


def setup_inputs(seed: int = 0) -> dict:
    key = jax.random.key(seed)
    ks = jax.random.split(key, 20)
    nrm = lambda k, shape, s: jax.random.normal(k, shape, jnp.float32) * s
    x_prompt = nrm(ks[0], (BATCH, SEQ, D_MODEL), 1.0)
    x_sample = nrm(ks[1], (DEC_BATCH, DEC_SEQ, D_MODEL), 1.0)
    norm_g = 1.0 + nrm(ks[2], (DEPTH, 3, D_MODEL), 0.02)
    ffn_w_gate = nrm(ks[3], (DEPTH, 2, D_MODEL, D_FF), D_MODEL ** -0.5)
    ffn_w_up = nrm(ks[4], (DEPTH, 2, D_MODEL, D_FF), D_MODEL ** -0.5)
    ffn_w_down = nrm(ks[5], (DEPTH, 2, D_FF, D_MODEL), D_FF ** -0.5)
    ssm_w_in = nrm(ks[6], (N_SSM_LAYERS, D_MODEL, SSM_IN_DIM), D_MODEL ** -0.5)
    ssm_conv_w = nrm(ks[7], (N_SSM_LAYERS, D_CONV, CONV_DIM), D_CONV ** -0.5)
    ssm_conv_b = nrm(ks[8], (N_SSM_LAYERS, CONV_DIM), 0.02)
    dt0 = jnp.exp(jax.random.uniform(ks[9], (N_SSM_LAYERS, 2, SSM_HEADS), jnp.float32,
                                     minval=math.log(1e-3), maxval=math.log(1e-1)))
    ssm_dt_bias = dt0 + jnp.log(-jnp.expm1(-dt0))
    ssm_A_log = jnp.log(jax.random.uniform(ks[10], (N_SSM_LAYERS, 2, SSM_HEADS), jnp.float32,
                                           minval=1.0, maxval=16.0))
    ssm_D = 1.0 + nrm(ks[11], (N_SSM_LAYERS, SSM_HEADS), 0.1)
    ssm_norm_g = 1.0 + nrm(ks[12], (N_SSM_LAYERS, D_INNER), 0.02)
    ssm_w_out = nrm(ks[13], (N_SSM_LAYERS, D_INNER, D_MODEL), D_INNER ** -0.5)
    attn_w_qkv = nrm(ks[14], (N_ATTN_LAYERS, D_MODEL, QKV_DIM), D_MODEL ** -0.5)
    attn_q_norm = 1.0 + nrm(ks[15], (N_ATTN_LAYERS, ATT_HEAD_DIM), 0.02)
    attn_k_norm = 1.0 + nrm(ks[16], (N_ATTN_LAYERS, ATT_HEAD_DIM), 0.02)
    attn_w_out = nrm(ks[17], (N_ATTN_LAYERS, N_HEADS * ATT_HEAD_DIM, D_MODEL), (N_HEADS * ATT_HEAD_DIM) ** -0.5)
    final_norm = 1.0 + nrm(ks[18], (D_MODEL,), 0.02)
    return {'x_prompt': x_prompt, 'x_sample': x_sample, 'norm_g': norm_g,
            'ffn_w_gate': ffn_w_gate, 'ffn_w_up': ffn_w_up, 'ffn_w_down': ffn_w_down,
            'ssm_w_in': ssm_w_in, 'ssm_conv_w': ssm_conv_w, 'ssm_conv_b': ssm_conv_b,
            'ssm_dt_bias': ssm_dt_bias, 'ssm_A_log': ssm_A_log, 'ssm_D': ssm_D,
            'ssm_norm_g': ssm_norm_g, 'ssm_w_out': ssm_w_out,
            'attn_w_qkv': attn_w_qkv, 'attn_q_norm': attn_q_norm, 'attn_k_norm': attn_k_norm,
            'attn_w_out': attn_w_out, 'final_norm': final_norm}


def reference(x_prompt, x_sample, norm_g, ffn_w_gate, ffn_w_up, ffn_w_down,
              ssm_w_in, ssm_conv_w, ssm_conv_b, ssm_dt_bias, ssm_A_log, ssm_D, ssm_norm_g, ssm_w_out,
              attn_w_qkv, attn_q_norm, attn_k_norm, attn_w_out, final_norm):
    y_prompt = trunk(x_prompt, norm_g, ffn_w_gate, ffn_w_up, ffn_w_down,
                     ssm_w_in, ssm_conv_w, ssm_conv_b, ssm_dt_bias, ssm_A_log, ssm_D, ssm_norm_g, ssm_w_out,
                     attn_w_qkv, attn_q_norm, attn_k_norm, attn_w_out, final_norm)
    y_sample = trunk(x_sample, norm_g, ffn_w_gate, ffn_w_up, ffn_w_down,
                     ssm_w_in, ssm_conv_w, ssm_conv_b, ssm_dt_bias, ssm_A_log, ssm_D, ssm_norm_g, ssm_w_out,
                     attn_w_qkv, attn_q_norm, attn_k_norm, attn_w_out, final_norm)
    return (y_prompt, y_sample)
```

```python
import numpy as np
from contextlib import ExitStack
import concourse.bass as bass
import concourse.mybir as mybir
from concourse.bass_utils import run_bass_kernel_spmd

F32 = mybir.dt.float32
BF16 = mybir.dt.bfloat16
ALU = mybir.AluOpType
AF = mybir.ActivationFunctionType

D_MODEL = 1024
D_FF = 2816
D_INNER = 2048
SSM_HEADS = 32
SSM_GROUPS = 8
D_STATE = 128
CONV_DIM = 4096
SSM_IN_DIM = 6208
N_HEADS = 16
N_KV = 4
HD = 64
EPS = 1e-6
GRID_W = 64
ROPE_THETA = 10000.0
NDMASEM = 12
SSM_SKIP = set()
PRO_SLOT = 0
PRO_RATE = 1
NCH = 3
PCS_DEDICATED = True


class Res:
    __slots__ = ("w", "r", "excl")

    def __init__(self, excl=False):
        self.w = None
        self.r = {}
        self.excl = excl


class Tile:
    def __init__(self, t, excl=False):
        self.t = t
        self.res = {}
        self.excl = excl

    def R(self, key=None):
        if self.excl:
            key = None
        r = self.res.get(key)
        if r is None:
            r = self.res[key] = Res(self.excl)
        return r


class PsumHandle:
    def __init__(self, bank):
        self.bank = bank

    @property
    def t(self):
        assert self.bank.owner is self, "stale PSUM handle"
        return self.bank.t

    def R(self, key=None):
        assert self.bank.owner is self, "stale PSUM handle"
        return self.bank.R(key)


class EW:
    def __init__(self, name, eng, sem):
        self.name = name
        self.eng = eng
        self.sem = sem
        self.count = 0
        self.waited = {}
        self.pending = False


class Ctx:
    def __init__(self, nc, es):
        self.nc = nc
        self.es = es
        self.E = {}
        for name, eng in (("pe", nc.tensor), ("act", nc.scalar), ("dve", nc.vector),
                          ("pool", nc.gpsimd), ("sp", nc.sync)):
            sem = es.enter_context(nc.semaphore("s_" + name))
            self.E[name] = EW(name, eng, sem)
        self.dsem = {}
        for q in ("sp", "pool"):
            self.dsem[q] = [[es.enter_context(nc.semaphore(f"d_{q}{i}")), 0] for i in range(NDMASEM)]
        self.dnext = {"sp": 0, "pool": 0}
        self.semid = {}
        self.psum_banks = []
        self.psum_next = 0
        self.uid = 0

    def _wait(self, E, dep):
        if dep is None:
            return
        sem, val = dep
        if sem is E.sem and val > E.count:
            return
        k = id(sem)
        if E.waited.get(k, 0) >= val:
            return
        E.eng.wait_ge(sem, val)
        E.waited[k] = val

    def _hazards(self, E, R, W):
        for r in R:
            self._wait(E, r.w)
            if r.excl:
                for d in r.r.values():
                    if d[0] is not E.sem:
                        self._wait(E, d)
        for w in W:
            self._wait(E, w.w)
            for d in w.r.values():
                self._wait(E, d)

    def _record(self, dep, R, W):
        k = id(dep[0])
        for r in R:
            old = r.r.get(k)
            if old is None or old[1] < dep[1]:
                r.r[k] = dep
        for w in W:
            w.w = dep
            w.r = {}

    def op(self, en, fn, R=(), W=(), inc=True):
        E = self.E[en]
        self._hazards(E, R, W)
        ins = fn(E.eng)
        dep = (E.sem, E.count + 1)
        if inc:
            ins.then_inc(E.sem, 1)
            E.count += 1
            E.pending = False
        else:
            E.pending = True
        self._record(dep, R, W)
        return ins

    def dma(self, q, out, in_, R=(), W=()):
        E = self.E[q]
        self._hazards(E, R, W)
        i = self.dnext[q]
        self.dnext[q] = (i + 1) % NDMASEM
        ent = self.dsem[q][i]
        if ent[1] > 0:
            self._wait(E, (ent[0], ent[1]))
        ent[1] += 16
        E.eng.dma_start(out=out, in_=in_).then_inc(ent[0], 16)
        dep = (ent[0], ent[1])
        self._record(dep, R, W)

    def barrier(self):
        deps = []
        for E in self.E.values():
            assert not E.pending
            if E.count:
                deps.append((E.sem, E.count))
        for q in self.dsem:
            for sem, val in self.dsem[q]:
                if val:
                    deps.append((sem, val))
        for E in self.E.values():
            for d in deps:
                if d[0] is not E.sem:
                    self._wait(E, d)

    def sb(self, es, shape, dt, name=None):
        self.uid += 1
        t = es.enter_context(self.nc.sbuf_tensor(f"{name or 't'}_{self.uid}", list(shape), dt))
        return Tile(t)

    def psum(self):
        b = self.psum_banks[self.psum_next]
        self.psum_next = (self.psum_next + 1) % len(self.psum_banks)
        h = PsumHandle(b)
        b.owner = h
        return h


def rope_tables(T):
    rows = T // GRID_W
    r_idx, c_idx = np.meshgrid(np.arange(rows), np.arange(GRID_W), indexing="ij")
    r_idx = r_idx.reshape(-1).astype(np.float32)
    c_idx = c_idx.reshape(-1).astype(np.float32)
    inv_freq = (np.float32(ROPE_THETA) ** (-np.arange(0, 32, 2, dtype=np.float32) / np.float32(32))).astype(np.float32)
    ang = np.stack([r_idx[:, None] * inv_freq, c_idx[:, None] * inv_freq], axis=1).astype(np.float32)
    cos = np.cos(ang).astype(np.float32)
    sin = np.sin(ang).astype(np.float32)
    cosT = np.zeros((64, T), np.float32)
    sinT = np.zeros((64, T), np.float32)
    for a in range(2):
        for s in range(2):
            cosT[a * 32 + s * 16:a * 32 + s * 16 + 16, :] = cos[:, a, :].T
            sinT[a * 32 + s * 16:a * 32 + s * 16 + 16, :] = sin[:, a, :].T
    cosT = np.concatenate([cosT, cosT], 0)
    sinT = np.concatenate([sinT, sinT], 0)
    Rm = np.zeros((128, 128), np.float32)
    for hh in range(2):
        for a in range(2):
            for i in range(16):
                d0 = hh * 64 + a * 32 + i
                d1 = d0 + 16
                Rm[d1, d0] = -1.0
                Rm[d0, d1] = 1.0
    return np.ascontiguousarray(cosT), np.ascontiguousarray(sinT), Rm


class Builder:
    def __init__(self, T, NSEQ, layers, final=True):
        self.T = T
        self.NSEQ = NSEQ
        self.layers = layers
        self.final = final
        self.n_ffn = len(layers)
        self.n_ssm = sum(1 for l in layers if l[1] == "ssm")
        self.n_att = sum(1 for l in layers if l[1] == "attn")

    def build(self):
        T, NSEQ = self.T, self.NSEQ
        L = len(self.layers)
        nc = bass.Bass("TRN2", target_bir_lowering=False)
        self.nc = nc
        din = lambda n, s: nc.dram_tensor(n, list(s), F32, kind="ExternalInput").ap()
        dsc = lambda n, s, dt=F32: nc.dram_tensor(n, list(s), dt, kind="Internal").ap()
        I = self.I = {}
        I["x"] = din("x", (NSEQ, T, D_MODEL))
        I["norm_g"] = din("norm_g", (128, L * 3 * 8))
        I["ffn_w_gate"] = din("ffn_w_gate", (L, 2, D_MODEL, D_FF))
        I["ffn_w_up"] = din("ffn_w_up", (L, 2, D_MODEL, D_FF))
        I["ffn_w_down"] = din("ffn_w_down", (L, 2, D_FF, D_MODEL))
        ns, na = max(self.n_ssm, 1), max(self.n_att, 1)
        I["ssm_w_in"] = din("ssm_w_in", (ns, D_MODEL, SSM_IN_DIM))
        I["ssm_conv_w"] = din("ssm_conv_w", (128, ns * 32 * 5))
        I["ssm_conv_b"] = din("ssm_conv_b", (128, ns * 32))
        I["ssm_dtb_col"] = din("ssm_dtb_col", (64, ns))
        I["ssm_dtb_row"] = din("ssm_dtb_row", (ns, 64))
        I["ssm_alog_row"] = din("ssm_alog_row", (ns, 64))
        I["ssm_D_col"] = din("ssm_D_col", (128, ns * 16))
        I["ssm_ng_col"] = din("ssm_ng_col", (128, ns * 16))
        I["ssm_w_out"] = din("ssm_w_out", (ns, D_INNER, D_MODEL))
        I["attn_w_qkv"] = din("attn_w_qkv", (na, D_MODEL, 1536))
        I["attn_qg_col"] = din("attn_qg_col", (128, na))
        I["attn_kg_col"] = din("attn_kg_col", (128, na))
        I["attn_w_out"] = din("attn_w_out", (na, D_MODEL, D_MODEL))
        I["final_g"] = din("final_g", (128, 8))
        I["ident"] = din("ident", (128, 128))
        I["cosT"] = din("cosT", (128, T))
        I["sinT"] = din("sinT", (128, T))
        I["Rm"] = din("Rm", (128, 128))
        I["tri"] = din("tri", (128, 128))
        I["triT"] = din("triT", (128, 128))
        I["sel"] = din("sel", (32, 32 * 128))
        I["sel3"] = din("sel3", (96, 32 * 128))
        self.y = nc.dram_tensor("y", [NSEQ, T, D_MODEL], F32, kind="ExternalOutput").ap()
        self.XT = dsc("XT", (NSEQ, D_MODEL, T))
        S = self.S = {}
        S["wg"] = dsc("wg_b", (L, 2, 22, 128, 8, 128), BF16)
        S["wu"] = dsc("wu_b", (L, 2, 22, 128, 8, 128), BF16)
        S["wd"] = dsc("wd_b", (L, 2, 8, 128, 22, 128), BF16)
        S["w_in"] = dsc("w_in_b", (ns, 48, 128, 8, 128), BF16)
        S["w_dt"] = dsc("w_dt_b", (ns, 128, 8, 64), BF16)
        S["w_sout"] = dsc("w_sout_b", (ns, 8, 128, 16, 128), BF16)
        S["w_q"] = dsc("w_q_b", (na, 8, 128, 8, 128), BF16)
        S["w_k"] = dsc("w_k_b", (na, 2, 128, 8, 128), BF16)
        S["w_v"] = dsc("w_v_b", (na, 128, 8, 256), BF16)
        S["w_aout"] = dsc("w_aout_b", (na, 8, 128, 8, 128), BF16)
        if self.n_ssm:
            S["zs"] = dsc("zs", (NSEQ, D_INNER, T), BF16)
            S["xbc"] = dsc("xbc", (NSEQ, CONV_DIM, T), F32)
            S["xsT"] = dsc("xsT", (NSEQ, D_INNER, T), BF16)
            S["xs_tm"] = dsc("xs_tm", (NSEQ, T, D_INNER), BF16)
            S["B_tm"] = dsc("B_tm", (NSEQ, T, 1024), BF16)
            S["BT"] = dsc("BT", (NSEQ, 1024, T), BF16)
            S["CT"] = dsc("CT", (NSEQ, 1024, T), BF16)
            S["yf"] = dsc("yf", (NSEQ, D_INNER, T), F32)

        with ExitStack() as es:
            c = self.c = Ctx(nc, es)
            psbig = es.enter_context(nc.psum_tensor("psbig", [128, 5 * 512], F32))
            self.psbig = psbig
            for i in range(5):
                c.psum_banks.append(Tile(psbig[:, i * 512:(i + 1) * 512], excl=True))
            self.pacc = [Tile(es.enter_context(nc.psum_tensor(f"pacc{i}", [128, 512], F32)), excl=True) for i in range(2)]
            self.psb = Tile(es.enter_context(nc.psum_tensor("psb", [128, 1024], BF16)))
            K = self.K = {}
            K["ident"] = c.sb(es, [128, 128], F32, "ident")
            K["identb"] = c.sb(es, [128, 128], BF16, "identb")
            K["onesb"] = c.sb(es, [128, 128], BF16, "onesb")
            K["ones"] = c.sb(es, [128, 128], F32, "ones")
            K["blk"] = c.sb(es, [128, 128], F32, "blk")
            K["norm_g"] = c.sb(es, [128, L * 3 * 8], F32, "normg")
            K["final_g"] = c.sb(es, [128, 8], F32, "finalg")
            c.dma("sp", K["ident"].t[:], I["ident"][:, :], W=[K["ident"].R()])
            c.dma("sp", K["norm_g"].t[:], I["norm_g"][:, :], W=[K["norm_g"].R()])
            c.dma("sp", K["final_g"].t[:], I["final_g"][:, :], W=[K["final_g"].R()])
            c.op("dve", lambda e: e.tensor_copy(out=K["identb"].t[:], in_=K["ident"].t[:]),
                 R=[K["ident"].R()], W=[K["identb"].R()])
            c.op("dve", lambda e: e.memset(K["onesb"].t[:], 1.0), W=[K["onesb"].R()])
            c.op("dve", lambda e: e.memset(K["ones"].t[:], 1.0), W=[K["ones"].R()])
            c.op("dve", lambda e: e.memset(K["blk"].t[:], 0.0), W=[K["blk"].R()])
            c.op("dve", lambda e: e.memset(K["blk"].t[0:64, 0:64], 1.0), W=[K["blk"].R()])
            c.op("dve", lambda e: e.memset(K["blk"].t[64:128, 64:128], 1.0), W=[K["blk"].R()])

            self.wres = {}
            self.convert_weights()
            self.transpose_in()
            c.barrier()
            fi = si = ai = 0
            for li, (f1, mixer, f2) in enumerate(self.layers):
                if f1:
                    self.ffn(li, 0)
                if mixer == "ssm":
                    self.ssm(li, si)
                    si += 1
                elif mixer == "attn":
                    self.attn(li, ai)
                    ai += 1
                if f2:
                    self.ffn(li, 1)
            self.final_out()
            c.barrier()
        return nc

    def wr(self, key):
        r = self.wres.get(key)
        if r is None:
            r = self.wres[key] = Res()
        return r

    def convert_weights(self):
        c, I, S = self.c, self.I, self.S

        def blocked(dst, src, nm, key):
            for m in range(nm):
                c.dma("pool", dst[m], src[:, m * 128:(m + 1) * 128].rearrange("(kc p) j -> p kc j", p=128),
                      W=[self.wr((key, m))])

        si = ai = 0
        for li, (f1, mixer, f2) in enumerate(self.layers):
            for j, on in ((0, f1), (1, f2)):
                if not on:
                    continue
                blocked(S["wg"][li, j], I["ffn_w_gate"][li, j], 22, ("wg", li, j))
                blocked(S["wu"][li, j], I["ffn_w_up"][li, j], 22, ("wu", li, j))
                blocked(S["wd"][li, j], I["ffn_w_down"][li, j], 8, ("wd", li, j))
            if mixer == "ssm":
                blocked(S["w_in"][si], I["ssm_w_in"][si, :, 0:6144], 48, ("w_in", si))
                c.dma("pool", S["w_dt"][si], I["ssm_w_in"][si, :, 6144:6208].rearrange("(kc p) j -> p kc j", p=128),
                      W=[self.wr(("w_dt", si))])
                blocked(S["w_sout"][si], I["ssm_w_out"][si], 8, ("w_sout", si))
                si += 1
            elif mixer == "attn":
                blocked(S["w_q"][ai], I["attn_w_qkv"][ai, :, 0:1024], 8, ("w_q", ai))
                blocked(S["w_k"][ai], I["attn_w_qkv"][ai, :, 1024:1280], 2, ("w_k", ai))
                c.dma("pool", S["w_v"][ai], I["attn_w_qkv"][ai, :, 1280:1536].rearrange("(kc p) j -> p kc j", p=128),
                      W=[self.wr(("w_v", ai))])
                blocked(S["w_aout"][ai], I["attn_w_out"][ai], 8, ("w_aout", ai))
                ai += 1

    def xres(self, s, i):
        return self.wr(("XT", s, i))

    def xres_range(self, s, t0, n):
        return [self.xres(s, i) for i in range(t0 // 512, (t0 + n + 511) // 512)]

    def rmsnorm(self, es, xt, ht, gcol, NT, nkc=8, tmp=None):
        c, K = self.c, self.K
        sq, ln, rs = tmp
        for st in range(NT // 512):
            sl = slice(st * 512, (st + 1) * 512)
            ps = c.psum()
            for kc in range(nkc):
                c.op("act", lambda e: e.activation(out=sq[kc % 2].t[:], in_=xt.t[:, kc, sl], func=AF.Square),
                     R=[xt.R()], W=[sq[kc % 2].R()])
                c.op("pe", lambda e: e.matmul(ps.t[:], K["onesb"].t[:], sq[kc % 2].t[:], start=(kc == 0), stop=(kc == nkc - 1)),
                     R=[K["onesb"].R(), sq[kc % 2].R()], W=[ps.R()])
            c.op("act", lambda e: e.activation(out=ln.t[:], in_=ps.t[:], func=AF.Ln, scale=1.0 / (nkc * 128), bias=self.eps_col()),
                 R=[ps.R(), self.K["eps"].R()], W=[ln.R()])
            c.op("act", lambda e: e.activation(out=rs.t[:], in_=ln.t[:], func=AF.Exp, scale=-0.5),
                 R=[ln.R()], W=[rs.R()])
            for kc in range(nkc):
                c.op("dve", lambda e: e.scalar_tensor_tensor(out=ht.t[:, kc, sl], in0=xt.t[:, kc, sl], scalar=gcol(kc),
                                                             in1=rs.t[:], op0=ALU.mult, op1=ALU.mult),
                     R=[xt.R(), rs.R(), self.K["norm_g"].R(), self.K["final_g"].R()], W=[ht.R()])

    def eps_col(self):
        return self.K["eps"].t[:, 0:1]

    def norm_tmp(self, es):
        c = self.c
        sq = [c.sb(es, [128, 512], BF16, "sq") for _ in range(2)]
        ln = c.sb(es, [128, 512], F32, "ln")
        rs = c.sb(es, [128, 512], F32, "rs")
        return sq, ln, rs

    def transpose_in(self):
        c, I, K = self.c, self.I, self.K
        T = self.T
        with ExitStack() as es:
            K["eps"] = c.sb(self.c.es, [128, 1], F32, "eps")
            c.op("dve", lambda e: e.memset(K["eps"].t[:], EPS), W=[K["eps"].R()])
            xin = [c.sb(es, [128, 4, 1024], F32, "xin") for _ in range(2)]
            xo = [c.sb(es, [128, 8, 512], F32, "xo") for _ in range(2)]
            n = 0
            for s in range(self.NSEQ):
                for tt in range(T // 512):
                    b = n % 2
                    n += 1
                    c.dma("sp", xin[b].t[:], I["x"][s, tt * 512:(tt + 1) * 512, :].rearrange("(a p) f -> p a f", p=128),
                          W=[xin[b].R()])
                    for kc in range(8):
                        ps = c.psum()
                        for a in range(4):
                            c.op("pe", lambda e: e.transpose(ps.t[:, a * 128:(a + 1) * 128], xin[b].t[:, a, kc * 128:(kc + 1) * 128], K["ident"].t[:]),
                                 R=[xin[b].R(), K["ident"].R()], W=[ps.R()], inc=(a == 3))
                        eng = "act" if kc % 2 else "dve"
                        if eng == "act":
                            c.op("act", lambda e: e.copy(out=xo[b].t[:, kc, :], in_=ps.t[:]), R=[ps.R()], W=[xo[b].R()])
                        else:
                            c.op("dve", lambda e: e.tensor_copy(out=xo[b].t[:, kc, :], in_=ps.t[:]), R=[ps.R()], W=[xo[b].R()])
                    c.dma("pool", self.XT[s][:, tt * 512:(tt + 1) * 512].rearrange("(kc p) t -> p kc t", p=128), xo[b].t[:],
                          R=[xo[b].R()], W=[self.xres(s, tt)])

    def final_out(self):
        c, K = self.c, self.K
        T = self.T
        c.barrier()
        with ExitStack() as es:
            xt = [c.sb(es, [128, 8, 512], F32, "fx") for _ in range(2)]
            hn = [c.sb(es, [128, 8, 512], F32, "fh") for _ in range(2)]
            yo = [c.sb(es, [128, 4, 1024], F32, "fy") for _ in range(2)]
            tmp = self.norm_tmp(es)
            n = 0
            for s in range(self.NSEQ):
                for tt in range(T // 512):
                    b = n % 2
                    n += 1
                    c.dma("sp", xt[b].t[:], self.XT[s][:, tt * 512:(tt + 1) * 512].rearrange("(kc p) t -> p kc t", p=128),
                          R=[self.xres(s, tt)], W=[xt[b].R()])
                    if self.final:
                        self.rmsnorm(es, xt[b], hn[b], lambda kc: K["final_g"].t[:, kc:kc + 1], 512, tmp=tmp)
                        src = hn[b]
                    else:
                        src = xt[b]
                    for a in range(4):
                        for half in range(2):
                            ps = c.psum()
                            for q in range(4):
                                kc = half * 4 + q
                                c.op("pe", lambda e: e.transpose(ps.t[:, q * 128:(q + 1) * 128], src.t[:, kc, a * 128:(a + 1) * 128], K["ident"].t[:]),
                                     R=[src.R(), K["ident"].R()], W=[ps.R()], inc=(q == 3))
                            if half:
                                c.op("act", lambda e: e.copy(out=yo[b].t[:, a, half * 512:(half + 1) * 512], in_=ps.t[:]), R=[ps.R()], W=[yo[b].R()])
                            else:
                                c.op("dve", lambda e: e.tensor_copy(out=yo[b].t[:, a, half * 512:(half + 1) * 512], in_=ps.t[:]), R=[ps.R()], W=[yo[b].R()])
                    c.dma("pool", self.y[s, tt * 512:(tt + 1) * 512, :].rearrange("(a p) f -> p a f", p=128), yo[b].t[:],
                          R=[yo[b].R()], W=[self.wr(("y", s, tt))])

    def ffn(self, li, j):
        c, K, S = self.c, self.K, self.S
        T = self.T
        NT = 1024 if T >= 1024 else 512
        nst = NT // 512
        gbase = (li * 3 + (0 if j == 0 else 2)) * 8
        c.barrier()
        with ExitStack() as es:
            xt = [c.sb(es, [128, 8, NT], F32, "x") for _ in range(2)]
            ht = [c.sb(es, [128, 8, NT], BF16, "h") for _ in range(2)]
            act = c.sb(es, [128, 22, NT], BF16, "act")
            NB = 4
            wbuf = [c.sb(es, [128, 22 * 128], BF16, "w") for _ in range(NB)]
            sg = [c.sb(es, [128, 512], F32, "sg") for _ in range(2)]
            tmp = self.norm_tmp(es)
            tiles = [(s, i) for s in range(self.NSEQ) for i in range(T // NT)]
            stream = []
            for _ in tiles:
                for m in range(22):
                    stream.append(("gu", m))
                for m in range(8):
                    stream.append(("d", m))
            state = {"issued": 0}

            def issue():
                k = state["issued"]
                if k >= len(stream):
                    return
                kind, m = stream[k]
                wb = wbuf[k % NB]
                if kind == "gu":
                    c.dma("sp", wb.t[:, 0:1024], S["wg"][li, j, m].rearrange("p kc j -> p (kc j)"),
                          R=[self.wr((("wg", li, j), m))], W=[wb.R()])
                    c.dma("sp", wb.t[:, 1024:2048], S["wu"][li, j, m].rearrange("p kc j -> p (kc j)"),
                          R=[self.wr((("wu", li, j), m))], W=[wb.R()])
                else:
                    c.dma("sp", wb.t[:, :], S["wd"][li, j, m].rearrange("p kc j -> p (kc j)"),
                          R=[self.wr((("wd", li, j), m))], W=[wb.R()])
                state["issued"] = k + 1

            used = {"n": 0}

            def nextw():
                k = used["n"]
                used["n"] += 1
                while state["issued"] < min(k + NB, len(stream)):
                    issue()
                return wbuf[k % NB]

            def load_norm(idx):
                s, i = tiles[idx]
                b = idx % 2
                c.dma("sp", xt[b].t[:], self.XT[s][:, i * NT:(i + 1) * NT].rearrange("(kc p) t -> p kc t", p=128),
                      R=self.xres_range(s, i * NT, NT), W=[xt[b].R()])
                self.rmsnorm(es, xt[b], ht[b], lambda kc: K["norm_g"].t[:, gbase + kc:gbase + kc + 1], NT, tmp=tmp)

            load_norm(0)
            for idx, (s, i) in enumerate(tiles):
                b = idx % 2
                h = ht[b]
                for m in range(22):
                    wb = nextw()
                    for st in range(nst):
                        sl = slice(st * 512, (st + 1) * 512)
                        pg = c.psum()
                        pu = c.psum()
                        for kc in range(8):
                            c.op("pe", lambda e: e.matmul(pg.t[:], wb.t[:, kc * 128:(kc + 1) * 128], h.t[:, kc, sl], start=(kc == 0), stop=(kc == 7)),
                                 R=[wb.R(), h.R()], W=[pg.R()], inc=(kc == 7))
                        for kc in range(8):
                            c.op("pe", lambda e: e.matmul(pu.t[:], wb.t[:, 1024 + kc * 128:1024 + (kc + 1) * 128], h.t[:, kc, sl], start=(kc == 0), stop=(kc == 7)),
                                 R=[wb.R(), h.R()], W=[pu.R()], inc=(kc == 7))
                        sgt = sg[(m * nst + st) % 2]
                        c.op("act", lambda e: e.activation(out=sgt.t[:], in_=pg.t[:], func=AF.Silu), R=[pg.R()], W=[sgt.R()])
                        c.op("dve", lambda e: e.tensor_tensor(out=act.t[:, m, sl], in0=sgt.t[:], in1=pu.t[:], op=ALU.mult),
                             R=[sgt.R(), pu.R()], W=[act.R(("w", st))])
                if idx + 1 < len(tiles):
                    load_norm(idx + 1)
                for m in range(8):
                    wb = nextw()
                    for st in range(nst):
                        sl = slice(st * 512, (st + 1) * 512)
                        pd = c.psum()
                        for kc in range(22):
                            c.op("pe", lambda e: e.matmul(pd.t[:], wb.t[:, kc * 128:(kc + 1) * 128], act.t[:, kc, sl], start=(kc == 0), stop=(kc == 21)),
                                 R=[wb.R(), act.R(("w", st))], W=[pd.R()], inc=(kc == 21))
                        c.op("dve", lambda e: e.scalar_tensor_tensor(out=xt[b].t[:, m, sl], in0=pd.t[:], scalar=0.5, in1=xt[b].t[:, m, sl],
                                                                     op0=ALU.mult, op1=ALU.add),
                             R=[pd.R()], W=[xt[b].R()])
                c.dma("pool", self.XT[s][:, i * NT:(i + 1) * NT].rearrange("(kc p) t -> p kc t", p=128), xt[b].t[:],
                      R=[xt[b].R()], W=self.xres_range(s, i * NT, NT))

    def ssm(self, li, si):
        c, K, S, I = self.c, self.K, self.S, self.I
        T = self.T
        NT = 512
        ntile = T // NT
        nch = T // 128
        gbase = (li * 3 + 1) * 8
        gcol = lambda kc: K["norm_g"].t[:, gbase + kc:gbase + kc + 1]
        c.barrier()
        with ExitStack() as es:
            DT = c.sb(es, [128, nch, 64], F32, "DT")
            cw = c.sb(es, [128, 160], F32, "cw")
            cb = c.sb(es, [128, 32], F32, "cb")
            dtb = c.sb(es, [128, 64], F32, "dtb")
            Arow = c.sb(es, [128, 64], F32, "Arow")
            Dcol = c.sb(es, [128, 16], F32, "Dcol")
            ngc = c.sb(es, [128, 16], F32, "ngc")
            tri = [c.sb(es, [128, 128], F32, "tri") for _ in range(2)]
            one1 = c.sb(es, [128, 1], F32, "one1")
            c.dma("sp", cw.t[:], I["ssm_conv_w"][:, si * 160:(si + 1) * 160], W=[cw.R()])
            c.dma("sp", cb.t[:], I["ssm_conv_b"][:, si * 32:(si + 1) * 32], W=[cb.R()])
            c.dma("sp", dtb.t[:], I["ssm_dtb_row"][si:si + 1, :].to_broadcast([128, 64]), W=[dtb.R()])
            c.dma("sp", Arow.t[:], I["ssm_alog_row"][si:si + 1, :].to_broadcast([128, 64]), W=[Arow.R()])
            c.dma("sp", Dcol.t[:], I["ssm_D_col"][:, si * 16:(si + 1) * 16], W=[Dcol.R()])
            c.dma("sp", ngc.t[:], I["ssm_ng_col"][:, si * 16:(si + 1) * 16], W=[ngc.R()])
            c.dma("sp", tri[0].t[:], I["tri"][:, :], W=[tri[0].R()])
            c.dma("sp", tri[1].t[:], I["triT"][:, :], W=[tri[1].R()])
            c.op("dve", lambda e: e.memset(one1.t[:], 1.0), W=[one1.R()])
            c.op("act", lambda e: e.activation(out=Arow.t[:], in_=Arow.t[:], func=AF.Exp), R=[], W=[Arow.R()])
            c.op("dve", lambda e: e.tensor_scalar(out=Arow.t[:], in0=Arow.t[:], scalar1=-1.0, scalar2=None, op0=ALU.mult), W=[Arow.R()])
            for s in range(self.NSEQ):
                if 1 not in SSM_SKIP:
                    self.ssm_s1(es, s, si, gcol, DT, dtb, one1)
                c.barrier()
                if 2 not in SSM_SKIP:
                    self.ssm_s2(s, si, cw, cb)
                c.barrier()
                if 3 not in SSM_SKIP:
                    self.ssm_s3(s, li, si, DT, Arow, Dcol, ngc, tri)
                c.barrier()

    def ssm_s1(self, es0, s, si, gcol, DT, dtb, one1):
        c, K, S = self.c, self.K, self.S
        T = self.T
        NT = 512
        ntile = T // NT
        with ExitStack() as es:
            Win = c.sb(es, [128, 48, 1024], BF16, "Win")
            xt = c.sb(es, [128, 8, NT], F32, "sx")
            hv = [c.sb(es, [128, 8, NT], BF16, "shv") for _ in range(2)]
            ntmp = self.norm_tmp(es)
            wdt = c.sb(es, [128, 8, 64], BF16, "wdt")
            zo = [c.sb(es, [128, 4, 512], BF16, "zo") for _ in range(2)]
            xo = [c.sb(es, [128, 4, 512], F32, "xo") for _ in range(2)]
            t64 = [c.sb(es, [128, 64], F32, "t64") for _ in range(2)]
            c.dma("sp", wdt.t[:], S["w_dt"][si], R=[self.wr(("w_dt", si))], W=[wdt.R()])

            def load_norm(tt):
                c.dma("sp", xt.t[:], self.XT[s][:, tt * NT:(tt + 1) * NT].rearrange("(kc p) t -> p kc t", p=128),
                      R=self.xres_range(s, tt * NT, NT), W=[xt.R()])
                self.rmsnorm(es, xt, hv[tt % 2], gcol, NT, tmp=ntmp)

            load_norm(0)
            for q6 in range(6):
                c.dma("sp", Win.t[:, q6 * 8:(q6 + 1) * 8, :], S["w_in"][si, q6 * 8:(q6 + 1) * 8].rearrange("m p kc j -> p m (kc j)"),
                      R=[self.wr((("w_in", si), mm)) for mm in range(q6 * 8, (q6 + 1) * 8)], W=[Win.R(q6)])
            n = 0
            for tt in range(ntile):
                h = hv[tt % 2]
                if tt + 1 < ntile:
                    load_norm(tt + 1)
                for a in range(4):
                    ch = tt * 4 + a
                    ps = c.psum()
                    for kc in range(8):
                        c.op("pe", lambda e: e.matmul(ps.t[:, 0:64], h.t[:, kc, a * 128:(a + 1) * 128], wdt.t[:, kc, :], start=(kc == 0), stop=(kc == 7)),
                             R=[h.R(), wdt.R()], W=[ps.R()], inc=(kc == 7))
                    t = t64[ch % 2]
                    c.op("dve", lambda e: e.tensor_tensor(out=t.t[:], in0=ps.t[:, 0:64], in1=dtb.t[:], op=ALU.add), R=[ps.R(), dtb.R()], W=[t.R()])
                    c.op("act", lambda e: e.activation(out=t.t[:], in_=t.t[:], func=AF.Exp), W=[t.R()])
                    c.op("act", lambda e: e.activation(out=DT.t[:, ch, :], in_=t.t[:], func=AF.Ln, bias=one1.t[:, 0:1]), R=[t.R(), one1.R()], W=[DT.R()])
                for m in range(48):
                    ps = c.psum()
                    for kc in range(8):
                        c.op("pe", lambda e: e.matmul(ps.t[:], Win.t[:, m, kc * 128:(kc + 1) * 128], h.t[:, kc, :], start=(kc == 0), stop=(kc == 7)),
                             R=[Win.R(m // 8), h.R()], W=[ps.R()], inc=(kc == 7))
                    q, grp = m % 4, m // 4
                    if m < 16:
                        o = zo[grp % 2]
                        c.op("act", lambda e: e.activation(out=o.t[:, q, :], in_=ps.t[:], func=AF.Silu), R=[ps.R()], W=[o.R()])
                        if q == 3:
                            c.dma("pool", S["zs"][s, grp * 512:(grp + 1) * 512, tt * NT:(tt + 1) * NT].rearrange("(q p) t -> p q t", p=128), o.t[:],
                                  R=[o.R()], W=[self.wr(("zs", s, tt))])
                    else:
                        o = xo[grp % 2]
                        c.op("dve", lambda e: e.tensor_copy(out=o.t[:, q, :], in_=ps.t[:]), R=[ps.R()], W=[o.R()])
                        if q == 3:
                            c.dma("pool", S["xbc"][s, (grp - 4) * 512:(grp - 3) * 512, tt * NT:(tt + 1) * NT].rearrange("(q p) t -> p q t", p=128), o.t[:],
                                  R=[o.R()], W=[self.wr(("xbc", s))])

    def ssm_s2(self, s, si, cw, cb):
        c, K, S = self.c, self.K, self.S
        T = self.T
        NT = 512
        ntile = T // NT
        NBUF = 4
        with ExitStack() as es:
            cin = [c.sb(es, [128, NT + 4], F32, "cin") for _ in range(NBUF)]
            acc = [c.sb(es, [128, NT], F32, "cacc") for _ in range(NBUF)]
            ptm = [c.sb(es, [128, NT], F32, "cptm") for _ in range(NBUF)]
            fm = c.sb(es, [128, 32, NT], BF16, "fm")
            xtm = c.sb(es, [128, 4, 2048], BF16, "cxtm")
            btm = c.sb(es, [128, 4, 1024], BF16, "cbtm")
            its = [(tt, cc) for tt in range(ntile) for cc in range(32)]

            def load(n):
                tt, cc = its[n]
                t0 = tt * NT
                ci = cin[n % NBUF]
                lo = 2 if tt == 0 else 0
                hi = NT + 2 if tt == ntile - 1 else NT + 4
                if tt == 0:
                    c.op("pool", lambda e: e.memset(ci.t[:, 0:2], 0.0), W=[ci.R()])
                if tt == ntile - 1:
                    c.op("pool", lambda e: e.memset(ci.t[:, NT + 2:NT + 4], 0.0), W=[ci.R()])
                c.dma("sp", ci.t[:, lo:hi], S["xbc"][s, cc * 128:(cc + 1) * 128, t0 - 2 + lo:t0 - 2 + hi], R=[self.wr(("xbc", s))], W=[ci.R()])

            def ident(n):
                tt_, cc_ = its[n]
                ci_ = cin[n % NBUF]
                ac_ = acc[n % NBUF]
                c.op("act", lambda e: e.activation(out=ac_.t[:], in_=ci_.t[:, 2:NT + 2], func=AF.Identity, scale=cw.t[:, cc_ * 5 + 2:cc_ * 5 + 3], bias=cb.t[:, cc_:cc_ + 1]),
                     R=[ci_.R(), cw.R(), cb.R()], W=[ac_.R()])

            for n in range(min(NBUF - 1, len(its))):
                load(n)
            ident(0)
            for n, (tt, cc) in enumerate(its):
                t0 = tt * NT
                if n + NBUF - 1 < len(its):
                    load(n + NBUF - 1)
                if n + 1 < len(its):
                    ident(n + 1)
                ci = cin[n % NBUF]
                ac = acc[n % NBUF]
                pt = ptm[n % NBUF]
                wcol = lambda j: cw.t[:, cc * 5 + j:cc * 5 + j + 1]
                for jn, j in enumerate((0, 1, 3, 4)):
                    src = ac if jn == 0 else pt
                    c.op("dve", lambda e: e.scalar_tensor_tensor(out=pt.t[:], in0=ci.t[:, j:j + NT], scalar=wcol(j), in1=src.t[:], op0=ALU.mult, op1=ALU.add),
                         R=[ci.R(), cw.R(), src.R()], W=[pt.R()])
                c.op("act", lambda e: e.activation(out=fm.t[:, cc, :], in_=pt.t[:], func=AF.Silu), R=[pt.R()], W=[fm.R(cc)])
                if cc < 24:
                    half = cc % 2
                    pb = self.psb
                    for a in range(4):
                        c.op("pe", lambda e: e.transpose(pb.t[:, half * 512 + a * 128:half * 512 + (a + 1) * 128], fm.t[:, cc, a * 128:(a + 1) * 128], K["identb"].t[:]),
                             R=[fm.R(cc), K["identb"].R()], W=[pb.R(half)], inc=(a == 3))
                    if cc < 16:
                        dst, dr = xtm.t[:, :, cc * 128:(cc + 1) * 128], xtm.R()
                    else:
                        dst, dr = btm.t[:, :, (cc - 16) * 128:(cc - 15) * 128], btm.R()
                    c.op("act", lambda e: e.copy(out=dst, in_=pb.t[:, half * 512:(half + 1) * 512].rearrange("p (a f) -> p a f", a=4)),
                         R=[pb.R(half)], W=[dr])
                if cc == 31:
                    allfm = [fm.R(q) for q in range(32)]
                    c.dma("pool", S["xsT"][s][:, t0:t0 + NT].rearrange("(cc p) t -> p cc t", p=128), fm.t[:, 0:16, :], R=allfm[0:16], W=[self.wr(("xsT", s, tt))])
                    c.dma("pool", S["BT"][s][:, t0:t0 + NT].rearrange("(cc p) t -> p cc t", p=128), fm.t[:, 16:24, :], R=allfm[16:24], W=[self.wr(("BT", s, tt))])
                    c.dma("pool", S["CT"][s][:, t0:t0 + NT].rearrange("(cc p) t -> p cc t", p=128), fm.t[:, 24:32, :], R=allfm[24:32], W=[self.wr(("CT", s, tt))])
                    c.dma("pool", S["xs_tm"][s, t0:t0 + NT, :].rearrange("(a p) f -> p a f", p=128), xtm.t[:], R=[xtm.R()], W=[self.wr(("xs_tm", s, tt))])
                    c.dma("pool", S["B_tm"][s, t0:t0 + NT, :].rearrange("(a p) f -> p a f", p=128), btm.t[:], R=[btm.R()], W=[self.wr(("B_tm", s, tt))])

    def ssm_s3(self, s, li, si, DT, Arow, Dcol, ngc, tri):
        c = self.c
        base_banks = c.psum_banks
        c.psum_banks = base_banks + (self.pacc[0:1] if PCS_DEDICATED else self.pacc)
        c.psum_next = 0
        try:
            self._ssm_s3(s, li, si, DT, Arow, Dcol, ngc, tri)
        finally:
            c.psum_banks = base_banks
            c.psum_next = 0

    def _ssm_s3(self, s, li, si, DT, Arow, Dcol, ngc, tri):
        c, K, S = self.c, self.K, self.S
        T = self.T
        NT = 512
        ntile = T // NT
        with ExitStack() as es:
            St = c.sb(es, [128, 8, 256], F32, "St")
            Sb = c.sb(es, [128, 8, 256], BF16, "Sb")
            xtm = c.sb(es, [128, 4, 2048], BF16, "xtm")
            btm = c.sb(es, [128, 4, 1024], BF16, "btm")
            BTt = c.sb(es, [128, 8, NT], BF16, "BTt")
            CTt = c.sb(es, [128, 8, NT], BF16, "CTt")
            yacc = c.sb(es, [128, 16, NT], F32, "yacc")
            CH = []
            for _pb in range(NCH):
                CH.append((c.sb(es, [128, 32], F32, "atm"), c.sb(es, [128, 32], F32, "ncs"), c.sb(es, [128, 32], F32, "d1"),
                           c.sb(es, [128, 32], F32, "wend"), c.sb(es, [128, 32], F32, "dec"), c.sb(es, [96, 128], BF16, "cs3"),
                           c.sb(es, [128, 32], F32, "nb"), None))
            r1 = c.sb(es, [32, 128], F32, "r1")
            r2 = c.sb(es, [32, 128], F32, "r2")
            midt = c.sb(es, [32, 128], BF16, "midt")
            lot = c.sb(es, [32, 128], BF16, "lot")
            lndt = c.sb(es, [128, 32], F32, "lndt")
            XW = [c.sb(es, [128, 256], BF16, "xwg") for _ in range(2)]
            Sel3 = c.sb(es, [96, 32 * 128], BF16, "sel3")
            c.dma("pool", Sel3.t[:], self.I["sel3"][:, :], W=[Sel3.R()])
            cbm = [c.sb(es, [128, 128], F32, "cbm") for _ in range(2)]
            ECS = [c.sb(es, [128, 512], F32, "ECS") for _ in range(2)]
            Lt = [c.sb(es, [128, 512], F32, "Lt") for _ in range(2)]
            Mt = [c.sb(es, [128, 512], BF16, "Mt") for _ in range(2)]
            Cs = [c.sb(es, [128, 512], BF16, "Cs") for _ in range(2)]
            stmp = [c.sb(es, [128, 256], F32, "stmp") for _ in range(2)]
            zt = c.sb(es, [128, 16, NT], BF16, "zt")
            xf = c.sb(es, [128, 16, NT], BF16, "xf")
            xr = c.sb(es, [128, 8, NT], F32, "xr")
            wso = [c.sb(es, [128, 2048], BF16, "wso") for _ in range(2)]
            ntmp = self.norm_tmp(es)
            rsb = c.sb(es, [128, 512], F32, "rsb")

            def prologue_pieces(d, tt, a, pb):
                ch = tt * 4 + a
                dt = DT.t[:, ch, d * 32:(d + 1) * 32]
                trm = tri[d]
                atm, ncs, d1, wend, dec, cs3, nb, _ = CH[pb]
                hold = {}

                def p0():
                    c.op("dve", lambda e: e.tensor_tensor(out=atm.t[:], in0=dt, in1=Arow.t[:, d * 32:(d + 1) * 32], op=ALU.mult),
                         R=[DT.R(), Arow.R()], W=[atm.R()])
                    pcs = hold["pcs"] = self.pacc[1] if PCS_DEDICATED else c.psum()
                    c.op("pe", lambda e: e.matmul(pcs.t[:, 0:32], trm.t[:], atm.t[:], start=True, stop=True), R=[trm.R(), atm.R()], W=[pcs.R()], inc=False)
                    c.op("pe", lambda e: e.matmul(pcs.t[:, 32:64], K["ones"].t[:], atm.t[:], start=True, stop=True), R=[K["ones"].R(), atm.R()], W=[pcs.R()], inc=False)
                    c.op("pe", lambda e: e.matmul(pcs.t[0:32, 64:192], atm.t[:], trm.t[:], start=True, stop=True), R=[trm.R(), atm.R()], W=[pcs.R()])

                def p1():
                    pcs = hold["pcs"]
                    c.op("dve", lambda e: e.tensor_scalar(out=ncs.t[:], in0=pcs.t[:, 0:32], scalar1=-1.0, scalar2=None, op0=ALU.mult), R=[pcs.R()], W=[ncs.R()])
                    c.op("dve", lambda e: e.tensor_tensor(out=d1.t[:], in0=pcs.t[:, 32:64], in1=ncs.t[:], op=ALU.add), R=[pcs.R(), ncs.R()], W=[d1.R()])
                    c.op("act", lambda e: e.activation(out=lndt.t[:], in_=dt, func=AF.Ln), R=[DT.R()], W=[lndt.R()])

                def p2():
                    pcs = hold["pcs"]
                    c.op("act", lambda e: e.activation(out=d1.t[:], in_=d1.t[:], func=AF.Exp), W=[d1.R()])
                    c.op("act", lambda e: e.activation(out=dec.t[:], in_=pcs.t[:, 32:64], func=AF.Exp), R=[pcs.R()], W=[dec.R()])
                    c.op("act", lambda e: e.copy(out=cs3.t[0:32, :], in_=pcs.t[0:32, 64:192]), R=[pcs.R()], W=[cs3.R()])
                    c.op("dve", lambda e: e.tensor_tensor(out=nb.t[:], in0=lndt.t[:], in1=ncs.t[:], op=ALU.add), R=[lndt.R(), ncs.R()], W=[nb.R()])

                def p3():
                    pcs = hold["pcs"]
                    c.op("dve", lambda e: e.tensor_tensor(out=wend.t[:], in0=d1.t[:], in1=dt, op=ALU.mult), R=[d1.R(), DT.R()], W=[wend.R()])
                    c.op("dve", lambda e: e.tensor_tensor(out=r1.t[:], in0=pcs.t[0:32, 64:192], in1=cs3.t[0:32, :], op=ALU.subtract), R=[pcs.R(), cs3.R()], W=[r1.R()])

                def p4():
                    c.op("act", lambda e: e.copy(out=midt.t[:], in_=r1.t[:]), R=[r1.R()], W=[midt.R()])

                def p5():
                    c.op("dve", lambda e: e.tensor_tensor(out=r2.t[:], in0=r1.t[:], in1=midt.t[:], op=ALU.subtract), R=[r1.R(), midt.R()], W=[r2.R()])
                    c.op("dve", lambda e: e.tensor_copy(out=cs3.t[32:64, :], in_=midt.t[:]), R=[midt.R()], W=[cs3.R()])

                def p6():
                    c.op("act", lambda e: e.copy(out=lot.t[:], in_=r2.t[:]), R=[r2.R()], W=[lot.R()])

                def p7():
                    c.op("dve", lambda e: e.tensor_copy(out=cs3.t[64:96, :], in_=lot.t[:]), R=[lot.R()], W=[cs3.R()])

                return [p0, p1, p2, p3, p4, p5, p6, p7]

            PS = {}

            def st1(d, tt, a, pb, g):
                asl = slice(a * 128, (a + 1) * 128)
                cs3 = CH[pb][5]
                pcb = c.psum()
                c.op("pe", lambda e: e.matmul(pcb.t[:, 0:128], BTt.t[:, g, asl], CTt.t[:, g, asl], start=True, stop=True), R=[BTt.R(), CTt.R()], W=[pcb.R()])
                pcr = c.psum()
                for rr in range(4):
                    r = g * 4 + rr
                    c.op("pe", lambda e: e.matmul(pcr.t[:, rr * 128:(rr + 1) * 128], Sel3.t[:, r * 128:(r + 1) * 128], cs3.t[:], start=True, stop=True),
                         R=[Sel3.R(), cs3.R()], W=[pcr.R()], inc=(rr == 3))
                PS[(a, g)] = [pcb, pcr, None]

            def st2(d, tt, a, pb, g):
                trm = tri[d]
                ncs = CH[pb][6]
                k = g % 2
                pcb, pcr, _ = PS[(a, g)]
                c.op("act", lambda e: e.activation(out=ECS[k].t[:], in_=pcr.t[:], func=AF.Exp), R=[pcr.R()], W=[ECS[k].R()])
                for rr in range(4):
                    r = g * 4 + rr
                    c.op("act", lambda e: e.activation(out=Lt[k].t[:, rr * 128:(rr + 1) * 128], in_=pcr.t[:, rr * 128:(rr + 1) * 128], func=AF.Exp, bias=ncs.t[:, r:r + 1]),
                         R=[pcr.R(), ncs.R()], W=[Lt[k].R()])
                c.op("dve", lambda e: e.tensor_tensor(out=cbm[k].t[:], in0=pcb.t[:, 0:128], in1=trm.t[:], op=ALU.mult), R=[pcb.R(), trm.R()], W=[cbm[k].R()])

            def st3(d, tt, a, pb, g):
                asl = slice(a * 128, (a + 1) * 128)
                k = g % 2
                c.op("dve", lambda e: e.scalar_tensor_tensor(out=Mt[k].t[:].rearrange("p (r i) -> p r i", r=4), in0=Lt[k].t[:].rearrange("p (r i) -> p r i", r=4), scalar=1e30,
                                                             in1=cbm[k].t[:].unsqueeze(1).to_broadcast([128, 4, 128]), op0=ALU.min, op1=ALU.mult),
                     R=[Lt[k].R(), cbm[k].R()], W=[Mt[k].R()])
                c.op("pool", lambda e: e.tensor_tensor(out=Cs[k].t[:].rearrange("p (r i) -> p r i", r=4), in0=ECS[k].t[:].rearrange("p (r i) -> p r i", r=4),
                                                      in1=CTt.t[:, g, asl].unsqueeze(1).to_broadcast([128, 4, 128]), op=ALU.mult),
                     R=[ECS[k].R(), CTt.R()], W=[Cs[k].R()])
                wend = CH[pb][3]
                c.op("pool", lambda e: e.tensor_tensor(out=XW[k].t[:].rearrange("p (r d) -> p r d", r=4), in0=xtm.t[:, a, g * 256:(g + 1) * 256].rearrange("p (r d) -> p r d", r=4),
                                                      in1=wend.t[:, g * 4:(g + 1) * 4].unsqueeze(2).to_broadcast([128, 4, 64]), op=ALU.mult),
                     R=[xtm.R(), wend.R()], W=[XW[k].R()])

            def st4(d, tt, a, pb, g):
                k = g % 2
                py = c.psum()
                PS[(a, g)][2] = py
                for rr in range(4):
                    r = g * 4 + rr
                    half = rr % 2
                    osl = py.t[half * 64:(half + 1) * 64, (rr // 2) * 128:(rr // 2 + 1) * 128]
                    c.op("pe", lambda e: e.matmul(osl, xtm.t[:, a, r * 64:(r + 1) * 64], Mt[k].t[:, rr * 128:(rr + 1) * 128], start=True, stop=False),
                         R=[xtm.R(), Mt[k].R()], W=[py.R()], inc=False)
                    c.op("pe", lambda e: e.matmul(osl, Sb.t[:, g, rr * 64:(rr + 1) * 64], Cs[k].t[:, rr * 128:(rr + 1) * 128], start=False, stop=True),
                         R=[Sb.R(g), Cs[k].R()], W=[py.R()], inc=(rr == 3))
                pst = c.psum()
                PS[(a, g)].append(pst)
                c.op("pe", lambda e: e.matmul(pst.t[:, 0:256], btm.t[:, a, g * 128:(g + 1) * 128], XW[k].t[:], start=True, stop=True),
                     R=[btm.R(), XW[k].R()], W=[pst.R()])

            def st5(d, tt, a, pb, g):
                asl = slice(a * 128, (a + 1) * 128)
                dec = CH[pb][4]
                k = g % 2
                _ps = PS.pop((a, g))
                py, pst = _ps[2], _ps[3]
                ydst = yacc.t[:, 2 * g:2 * g + 2, asl]
                ysrc = py.t[:, 0:256].rearrange("p (c i) -> p c i", c=2)
                if d == 0:
                    c.op("act", lambda e: e.copy(out=ydst, in_=ysrc), R=[py.R()], W=[yacc.R(g)])
                else:
                    c.op("dve", lambda e: e.tensor_tensor(out=ydst, in0=ysrc, in1=ydst, op=ALU.add), R=[py.R()], W=[yacc.R(g)])
                st = stmp[k]
                c.op("dve", lambda e: e.tensor_tensor(out=st.t[:].rearrange("p (r d) -> p r d", r=4), in0=St.t[:, g, :].rearrange("p (r d) -> p r d", r=4),
                                                     in1=dec.t[:, g * 4:(g + 1) * 4].unsqueeze(2).to_broadcast([128, 4, 64]), op=ALU.mult),
                     R=[St.R(g), dec.R()], W=[st.R()])
                c.op("dve", lambda e: e.tensor_tensor(out=St.t[:, g, :], in0=pst.t[:, 0:256], in1=st.t[:], op=ALU.add), R=[pst.R(), st.R()], W=[St.R(g)])
                if d == 0:
                    c.op("dve", lambda e: e.tensor_copy(out=Sb.t[:, g, :], in_=St.t[:, g, :]), R=[St.R(g)], W=[Sb.R(g)])
                else:
                    c.op("act", lambda e: e.copy(out=Sb.t[:, g, :], in_=St.t[:, g, :]), R=[St.R(g)], W=[Sb.R(g)])

            def run_tile(d, tt):
                aorder = list(range(4)) if d == 0 else [3, 2, 1, 0]
                items = [(ai_, a, g) for ai_, a in enumerate(aorder) for g in range(8)]
                n = len(items)
                base = self._chunk_ctr
                self._chunk_ctr += 4
                for p in prologue_pieces(d, tt, aorder[0], base % NCH):
                    p()
                stages = [st1, st2, st3, st4, st5]
                pend = []
                for t in range(n + 4):
                    if t < n:
                        ai_, a, g = items[t]
                        if g == PRO_SLOT and ai_ + 1 < 4:
                            pend = prologue_pieces(d, tt, aorder[ai_ + 1], (base + ai_ + 1) % NCH)
                        for _ in range(PRO_RATE):
                            if pend:
                                pend.pop(0)()
                    for si_, fn in enumerate(stages):
                        j = t - si_
                        if 0 <= j < n:
                            ai_, a, g = items[j]
                            fn(d, tt, a, (base + ai_) % NCH, g)
                assert not pend

            def load_tile(tt):
                t0 = tt * NT
                c.dma("sp", xtm.t[:], S["xs_tm"][s, t0:t0 + NT, :].rearrange("(a p) f -> p a f", p=128), R=[self.wr(("xs_tm", s, tt))], W=[xtm.R()])
                c.dma("sp", btm.t[:], S["B_tm"][s, t0:t0 + NT, :].rearrange("(a p) f -> p a f", p=128), R=[self.wr(("B_tm", s, tt))], W=[btm.R()])
                c.dma("sp", BTt.t[:], S["BT"][s][:, t0:t0 + NT].rearrange("(cc p) t -> p cc t", p=128), R=[self.wr(("BT", s, tt))], W=[BTt.R()])
                c.dma("sp", CTt.t[:], S["CT"][s][:, t0:t0 + NT].rearrange("(cc p) t -> p cc t", p=128), R=[self.wr(("CT", s, tt))], W=[CTt.R()])

            self._chunk_ctr = 0
            for d in range(2):
                c.op("dve", lambda e: e.memset(St.t[:], 0.0), W=[St.R(g) for g in range(8)])
                c.op("dve", lambda e: e.memset(Sb.t[:], 0.0), W=[Sb.R(g) for g in range(8)])
                order = range(ntile) if d == 0 else range(ntile - 1, -1, -1)
                for tt in order:
                    t0 = tt * NT
                    load_tile(tt)
                    if d == 1:
                        c.dma("sp", yacc.t[:], S["yf"][s][:, t0:t0 + NT].rearrange("(cc p) t -> p cc t", p=128), R=[self.wr(("yf", s, tt))], W=[yacc.R(g) for g in range(8)])
                        c.dma("sp", xf.t[:], S["xsT"][s][:, t0:t0 + NT].rearrange("(cc p) t -> p cc t", p=128), R=[self.wr(("xsT", s, tt))], W=[xf.R(q_) for q_ in range(16)])
                        c.dma("sp", zt.t[:], S["zs"][s][:, t0:t0 + NT].rearrange("(cc p) t -> p cc t", p=128), R=[self.wr(("zs", s, tt))], W=[zt.R()])
                        c.dma("sp", xr.t[:], self.XT[s][:, t0:t0 + NT].rearrange("(kc p) t -> p kc t", p=128), R=self.xres_range(s, t0, NT), W=[xr.R()])
                        for mo in range(2):
                            c.dma("sp", wso[mo].t[:], S["w_sout"][si, mo].rearrange("p kc j -> p (kc j)"), R=[self.wr((("w_sout", si), mo))], W=[wso[mo].R()])
                    run_tile(d, tt)
                    if d == 0:
                        c.dma("pool", S["yf"][s][:, t0:t0 + NT].rearrange("(cc p) t -> p cc t", p=128), yacc.t[:], R=[yacc.R(g) for g in range(8)], W=[self.wr(("yf", s, tt))])
                        continue
                    sq, ln, rs = ntmp
                    rs2 = [rs, rsb]

                    def epiA(g):
                        ps = c.psum()
                        for q in range(2):
                            cc = 2 * g + q
                            c.op("dve", lambda e: e.scalar_tensor_tensor(out=yacc.t[:, cc, :], in0=xf.t[:, cc, :], scalar=Dcol.t[:, cc:cc + 1], in1=yacc.t[:, cc, :], op0=ALU.mult, op1=ALU.add),
                                 R=[xf.R(cc), Dcol.R()], W=[yacc.R(g)])
                            c.op("pool", lambda e: e.tensor_tensor(out=yacc.t[:, cc, :], in0=yacc.t[:, cc, :], in1=zt.t[:, cc, :], op=ALU.mult), R=[zt.R()], W=[yacc.R(g)])
                            c.op("act", lambda e: e.activation(out=sq[q].t[:], in_=yacc.t[:, cc, :], func=AF.Square), R=[yacc.R(g)], W=[sq[q].R()])
                            c.op("pe", lambda e: e.matmul(ps.t[:], K["onesb"].t[:], sq[q].t[:], start=(q == 0), stop=(q == 1)), R=[K["onesb"].R(), sq[q].R()], W=[ps.R()])
                        c.op("act", lambda e: e.activation(out=ln.t[:], in_=ps.t[:], func=AF.Ln, scale=1.0 / 256, bias=self.eps_col()), R=[ps.R(), K["eps"].R()], W=[ln.R()])
                        c.op("act", lambda e: e.activation(out=rs2[g % 2].t[:], in_=ln.t[:], func=AF.Exp, scale=-0.5), R=[ln.R()], W=[rs2[g % 2].R()])

                    def epiB(g):
                        for q in range(2):
                            cc = 2 * g + q
                            c.op("dve", lambda e: e.scalar_tensor_tensor(out=xf.t[:, cc, :], in0=yacc.t[:, cc, :], scalar=ngc.t[:, cc:cc + 1], in1=rs2[g % 2].t[:], op0=ALU.mult, op1=ALU.mult),
                                 R=[yacc.R(g), ngc.R(), rs2[g % 2].R()], W=[xf.R(cc)])

                    epiA(0)
                    for g in range(8):
                        if g + 1 < 8:
                            epiA(g + 1)
                        epiB(g)
                    for mo in range(8):
                        wb = wso[mo % 2]
                        ps = c.psum()
                        for kc in range(16):
                            c.op("pe", lambda e: e.matmul(ps.t[:], wb.t[:, kc * 128:(kc + 1) * 128], xf.t[:, kc, :], start=(kc == 0), stop=(kc == 15)),
                                 R=[wb.R(), xf.R(kc)], W=[ps.R()], inc=(kc == 15))
                        if mo + 2 < 8:
                            c.dma("sp", wb.t[:], S["w_sout"][si, mo + 2].rearrange("p kc j -> p (kc j)"), R=[self.wr((("w_sout", si), mo + 2))], W=[wb.R()])
                        c.op("dve", lambda e: e.tensor_tensor(out=xr.t[:, mo, :], in0=ps.t[:], in1=xr.t[:, mo, :], op=ALU.add), R=[ps.R()], W=[xr.R()])
                    c.dma("pool", self.XT[s][:, t0:t0 + NT].rearrange("(kc p) t -> p kc t", p=128), xr.t[:], R=[xr.R()], W=self.xres_range(s, t0, NT))

    def qk_post(self, ps, gcol, out_ap, bias_col, cs, sn, tm):
        c, K = self.c, self.K
        qraw, sqf, ln, rs, qn, t1, t2 = tm
        c.op("act", lambda e: e.copy(out=qraw.t[:], in_=ps.t[:]), R=[ps.R()], W=[qraw.R()])
        c.op("act", lambda e: e.activation(out=sqf.t[:], in_=ps.t[:], func=AF.Square), R=[ps.R()], W=[sqf.R()])
        p2 = c.psum()
        c.op("pe", lambda e: e.matmul(p2.t[:], K["blk"].t[:], sqf.t[:], start=True, stop=True), R=[K["blk"].R(), sqf.R()], W=[p2.R()])
        c.op("act", lambda e: e.activation(out=ln.t[:], in_=p2.t[:], func=AF.Ln, scale=1.0 / 64, bias=self.eps_col()),
             R=[p2.R(), K["eps"].R()], W=[ln.R()])
        c.op("act", lambda e: e.activation(out=rs.t[:], in_=ln.t[:], func=AF.Exp, scale=-0.5, bias=bias_col),
             R=[ln.R(), K["lnq"].R()], W=[rs.R()])
        c.op("dve", lambda e: e.scalar_tensor_tensor(out=qn.t[:], in0=qraw.t[:], scalar=gcol, in1=rs.t[:], op0=ALU.mult, op1=ALU.mult),
             R=[qraw.R(), rs.R(), K["qkg"].R()], W=[qn.R()])
        p3 = c.psum()
        c.op("pe", lambda e: e.matmul(p3.t[:], K["Rm"].t[:], qn.t[:], start=True, stop=True), R=[K["Rm"].R(), qn.R()], W=[p3.R()])
        c.op("pool", lambda e: e.tensor_tensor(out=t1.t[:], in0=qn.t[:], in1=cs.t[:], op=ALU.mult), R=[qn.R(), cs.R()], W=[t1.R()])
        c.op("dve", lambda e: e.tensor_tensor(out=t2.t[:], in0=p3.t[:], in1=sn.t[:], op=ALU.mult), R=[p3.R(), sn.R()], W=[t2.R()])
        if callable(out_ap):
            out_ap(t1, t2)
        else:
            c.op("pool", lambda e: e.tensor_tensor(out=out_ap[0], in0=t1.t[:], in1=t2.t[:], op=ALU.add), R=[t1.R(), t2.R()], W=[out_ap[1]])

    def attn(self, li, ai):
        c, K, S, I = self.c, self.K, self.S, self.I
        T = self.T
        NT = 512
        ntile = T // NT
        nkc = T // 128
        gbase = (li * 3 + 1) * 8
        c.barrier()
        with ExitStack() as es:
            KT = c.sb(es, [128, 2, T], BF16, "KT")
            Vx = c.sb(es, [128, nkc, 4, 128], BF16, "Vx")
            xt = c.sb(es, [128, 8, NT], F32, "ax")
            ht = c.sb(es, [128, 8, NT], BF16, "ah")
            cs = c.sb(es, [128, NT], F32, "cos")
            sn = c.sb(es, [128, NT], F32, "sin")
            tms = [[c.sb(es, [128, 512], F32, "qk") for _ in range(7)] for _ in range(2)]
            tmi = [0]

            def next_tm():
                tmi[0] += 1
                return tms[tmi[0] % 2]
            ntmp = self.norm_tmp(es)
            K["Rm"] = c.sb(es, [128, 128], F32, "Rm")
            K["qkg"] = c.sb(es, [128, 2], F32, "qkg")
            qkall = c.sb(es, [128, 2 * max(self.n_att, 1)], F32, "qkall")
            K["lnq"] = c.sb(es, [128, 2], F32, "lnq")
            c.dma("sp", K["Rm"].t[:], I["Rm"][:, :], W=[K["Rm"].R()])
            na_ = max(self.n_att, 1)
            c.dma("sp", qkall.t[:, 0:na_], I["attn_qg_col"][:, :], W=[qkall.R()])
            c.dma("sp", qkall.t[:, na_:2 * na_], I["attn_kg_col"][:, :], W=[qkall.R()])
            c.op("dve", lambda e: e.tensor_copy(out=K["qkg"].t[:, 0:1], in_=qkall.t[:, ai:ai + 1]), R=[qkall.R()], W=[K["qkg"].R()])
            c.op("dve", lambda e: e.tensor_copy(out=K["qkg"].t[:, 1:2], in_=qkall.t[:, na_ + ai:na_ + ai + 1]), R=[qkall.R()], W=[K["qkg"].R()])
            c.op("dve", lambda e: e.memset(K["lnq"].t[:, 0:1], float(np.log(0.125))), W=[K["lnq"].R()])
            c.op("dve", lambda e: e.memset(K["lnq"].t[:, 1:2], 0.0), W=[K["lnq"].R()])
            c.op("dve", lambda e: e.memset(Vx.t[:, :, :, 64:128], 1.0), W=[Vx.R()])
            gcol = lambda kc: K["norm_g"].t[:, gbase + kc:gbase + kc + 1]

            def load_tile(s, tt):
                c.dma("sp", xt.t[:], self.XT[s][:, tt * NT:(tt + 1) * NT].rearrange("(kc p) t -> p kc t", p=128),
                      R=self.xres_range(s, tt * NT, NT), W=[xt.R()])
                c.dma("sp", cs.t[:], I["cosT"][:, tt * NT:(tt + 1) * NT], W=[cs.R()])
                c.dma("sp", sn.t[:], I["sinT"][:, tt * NT:(tt + 1) * NT], W=[sn.R()])
                self.rmsnorm(es, xt, ht, gcol, NT, tmp=ntmp)

            for s in range(self.NSEQ):
                with ExitStack() as e1:
                    wk = c.sb(e1, [128, 2, 1024], BF16, "wk")
                    wv = c.sb(e1, [128, 8, 256], BF16, "wv")
                    c.dma("sp", wk.t[:], S["w_k"][ai].rearrange("m p kc j -> p m (kc j)"),
                          R=[self.wr((("w_k", ai), m)) for m in range(2)], W=[wk.R()])
                    c.dma("sp", wv.t[:], S["w_v"][ai], R=[self.wr(("w_v", ai))], W=[wv.R()])
                    for tt in range(ntile):
                        load_tile(s, tt)
                        for kv in range(2):
                            ps = c.psum()
                            for kc in range(8):
                                c.op("pe", lambda e: e.matmul(ps.t[:], wk.t[:, kv, kc * 128:(kc + 1) * 128], ht.t[:, kc, :], start=(kc == 0), stop=(kc == 7)),
                                     R=[wk.R(), ht.R()], W=[ps.R()], inc=(kc == 7))
                            self.qk_post(ps, K["qkg"].t[:, 1:2], (KT.t[:, kv, tt * NT:(tt + 1) * NT], KT.R()), K["lnq"].t[:, 1:2], cs, sn, next_tm())
                        for a in range(NT // 128):
                            ps = c.psum()
                            for kc in range(8):
                                c.op("pe", lambda e: e.matmul(ps.t[:, 0:256], ht.t[:, kc, a * 128:(a + 1) * 128], wv.t[:, kc, :], start=(kc == 0), stop=(kc == 7)),
                                     R=[wv.R(), ht.R()], W=[ps.R()], inc=(kc == 7))
                            c.op("dve", lambda e: e.tensor_copy(out=Vx.t[:, tt * (NT // 128) + a, :, 0:64], in_=ps.t[:, 0:256].rearrange("p (g d) -> p g d", g=4)),
                                 R=[ps.R()], W=[Vx.R()])
                    c.barrier()
                with ExitStack() as e2:
                    wq = c.sb(e2, [128, 8, 1024], BF16, "wq")
                    wob = [c.sb(e2, [128, 1024], BF16, "wob") for _ in range(2)]
                    QT2 = [c.sb(e2, [128, 16, NT], BF16, "QP") for _ in range(2)]
                    for QT_ in QT2:
                        c.op("pool", lambda e: e.memset(QT_.t[:], 0.0), W=[QT_.R(h) for h in range(16)])
                    OT = c.sb(e2, [128, 8, NT], BF16, "OT")
                    PT = [c.sb(e2, [128, 512], BF16, "PT") for _ in range(3)]
                    rec = [c.sb(e2, [128, 512], F32, "rec") for _ in range(2)]
                    xr2 = [c.sb(e2, [128, 512], F32, "xr2") for _ in range(2)]
                    c.dma("sp", wq.t[:], S["w_q"][ai].rearrange("m p kc j -> p m (kc j)"),
                          R=[self.wr((("w_q", ai), m)) for m in range(8)], W=[wq.R()])
                    banks = c.psum_banks
                    loop_banks = banks[0:3]
                    prep_banks = banks[3:5]
                    rot = {"loop": 0, "prep": 0}

                    def use(which):
                        c.psum_banks = loop_banks if which == "loop" else prep_banks
                        c.psum_next = rot[which]

                    def save(which):
                        rot[which] = c.psum_next

                    def prep_pieces(tt, QTt):
                        P = []

                        def p_load():
                            c.dma("sp", xt.t[:], self.XT[s][:, tt * NT:(tt + 1) * NT].rearrange("(kc p) t -> p kc t", p=128), W=[xt.R()])
                            c.dma("sp", cs.t[:], I["cosT"][:, tt * NT:(tt + 1) * NT], W=[cs.R()])
                            c.dma("sp", sn.t[:], I["sinT"][:, tt * NT:(tt + 1) * NT], W=[sn.R()])
                        P.append(p_load)
                        P.append(lambda: self.rmsnorm(es, xt, ht, gcol, NT, tmp=ntmp))
                        for m in range(8):
                            H = {}
                            tm = tms[m % 2]
                            qraw, sqf, ln, rs, qn, t1, t2 = tm

                            def pa(m=m, H=H):
                                ps = H["ps"] = c.psum()
                                for kc in range(8):
                                    c.op("pe", lambda e: e.matmul(ps.t[:], wq.t[:, m, kc * 128:(kc + 1) * 128], ht.t[:, kc, :], start=(kc == 0), stop=(kc == 7)),
                                         R=[wq.R(), ht.R()], W=[ps.R()], inc=(kc == 7))

                            def pa2(H=H, qraw=qraw, sqf=sqf):
                                ps = H["ps"]
                                c.op("act", lambda e: e.copy(out=qraw.t[:], in_=ps.t[:]), R=[ps.R()], W=[qraw.R()])
                                c.op("act", lambda e: e.activation(out=sqf.t[:], in_=ps.t[:], func=AF.Square), R=[ps.R()], W=[sqf.R()])

                            def pb(H=H, sqf=sqf):
                                p2 = H["p2"] = c.psum()
                                c.op("pe", lambda e: e.matmul(p2.t[:], K["blk"].t[:], sqf.t[:], start=True, stop=True), R=[K["blk"].R(), sqf.R()], W=[p2.R()])

                            def pc(H=H, ln=ln, rs=rs, qn=qn, qraw=qraw):
                                p2 = H["p2"]
                                c.op("act", lambda e: e.activation(out=ln.t[:], in_=p2.t[:], func=AF.Ln, scale=1.0 / 64, bias=self.eps_col()),
                                     R=[p2.R(), K["eps"].R()], W=[ln.R()])
                                c.op("act", lambda e: e.activation(out=rs.t[:], in_=ln.t[:], func=AF.Exp, scale=-0.5, bias=K["lnq"].t[:, 0:1]),
                                     R=[ln.R(), K["lnq"].R()], W=[rs.R()])
                                c.op("dve", lambda e: e.scalar_tensor_tensor(out=qn.t[:], in0=qraw.t[:], scalar=K["qkg"].t[:, 0:1], in1=rs.t[:], op0=ALU.mult, op1=ALU.mult),
                                     R=[qraw.R(), rs.R(), K["qkg"].R()], W=[qn.R()])

                            def pd(H=H, qn=qn, t1=t1):
                                p3 = H["p3"] = c.psum()
                                c.op("pe", lambda e: e.matmul(p3.t[:], K["Rm"].t[:], qn.t[:], start=True, stop=True), R=[K["Rm"].R(), qn.R()], W=[p3.R()])
                                c.op("pool", lambda e: e.tensor_tensor(out=t1.t[:], in0=qn.t[:], in1=cs.t[:], op=ALU.mult), R=[qn.R(), cs.R()], W=[t1.R()])

                            def pe_(m=m, H=H, t1=t1, t2=t2):
                                p3 = H["p3"]
                                c.op("dve", lambda e: e.tensor_tensor(out=t2.t[:], in0=p3.t[:], in1=sn.t[:], op=ALU.mult), R=[p3.R(), sn.R()], W=[t2.R()])
                                for hh in range(2):
                                    h = 2 * m + hh
                                    kh = (h // 4) % 2
                                    c.op("pool", lambda e: e.tensor_tensor(out=QTt.t[kh * 64:(kh + 1) * 64, h, :], in0=t1.t[hh * 64:(hh + 1) * 64, :],
                                                                          in1=t2.t[hh * 64:(hh + 1) * 64, :], op=ALU.add),
                                         R=[t1.R(), t2.R()], W=[QTt.R(h)])
                            P += [pa, pa2, pb, pc, pd, pe_]
                        return P

                    def run_piece(p):
                        save("loop")
                        use("prep")
                        p()
                        save("prep")
                        use("loop")

                    use("loop")
                    for p in prep_pieces(0, QT2[0]):
                        run_piece(p)
                    for tt in range(ntile):
                        QT = QT2[tt % 2]
                        pend = prep_pieces(tt + 1, QT2[(tt + 1) % 2]) if tt + 1 < ntile else []
                        every = max(1, (16 * nkc) // (len(pend) + 1)) if pend else 0
                        for mo in range(2):
                            c.dma("sp", wob[mo].t[:], S["w_aout"][ai, mo].rearrange("p kc j -> p (kc j)"), R=[self.wr((("w_aout", ai), mo))], W=[wob[mo].R()])
                        c.dma("sp", xr2[0].t[:], self.XT[s][0:128, tt * NT:(tt + 1) * NT], W=[xr2[0].R()])
                        it = 0
                        for h in range(16):
                            m, half, kv = h // 2, h % 2, h // 4
                            hs = slice(half * 64, (half + 1) * 64)
                            po = self.pacc[h % 2]
                            pS = {}

                            def qk(kc):
                                p = c.psum()
                                pS[kc] = p
                                c.op("pe", lambda e: e.matmul(p.t[:], KT.t[:, kv // 2, kc * 128:(kc + 1) * 128], QT.t[:, h, :], start=True, stop=True),
                                     R=[KT.R(), QT.R(h)], W=[p.R()])

                            qk(0)
                            if nkc > 1:
                                qk(1)
                            for kc in range(nkc):
                                p = pS.pop(kc)
                                pt = PT[kc % 3]
                                c.op("act", lambda e: e.activation(out=pt.t[:], in_=p.t[:], func=AF.Exp), R=[p.R()], W=[pt.R()])
                                if kc + 2 < nkc:
                                    qk(kc + 2)
                                c.op("pe", lambda e: e.matmul(po.t[:], Vx.t[:, kc, kv, :], pt.t[:], start=(kc == 0), stop=(kc == nkc - 1)),
                                     R=[Vx.R(), pt.R()], W=[po.R()])
                                it += 1
                                if pend and it % every == 0:
                                    run_piece(pend.pop(0))
                            rc = rec[h % 2]
                            c.op("dve", lambda e: e.reciprocal(out=rc.t[0:64, :], in_=po.t[64:128, :]), R=[po.R()], W=[rc.R()])
                            c.op("dve", lambda e: e.tensor_tensor(out=OT.t[hs, m, :], in0=po.t[0:64, :], in1=rc.t[0:64, :], op=ALU.mult),
                                 R=[po.R(), rc.R()], W=[OT.R()])
                        while pend:
                            run_piece(pend.pop(0))
                        for mo in range(8):
                            wb = wob[mo % 2]
                            xr = xr2[mo % 2]
                            if mo + 1 < 8:
                                c.dma("sp", xr2[(mo + 1) % 2].t[:], self.XT[s][(mo + 1) * 128:(mo + 2) * 128, tt * NT:(tt + 1) * NT], W=[xr2[(mo + 1) % 2].R()])
                            ps = c.psum()
                            for kc in range(8):
                                c.op("pe", lambda e: e.matmul(ps.t[:], wb.t[:, kc * 128:(kc + 1) * 128], OT.t[:, kc, :], start=(kc == 0), stop=(kc == 7)),
                                     R=[wb.R(), OT.R()], W=[ps.R()], inc=(kc == 7))
                            if mo + 2 < 8:
                                c.dma("sp", wb.t[:], S["w_aout"][ai, mo + 2].rearrange("p kc j -> p (kc j)"), R=[self.wr((("w_aout", ai), mo + 2))], W=[wb.R()])
                            c.op("dve", lambda e: e.tensor_tensor(out=xr.t[:], in0=ps.t[:], in1=xr.t[:], op=ALU.add), R=[ps.R()], W=[xr.R()])
                            c.dma("pool", self.XT[s][mo * 128:(mo + 1) * 128, tt * NT:(tt + 1) * NT], xr.t[:], R=[xr.R()], W=[self.wr(("XTc", s, tt, mo))])
                    save("loop")
                    c.psum_banks = banks
                    c.psum_next = 0
                    c.barrier()


def host_inputs(T, NSEQ, layers, x_slots, inp):
    L = len(layers)
    f32 = np.float32
    m = {}
    m["x"] = np.ascontiguousarray(x_slots, dtype=f32)
    ng = np.asarray(inp["norm_g"], f32)[:L]
    m["norm_g"] = np.ascontiguousarray(ng.reshape(L, 3, 8, 128).transpose(3, 0, 1, 2).reshape(128, L * 24))
    for k in ("ffn_w_gate", "ffn_w_up", "ffn_w_down"):
        m[k] = np.ascontiguousarray(np.asarray(inp[k], f32)[:L])
    ns = max(sum(1 for l in layers if l[1] == "ssm"), 1)
    na = max(sum(1 for l in layers if l[1] == "attn"), 1)
    m["ssm_w_in"] = np.ascontiguousarray(np.asarray(inp["ssm_w_in"], f32)[:ns])
    cw = np.asarray(inp["ssm_conv_w"], f32)[:ns]
    m["ssm_conv_w"] = np.ascontiguousarray(cw.reshape(ns, 5, 32, 128).transpose(3, 0, 2, 1).reshape(128, ns * 32 * 5))
    cb = np.asarray(inp["ssm_conv_b"], f32)[:ns]
    m["ssm_conv_b"] = np.ascontiguousarray(cb.reshape(ns, 32, 128).transpose(2, 0, 1).reshape(128, ns * 32))
    dtb = np.asarray(inp["ssm_dt_bias"], f32)[:ns].reshape(ns, 64)
    m["ssm_dtb_col"] = np.ascontiguousarray(dtb.T)
    m["ssm_dtb_row"] = np.ascontiguousarray(dtb)
    m["ssm_alog_row"] = np.ascontiguousarray(np.asarray(inp["ssm_A_log"], f32)[:ns].reshape(ns, 64))
    Dv = np.asarray(inp["ssm_D"], f32)[:ns]
    m["ssm_D_col"] = np.ascontiguousarray(np.repeat(Dv, 64, axis=1).reshape(ns, 16, 128).transpose(2, 0, 1).reshape(128, ns * 16))
    ngm = np.asarray(inp["ssm_norm_g"], f32)[:ns]
    m["ssm_ng_col"] = np.ascontiguousarray(ngm.reshape(ns, 16, 128).transpose(2, 0, 1).reshape(128, ns * 16))
    m["ssm_w_out"] = np.ascontiguousarray(np.asarray(inp["ssm_w_out"], f32)[:ns])
    m["attn_w_qkv"] = np.ascontiguousarray(np.asarray(inp["attn_w_qkv"], f32)[:na])
    qg = np.asarray(inp["attn_q_norm"], f32)[:na]
    kg = np.asarray(inp["attn_k_norm"], f32)[:na]
    m["attn_qg_col"] = np.ascontiguousarray(np.concatenate([qg, qg], 1).T)
    m["attn_kg_col"] = np.ascontiguousarray(np.concatenate([kg, kg], 1).T)
    m["attn_w_out"] = np.ascontiguousarray(np.asarray(inp["attn_w_out"], f32)[:na])
    m["final_g"] = np.ascontiguousarray(np.asarray(inp["final_norm"], f32).reshape(8, 128).T)
    m["ident"] = np.eye(128, dtype=f32)
    cosT, sinT, Rm = rope_tables(T)
    m["cosT"], m["sinT"], m["Rm"] = cosT, sinT, Rm
    jj, ii = np.meshgrid(np.arange(128), np.arange(128), indexing="ij")
    m["tri"] = (jj <= ii).astype(f32)
    m["triT"] = (jj >= ii).astype(f32)
    sel = np.zeros((32, 32, 128), f32)
    for r in range(32):
        sel[r, r, :] = 1.0
    m["sel"] = sel.reshape(32, 32 * 128)
    m["sel3"] = np.ascontiguousarray(np.concatenate([sel, sel, sel], 0).reshape(96, 32 * 128))
    return m


FULL_LAYERS = [(True, "ssm", True), (True, "attn", True), (True, "ssm", True), (True, "attn", True)]
_CACHE = {}


def run(T, NSEQ, layers, slots_per_core, inp, final=True):
    key = (T, NSEQ, tuple(layers), final)
    if key not in _CACHE:
        _CACHE[key] = Builder(T, NSEQ, layers, final).build()
    nc = _CACHE[key]
    in_maps = [host_inputs(T, NSEQ, layers, xs, inp) for xs in slots_per_core]
    res = run_bass_kernel_spmd(nc, in_maps, core_ids=list(range(len(slots_per_core))))
    return [r["y"] for r in res.results]


def kernel(**inputs):
    xp = np.asarray(inputs["x_prompt"], np.float32)
    xs = np.asarray(inputs["x_sample"], np.float32)
    seqs = [xp[i] for i in range(xp.shape[0])] + [xs[i] for i in range(xs.shape[0])]
    T = xp.shape[1]
    slots = []
    for cidx in range(8):
        a = seqs[cidx]
        b = seqs[8 + cidx] if 8 + cidx < len(seqs) else seqs[cidx]
        slots.append(np.stack([a, b], 0))
    ys = run(T, 2, FULL_LAYERS, slots, inputs)
    out = [None] * len(seqs)
    for cidx in range(8):
        out[cidx] = ys[cidx][0]
        if 8 + cidx < len(seqs):
            out[8 + cidx] = ys[cidx][1]
    nP = xp.shape[0]
    y_prompt = np.stack(out[:nP], 0).astype(np.float32)
    y_sample = np.stack(out[nP:], 0).astype(np.float32)
    return (y_prompt, y_sample)
```

```python
import numpy as np
from contextlib import ExitStack
import concourse.bass as bass
import concourse.mybir as mybir
from concourse.bass_utils import run_bass_kernel_spmd

F32 = mybir.dt.float32
BF16 = mybir.dt.bfloat16
ALU = mybir.AluOpType
AF = mybir.ActivationFunctionType

D_MODEL = 1024
D_FF = 2816
D_INNER = 2048
SSM_HEADS = 32
SSM_GROUPS = 8
D_STATE = 128
CONV_DIM = 4096
SSM_IN_DIM = 6208
N_HEADS = 16
N_KV = 4
HD = 64
EPS = 1e-6
GRID_W = 64
ROPE_THETA = 10000.0
NDMASEM = 12
SSM_SKIP = set()
PRO_SLOT = 0
PRO_RATE = 1
NCH = 3
PCS_DEDICATED = True


class Res:
    __slots__ = ("w", "r", "excl")

    def __init__(self, excl=False):
        self.w = None
        self.r = {}
        self.excl = excl


class Tile:
    def __init__(self, t, excl=False):
        self.t = t
        self.res = {}
        self.excl = excl

    def R(self, key=None):
        if self.excl:
            key = None
        r = self.res.get(key)
        if r is None:
            r = self.res[key] = Res(self.excl)
        return r


class PsumHandle:
    def __init__(self, bank):
        self.bank = bank

    @property
    def t(self):
        assert self.bank.owner is self, "stale PSUM handle"
        return self.bank.t

    def R(self, key=None):
        assert self.bank.owner is self, "stale PSUM handle"
        return self.bank.R(key)


class EW:
    def __init__(self, name, eng, sem):
        self.name = name
        self.eng = eng
        self.sem = sem
        self.count = 0
        self.waited = {}
        self.pending = False


class Ctx:
    def __init__(self, nc, es):
        self.nc = nc
        self.es = es
        self.E = {}
        for name, eng in (("pe", nc.tensor), ("act", nc.scalar), ("dve", nc.vector),
                          ("pool", nc.gpsimd), ("sp", nc.sync)):
            sem = es.enter_context(nc.semaphore("s_" + name))
            self.E[name] = EW(name, eng, sem)
        self.dsem = {}
        for q in ("sp", "pool"):
            self.dsem[q] = [[es.enter_context(nc.semaphore(f"d_{q}{i}")), 0] for i in range(NDMASEM)]
        self.dnext = {"sp": 0, "pool": 0}
        self.semid = {}
        self.psum_banks = []
        self.psum_next = 0
        self.uid = 0

    def _wait(self, E, dep):
        if dep is None:
            return
        sem, val = dep
        if sem is E.sem and val > E.count:
            return
        k = id(sem)
        if E.waited.get(k, 0) >= val:
            return
        E.eng.wait_ge(sem, val)
        E.waited[k] = val

    def _hazards(self, E, R, W):
        for r in R:
            self._wait(E, r.w)
            if r.excl:
                for d in r.r.values():
                    if d[0] is not E.sem:
                        self._wait(E, d)
        for w in W:
            self._wait(E, w.w)
            for d in w.r.values():
                self._wait(E, d)

    def _record(self, dep, R, W):
        k = id(dep[0])
        for r in R:
            old = r.r.get(k)
            if old is None or old[1] < dep[1]:
                r.r[k] = dep
        for w in W:
            w.w = dep
            w.r = {}

    def op(self, en, fn, R=(), W=(), inc=True):
        E = self.E[en]
        self._hazards(E, R, W)
        ins = fn(E.eng)
        dep = (E.sem, E.count + 1)
        if inc:
            ins.then_inc(E.sem, 1)
            E.count += 1
            E.pending = False
        else:
            E.pending = True
        self._record(dep, R, W)
        return ins

    def dma(self, q, out, in_, R=(), W=()):
        E = self.E[q]
        self._hazards(E, R, W)
        i = self.dnext[q]
        self.dnext[q] = (i + 1) % NDMASEM
        ent = self.dsem[q][i]
        if ent[1] > 0:
            self._wait(E, (ent[0], ent[1]))
        ent[1] += 16
        E.eng.dma_start(out=out, in_=in_).then_inc(ent[0], 16)
        dep = (ent[0], ent[1])
        self._record(dep, R, W)

    def barrier(self):
        deps = []
        for E in self.E.values():
            assert not E.pending
            if E.count:
                deps.append((E.sem, E.count))
        for q in self.dsem:
            for sem, val in self.dsem[q]:
                if val:
                    deps.append((sem, val))
        for E in self.E.values():
            for d in deps:
                if d[0] is not E.sem:
                    self._wait(E, d)

    def sb(self, es, shape, dt, name=None):
        self.uid += 1
        t = es.enter_context(self.nc.sbuf_tensor(f"{name or 't'}_{self.uid}", list(shape), dt))
        return Tile(t)

    def psum(self):
        b = self.psum_banks[self.psum_next]
        self.psum_next = (self.psum_next + 1) % len(self.psum_banks)
        h = PsumHandle(b)
        b.owner = h
        return h


def rope_tables(T):
    rows = T // GRID_W
    r_idx, c_idx = np.meshgrid(np.arange(rows), np.arange(GRID_W), indexing="ij")
    r_idx = r_idx.reshape(-1).astype(np.float32)
    c_idx = c_idx.reshape(-1).astype(np.float32)
    inv_freq = (np.float32(ROPE_THETA) ** (-np.arange(0, 32, 2, dtype=np.float32) / np.float32(32))).astype(np.float32)
    ang = np.stack([r_idx[:, None] * inv_freq, c_idx[:, None] * inv_freq], axis=1).astype(np.float32)
    cos = np.cos(ang).astype(np.float32)
    sin = np.sin(ang).astype(np.float32)
    cosT = np.zeros((64, T), np.float32)
    sinT = np.zeros((64, T), np.float32)
    for a in range(2):
        for s in range(2):
            cosT[a * 32 + s * 16:a * 32 + s * 16 + 16, :] = cos[:, a, :].T
            sinT[a * 32 + s * 16:a * 32 + s * 16 + 16, :] = sin[:, a, :].T
    cosT = np.concatenate([cosT, cosT], 0)
    sinT = np.concatenate([sinT, sinT], 0)
    Rm = np.zeros((128, 128), np.float32)
    for hh in range(2):
        for a in range(2):
            for i in range(16):
                d0 = hh * 64 + a * 32 + i
                d1 = d0 + 16
                Rm[d1, d0] = -1.0
                Rm[d0, d1] = 1.0
    return np.ascontiguousarray(cosT), np.ascontiguousarray(sinT), Rm


class Builder:
    def __init__(self, T, NSEQ, layers, final=True):
        self.T = T
        self.NSEQ = NSEQ
        self.layers = layers
        self.final = final
        self.n_ffn = len(layers)
        self.n_ssm = sum(1 for l in layers if l[1] == "ssm")
        self.n_att = sum(1 for l in layers if l[1] == "attn")

    def build(self):
        T, NSEQ = self.T, self.NSEQ
        L = len(self.layers)
        nc = bass.Bass("TRN2", target_bir_lowering=False)
        self.nc = nc
        din = lambda n, s: nc.dram_tensor(n, list(s), F32, kind="ExternalInput").ap()
        dsc = lambda n, s, dt=F32: nc.dram_tensor(n, list(s), dt, kind="Internal").ap()
        I = self.I = {}
        I["x"] = din("x", (NSEQ, T, D_MODEL))
        I["norm_g"] = din("norm_g", (128, L * 3 * 8))
        I["ffn_w_gate"] = din("ffn_w_gate", (L, 2, D_MODEL, D_FF))
        I["ffn_w_up"] = din("ffn_w_up", (L, 2, D_MODEL, D_FF))
        I["ffn_w_down"] = din("ffn_w_down", (L, 2, D_FF, D_MODEL))
        ns, na = max(self.n_ssm, 1), max(self.n_att, 1)
        I["ssm_w_in"] = din("ssm_w_in", (ns, D_MODEL, SSM_IN_DIM))
        I["ssm_conv_w"] = din("ssm_conv_w", (128, ns * 32 * 5))
        I["ssm_conv_b"] = din("ssm_conv_b", (128, ns * 32))
        I["ssm_dtb_col"] = din("ssm_dtb_col", (64, ns))
        I["ssm_dtb_row"] = din("ssm_dtb_row", (ns, 64))
        I["ssm_alog_row"] = din("ssm_alog_row", (ns, 64))
        I["ssm_D_col"] = din("ssm_D_col", (128, ns * 16))
        I["ssm_ng_col"] = din("ssm_ng_col", (128, ns * 16))
        I["ssm_w_out"] = din("ssm_w_out", (ns, D_INNER, D_MODEL))
        I["attn_w_qkv"] = din("attn_w_qkv", (na, D_MODEL, 1536))
        I["attn_qg_col"] = din("attn_qg_col", (128, na))
        I["attn_kg_col"] = din("attn_kg_col", (128, na))
        I["attn_w_out"] = din("attn_w_out", (na, D_MODEL, D_MODEL))
        I["final_g"] = din("final_g", (128, 8))
        I["ident"] = din("ident", (128, 128))
        I["cosT"] = din("cosT", (128, T))
        I["sinT"] = din("sinT", (128, T))
        I["Rm"] = din("Rm", (128, 128))
        I["tri"] = din("tri", (128, 128))
        I["triT"] = din("triT", (128, 128))
        I["sel"] = din("sel", (32, 32 * 128))
        I["sel3"] = din("sel3", (96, 32 * 128))
        self.y = nc.dram_tensor("y", [NSEQ, T, D_MODEL], F32, kind="ExternalOutput").ap()
        self.XT = dsc("XT", (NSEQ, D_MODEL, T))
        S = self.S = {}
        S["wg"] = dsc("wg_b", (L, 2, 22, 128, 8, 128), BF16)
        S["wu"] = dsc("wu_b", (L, 2, 22, 128, 8, 128), BF16)
        S["wd"] = dsc("wd_b", (L, 2, 8, 128, 22, 128), BF16)
        S["w_in"] = dsc("w_in_b", (ns, 48, 128, 8, 128), BF16)
        S["w_dt"] = dsc("w_dt_b", (ns, 128, 8, 64), BF16)
        S["w_sout"] = dsc("w_sout_b", (ns, 8, 128, 16, 128), BF16)
        S["w_q"] = dsc("w_q_b", (na, 8, 128, 8, 128), BF16)
        S["w_k"] = dsc("w_k_b", (na, 2, 128, 8, 128), BF16)
        S["w_v"] = dsc("w_v_b", (na, 128, 8, 256), BF16)
        S["w_aout"] = dsc("w_aout_b", (na, 8, 128, 8, 128), BF16)
        if self.n_ssm:
            S["zs"] = dsc("zs", (NSEQ, D_INNER, T), BF16)
            S["xbc"] = dsc("xbc", (NSEQ, CONV_DIM, T), F32)
            S["xsT"] = dsc("xsT", (NSEQ, D_INNER, T), BF16)
            S["xs_tm"] = dsc("xs_tm", (NSEQ, T, D_INNER), BF16)
            S["B_tm"] = dsc("B_tm", (NSEQ, T, 1024), BF16)
            S["BT"] = dsc("BT", (NSEQ, 1024, T), BF16)
            S["CT"] = dsc("CT", (NSEQ, 1024, T), BF16)
            S["yf"] = dsc("yf", (NSEQ, D_INNER, T), F32)

        with ExitStack() as es:
            c = self.c = Ctx(nc, es)
            psbig = es.enter_context(nc.psum_tensor("psbig", [128, 5 * 512], F32))
            self.psbig = psbig
            for i in range(5):
                c.psum_banks.append(Tile(psbig[:, i * 512:(i + 1) * 512], excl=True))
            self.pacc = [Tile(es.enter_context(nc.psum_tensor(f"pacc{i}", [128, 512], F32)), excl=True) for i in range(2)]
            self.psb = Tile(es.enter_context(nc.psum_tensor("psb", [128, 1024], BF16)))
            K = self.K = {}
            K["ident"] = c.sb(es, [128, 128], F32, "ident")
            K["identb"] = c.sb(es, [128, 128], BF16, "identb")
            K["onesb"] = c.sb(es, [128, 128], BF16, "onesb")
            K["ones"] = c.sb(es, [128, 128], F32, "ones")
            K["blk"] = c.sb(es, [128, 128], F32, "blk")
            K["norm_g"] = c.sb(es, [128, L * 3 * 8], F32, "normg")
            K["final_g"] = c.sb(es, [128, 8], F32, "finalg")
            c.dma("sp", K["ident"].t[:], I["ident"][:, :], W=[K["ident"].R()])
            c.dma("sp", K["norm_g"].t[:], I["norm_g"][:, :], W=[K["norm_g"].R()])
            c.dma("sp", K["final_g"].t[:], I["final_g"][:, :], W=[K["final_g"].R()])
            c.op("dve", lambda e: e.tensor_copy(out=K["identb"].t[:], in_=K["ident"].t[:]),
                 R=[K["ident"].R()], W=[K["identb"].R()])
            c.op("dve", lambda e: e.memset(K["onesb"].t[:], 1.0), W=[K["onesb"].R()])
            c.op("dve", lambda e: e.memset(K["ones"].t[:], 1.0), W=[K["ones"].R()])
            c.op("dve", lambda e: e.memset(K["blk"].t[:], 0.0), W=[K["blk"].R()])
            c.op("dve", lambda e: e.memset(K["blk"].t[0:64, 0:64], 1.0), W=[K["blk"].R()])
            c.op("dve", lambda e: e.memset(K["blk"].t[64:128, 64:128], 1.0), W=[K["blk"].R()])

            self.wres = {}
            self.convert_weights()
            self.transpose_in()
            c.barrier()
            fi = si = ai = 0
            for li, (f1, mixer, f2) in enumerate(self.layers):
                if f1:
                    self.ffn(li, 0)
                if mixer == "ssm":
                    self.ssm(li, si)
                    si += 1
                elif mixer == "attn":
                    self.attn(li, ai)
                    ai += 1
                if f2:
                    self.ffn(li, 1)
            self.final_out()
            c.barrier()
        return nc

    def wr(self, key):
        r = self.wres.get(key)
        if r is None:
            r = self.wres[key] = Res()
        return r

    def convert_weights(self):
        c, I, S = self.c, self.I, self.S

        def blocked(dst, src, nm, key):
            for m in range(nm):
                c.dma("pool", dst[m], src[:, m * 128:(m + 1) * 128].rearrange("(kc p) j -> p kc j", p=128),
                      W=[self.wr((key, m))])

        si = ai = 0
        for li, (f1, mixer, f2) in enumerate(self.layers):
            for j, on in ((0, f1), (1, f2)):
                if not on:
                    continue
                blocked(S["wg"][li, j], I["ffn_w_gate"][li, j], 22, ("wg", li, j))
                blocked(S["wu"][li, j], I["ffn_w_up"][li, j], 22, ("wu", li, j))
                blocked(S["wd"][li, j], I["ffn_w_down"][li, j], 8, ("wd", li, j))
            if mixer == "ssm":
                blocked(S["w_in"][si], I["ssm_w_in"][si, :, 0:6144], 48, ("w_in", si))
                c.dma("pool", S["w_dt"][si], I["ssm_w_in"][si, :, 6144:6208].rearrange("(kc p) j -> p kc j", p=128),
                      W=[self.wr(("w_dt", si))])
                blocked(S["w_sout"][si], I["ssm_w_out"][si], 8, ("w_sout", si))
                si += 1
            elif mixer == "attn":
                blocked(S["w_q"][ai], I["attn_w_qkv"][ai, :, 0:1024], 8, ("w_q", ai))
                blocked(S["w_k"][ai], I["attn_w_qkv"][ai, :, 1024:1280], 2, ("w_k", ai))
                c.dma("pool", S["w_v"][ai], I["attn_w_qkv"][ai, :, 1280:1536].rearrange("(kc p) j -> p kc j", p=128),
                      W=[self.wr(("w_v", ai))])
                blocked(S["w_aout"][ai], I["attn_w_out"][ai], 8, ("w_aout", ai))
                ai += 1

    def xres(self, s, i):
        return self.wr(("XT", s, i))

    def xres_range(self, s, t0, n):
        return [self.xres(s, i) for i in range(t0 // 512, (t0 + n + 511) // 512)]

    def rmsnorm(self, es, xt, ht, gcol, NT, nkc=8, tmp=None):
        c, K = self.c, self.K
        sq, ln, rs = tmp
        for st in range(NT // 512):
            sl = slice(st * 512, (st + 1) * 512)
            ps = c.psum()
            for kc in range(nkc):
                c.op("act", lambda e: e.activation(out=sq[kc % 2].t[:], in_=xt.t[:, kc, sl], func=AF.Square),
                     R=[xt.R()], W=[sq[kc % 2].R()])
                c.op("pe", lambda e: e.matmul(ps.t[:], K["onesb"].t[:], sq[kc % 2].t[:], start=(kc == 0), stop=(kc == nkc - 1)),
                     R=[K["onesb"].R(), sq[kc % 2].R()], W=[ps.R()])
            c.op("act", lambda e: e.activation(out=ln.t[:], in_=ps.t[:], func=AF.Ln, scale=1.0 / (nkc * 128), bias=self.eps_col()),
                 R=[ps.R(), self.K["eps"].R()], W=[ln.R()])
            c.op("act", lambda e: e.activation(out=rs.t[:], in_=ln.t[:], func=AF.Exp, scale=-0.5),
                 R=[ln.R()], W=[rs.R()])
            for kc in range(nkc):
                c.op("dve", lambda e: e.scalar_tensor_tensor(out=ht.t[:, kc, sl], in0=xt.t[:, kc, sl], scalar=gcol(kc),
                                                             in1=rs.t[:], op0=ALU.mult, op1=ALU.mult),
                     R=[xt.R(), rs.R(), self.K["norm_g"].R(), self.K["final_g"].R()], W=[ht.R()])

    def eps_col(self):
        return self.K["eps"].t[:, 0:1]

    def norm_tmp(self, es):
        c = self.c
        sq = [c.sb(es, [128, 512], BF16, "sq") for _ in range(2)]
        ln = c.sb(es, [128, 512], F32, "ln")
        rs = c.sb(es, [128, 512], F32, "rs")
        return sq, ln, rs

    def transpose_in(self):
        c, I, K = self.c, self.I, self.K
        T = self.T
        with ExitStack() as es:
            K["eps"] = c.sb(self.c.es, [128, 1], F32, "eps")
            c.op("dve", lambda e: e.memset(K["eps"].t[:], EPS), W=[K["eps"].R()])
            xin = [c.sb(es, [128, 4, 1024], F32, "xin") for _ in range(2)]
            xo = [c.sb(es, [128, 8, 512], F32, "xo") for _ in range(2)]
            n = 0
            for s in range(self.NSEQ):
                for tt in range(T // 512):
                    b = n % 2
                    n += 1
                    c.dma("sp", xin[b].t[:], I["x"][s, tt * 512:(tt + 1) * 512, :].rearrange("(a p) f -> p a f", p=128),
                          W=[xin[b].R()])
                    for kc in range(8):
                        ps = c.psum()
                        for a in range(4):
                            c.op("pe", lambda e: e.transpose(ps.t[:, a * 128:(a + 1) * 128], xin[b].t[:, a, kc * 128:(kc + 1) * 128], K["ident"].t[:]),
                                 R=[xin[b].R(), K["ident"].R()], W=[ps.R()], inc=(a == 3))
                        eng = "act" if kc % 2 else "dve"
                        if eng == "act":
                            c.op("act", lambda e: e.copy(out=xo[b].t[:, kc, :], in_=ps.t[:]), R=[ps.R()], W=[xo[b].R()])
                        else:
                            c.op("dve", lambda e: e.tensor_copy(out=xo[b].t[:, kc, :], in_=ps.t[:]), R=[ps.R()], W=[xo[b].R()])
                    c.dma("pool", self.XT[s][:, tt * 512:(tt + 1) * 512].rearrange("(kc p) t -> p kc t", p=128), xo[b].t[:],
                          R=[xo[b].R()], W=[self.xres(s, tt)])

    def final_out(self):
        c, K = self.c, self.K
        T = self.T
        c.barrier()
        with ExitStack() as es:
            xt = [c.sb(es, [128, 8, 512], F32, "fx") for _ in range(2)]
            hn = [c.sb(es, [128, 8, 512], F32, "fh") for _ in range(2)]
            yo = [c.sb(es, [128, 4, 1024], F32, "fy") for _ in range(2)]
            tmp = self.norm_tmp(es)
            n = 0
            for s in range(self.NSEQ):
                for tt in range(T // 512):
                    b = n % 2
                    n += 1
                    c.dma("sp", xt[b].t[:], self.XT[s][:, tt * 512:(tt + 1) * 512].rearrange("(kc p) t -> p kc t", p=128),
                          R=[self.xres(s, tt)], W=[xt[b].R()])
                    if self.final:
                        self.rmsnorm(es, xt[b], hn[b], lambda kc: K["final_g"].t[:, kc:kc + 1], 512, tmp=tmp)
                        src = hn[b]
                    else:
                        src = xt[b]
                    for a in range(4):
                        for half in range(2):
                            ps = c.psum()
                            for q in range(4):
                                kc = half * 4 + q
                                c.op("pe", lambda e: e.transpose(ps.t[:, q * 128:(q + 1) * 128], src.t[:, kc, a * 128:(a + 1) * 128], K["ident"].t[:]),
                                     R=[src.R(), K["ident"].R()], W=[ps.R()], inc=(q == 3))
                            if half:
                                c.op("act", lambda e: e.copy(out=yo[b].t[:, a, half * 512:(half + 1) * 512], in_=ps.t[:]), R=[ps.R()], W=[yo[b].R()])
                            else:
                                c.op("dve", lambda e: e.tensor_copy(out=yo[b].t[:, a, half * 512:(half + 1) * 512], in_=ps.t[:]), R=[ps.R()], W=[yo[b].R()])
                    c.dma("pool", self.y[s, tt * 512:(tt + 1) * 512, :].rearrange("(a p) f -> p a f", p=128), yo[b].t[:],
                          R=[yo[b].R()], W=[self.wr(("y", s, tt))])

    def ffn(self, li, j):
        c, K, S = self.c, self.K, self.S
        T = self.T
        NT = 1024 if T >= 1024 else 512
        nst = NT // 512
        gbase = (li * 3 + (0 if j == 0 else 2)) * 8
        c.barrier()
        with ExitStack() as es:
            xt = [c.sb(es, [128, 8, NT], F32, "x") for _ in range(2)]
            ht = [c.sb(es, [128, 8, NT], BF16, "h") for _ in range(2)]
            act = c.sb(es, [128, 22, NT], BF16, "act")
            NB = 4
            wbuf = [c.sb(es, [128, 22 * 128], BF16, "w") for _ in range(NB)]
            sg = [c.sb(es, [128, 512], F32, "sg") for _ in range(2)]
            tmp = self.norm_tmp(es)
            tiles = [(s, i) for s in range(self.NSEQ) for i in range(T // NT)]
            stream = []
            for _ in tiles:
                for m in range(22):
                    stream.append(("gu", m))
                for m in range(8):
                    stream.append(("d", m))
            state = {"issued": 0}

            def issue():
                k = state["issued"]
                if k >= len(stream):
                    return
                kind, m = stream[k]
                wb = wbuf[k % NB]
                if kind == "gu":
                    c.dma("sp", wb.t[:, 0:1024], S["wg"][li, j, m].rearrange("p kc j -> p (kc j)"),
                          R=[self.wr((("wg", li, j), m))], W=[wb.R()])
                    c.dma("sp", wb.t[:, 1024:2048], S["wu"][li, j, m].rearrange("p kc j -> p (kc j)"),
                          R=[self.wr((("wu", li, j), m))], W=[wb.R()])
                else:
                    c.dma("sp", wb.t[:, :], S["wd"][li, j, m].rearrange("p kc j -> p (kc j)"),
                          R=[self.wr((("wd", li, j), m))], W=[wb.R()])
                state["issued"] = k + 1

            used = {"n": 0}

            def nextw():
                k = used["n"]
                used["n"] += 1
                while state["issued"] < min(k + NB, len(stream)):
                    issue()
                return wbuf[k % NB]

            def load_norm(idx):
                s, i = tiles[idx]
                b = idx % 2
                c.dma("sp", xt[b].t[:], self.XT[s][:, i * NT:(i + 1) * NT].rearrange("(kc p) t -> p kc t", p=128),
                      R=self.xres_range(s, i * NT, NT), W=[xt[b].R()])
                self.rmsnorm(es, xt[b], ht[b], lambda kc: K["norm_g"].t[:, gbase + kc:gbase + kc + 1], NT, tmp=tmp)

            load_norm(0)
            for idx, (s, i) in enumerate(tiles):
                b = idx % 2
                h = ht[b]
                for m in range(22):
                    wb = nextw()
                    for st in range(nst):
                        sl = slice(st * 512, (st + 1) * 512)
                        pg = c.psum()
                        pu = c.psum()
                        for kc in range(8):
                            c.op("pe", lambda e: e.matmul(pg.t[:], wb.t[:, kc * 128:(kc + 1) * 128], h.t[:, kc, sl], start=(kc == 0), stop=(kc == 7)),
                                 R=[wb.R(), h.R()], W=[pg.R()], inc=(kc == 7))
                        for kc in range(8):
                            c.op("pe", lambda e: e.matmul(pu.t[:], wb.t[:, 1024 + kc * 128:1024 + (kc + 1) * 128], h.t[:, kc, sl], start=(kc == 0), stop=(kc == 7)),
                                 R=[wb.R(), h.R()], W=[pu.R()], inc=(kc == 7))
                        sgt = sg[(m * nst + st) % 2]
                        c.op("act", lambda e: e.activation(out=sgt.t[:], in_=pg.t[:], func=AF.Silu), R=[pg.R()], W=[sgt.R()])
                        c.op("dve", lambda e: e.tensor_tensor(out=act.t[:, m, sl], in0=sgt.t[:], in1=pu.t[:], op=ALU.mult),
                             R=[sgt.R(), pu.R()], W=[act.R(("w", st))])
                if idx + 1 < len(tiles):
                    load_norm(idx + 1)
                for m in range(8):
                    wb = nextw()
                    for st in range(nst):
                        sl = slice(st * 512, (st + 1) * 512)
                        pd = c.psum()
                        for kc in range(22):
                            c.op("pe", lambda e: e.matmul(pd.t[:], wb.t[:, kc * 128:(kc + 1) * 128], act.t[:, kc, sl], start=(kc == 0), stop=(kc == 21)),
                                 R=[wb.R(), act.R(("w", st))], W=[pd.R()], inc=(kc == 21))
                        c.op("dve", lambda e: e.scalar_tensor_tensor(out=xt[b].t[:, m, sl], in0=pd.t[:], scalar=0.5, in1=xt[b].t[:, m, sl],
                                                                     op0=ALU.mult, op1=ALU.add),
                             R=[pd.R()], W=[xt[b].R()])
                c.dma("pool", self.XT[s][:, i * NT:(i + 1) * NT].rearrange("(kc p) t -> p kc t", p=128), xt[b].t[:],
                      R=[xt[b].R()], W=self.xres_range(s, i * NT, NT))

    def ssm(self, li, si):
        c, K, S, I = self.c, self.K, self.S, self.I
        T = self.T
        NT = 512
        ntile = T // NT
        nch = T // 128
        gbase = (li * 3 + 1) * 8
        gcol = lambda kc: K["norm_g"].t[:, gbase + kc:gbase + kc + 1]
        c.barrier()
        with ExitStack() as es:
            DT = c.sb(es, [128, nch, 64], F32, "DT")
            cw = c.sb(es, [128, 160], F32, "cw")
            cb = c.sb(es, [128, 32], F32, "cb")
            dtb = c.sb(es, [128, 64], F32, "dtb")
            Arow = c.sb(es, [128, 64], F32, "Arow")
            Dcol = c.sb(es, [128, 16], F32, "Dcol")
            ngc = c.sb(es, [128, 16], F32, "ngc")
            tri = [c.sb(es, [128, 128], F32, "tri") for _ in range(2)]
            one1 = c.sb(es, [128, 1], F32, "one1")
            c.dma("sp", cw.t[:], I["ssm_conv_w"][:, si * 160:(si + 1) * 160], W=[cw.R()])
            c.dma("sp", cb.t[:], I["ssm_conv_b"][:, si * 32:(si + 1) * 32], W=[cb.R()])
            c.dma("sp", dtb.t[:], I["ssm_dtb_row"][si:si + 1, :].to_broadcast([128, 64]), W=[dtb.R()])
            c.dma("sp", Arow.t[:], I["ssm_alog_row"][si:si + 1, :].to_broadcast([128, 64]), W=[Arow.R()])
            c.dma("sp", Dcol.t[:], I["ssm_D_col"][:, si * 16:(si + 1) * 16], W=[Dcol.R()])
            c.dma("sp", ngc.t[:], I["ssm_ng_col"][:, si * 16:(si + 1) * 16], W=[ngc.R()])
            c.dma("sp", tri[0].t[:], I["tri"][:, :], W=[tri[0].R()])
            c.dma("sp", tri[1].t[:], I["triT"][:, :], W=[tri[1].R()])
            c.op("dve", lambda e: e.memset(one1.t[:], 1.0), W=[one1.R()])
            c.op("act", lambda e: e.activation(out=Arow.t[:], in_=Arow.t[:], func=AF.Exp), R=[], W=[Arow.R()])
            c.op("dve", lambda e: e.tensor_scalar(out=Arow.t[:], in0=Arow.t[:], scalar1=-1.0, scalar2=None, op0=ALU.mult), W=[Arow.R()])
            for s in range(self.NSEQ):
                if 1 not in SSM_SKIP:
                    self.ssm_s1(es, s, si, gcol, DT, dtb, one1)
                c.barrier()
                if 2 not in SSM_SKIP:
                    self.ssm_s2(s, si, cw, cb)
                c.barrier()
                if 3 not in SSM_SKIP:
                    self.ssm_s3(s, li, si, DT, Arow, Dcol, ngc, tri)
                c.barrier()

    def ssm_s1(self, es0, s, si, gcol, DT, dtb, one1):
        c, K, S = self.c, self.K, self.S
        T = self.T
        NT = 512
        ntile = T // NT
        with ExitStack() as es:
            Win = c.sb(es, [128, 48, 1024], BF16, "Win")
            xt = c.sb(es, [128, 8, NT], F32, "sx")
            hv = [c.sb(es, [128, 8, NT], BF16, "shv") for _ in range(2)]
            ntmp = self.norm_tmp(es)
            wdt = c.sb(es, [128, 8, 64], BF16, "wdt")
            zo = [c.sb(es, [128, 4, 512], BF16, "zo") for _ in range(2)]
            xo = [c.sb(es, [128, 4, 512], F32, "xo") for _ in range(2)]
            t64 = [c.sb(es, [128, 64], F32, "t64") for _ in range(2)]
            c.dma("sp", wdt.t[:], S["w_dt"][si], R=[self.wr(("w_dt", si))], W=[wdt.R()])

            def load_norm(tt):
                c.dma("sp", xt.t[:], self.XT[s][:, tt * NT:(tt + 1) * NT].rearrange("(kc p) t -> p kc t", p=128),
                      R=self.xres_range(s, tt * NT, NT), W=[xt.R()])
                self.rmsnorm(es, xt, hv[tt % 2], gcol, NT, tmp=ntmp)

            load_norm(0)
            for q6 in range(6):
                c.dma("sp", Win.t[:, q6 * 8:(q6 + 1) * 8, :], S["w_in"][si, q6 * 8:(q6 + 1) * 8].rearrange("m p kc j -> p m (kc j)"),
                      R=[self.wr((("w_in", si), mm)) for mm in range(q6 * 8, (q6 + 1) * 8)], W=[Win.R(q6)])
            n = 0
            for tt in range(ntile):
                h = hv[tt % 2]
                if tt + 1 < ntile:
                    load_norm(tt + 1)
                for a in range(4):
                    ch = tt * 4 + a
                    ps = c.psum()
                    for kc in range(8):
                        c.op("pe", lambda e: e.matmul(ps.t[:, 0:64], h.t[:, kc, a * 128:(a + 1) * 128], wdt.t[:, kc, :], start=(kc == 0), stop=(kc == 7)),
                             R=[h.R(), wdt.R()], W=[ps.R()], inc=(kc == 7))
                    t = t64[ch % 2]
                    c.op("dve", lambda e: e.tensor_tensor(out=t.t[:], in0=ps.t[:, 0:64], in1=dtb.t[:], op=ALU.add), R=[ps.R(), dtb.R()], W=[t.R()])
                    c.op("act", lambda e: e.activation(out=t.t[:], in_=t.t[:], func=AF.Exp), W=[t.R()])
                    c.op("act", lambda e: e.activation(out=DT.t[:, ch, :], in_=t.t[:], func=AF.Ln, bias=one1.t[:, 0:1]), R=[t.R(), one1.R()], W=[DT.R()])
                for m in range(48):
                    ps = c.psum()
                    for kc in range(8):
                        c.op("pe", lambda e: e.matmul(ps.t[:], Win.t[:, m, kc * 128:(kc + 1) * 128], h.t[:, kc, :], start=(kc == 0), stop=(kc == 7)),
                             R=[Win.R(m // 8), h.R()], W=[ps.R()], inc=(kc == 7))
                    q, grp = m % 4, m // 4
                    if m < 16:
                        o = zo[grp % 2]
                        c.op("act", lambda e: e.activation(out=o.t[:, q, :], in_=ps.t[:], func=AF.Silu), R=[ps.R()], W=[o.R()])
                        if q == 3:
                            c.dma("pool", S["zs"][s, grp * 512:(grp + 1) * 512, tt * NT:(tt + 1) * NT].rearrange("(q p) t -> p q t", p=128), o.t[:],
                                  R=[o.R()], W=[self.wr(("zs", s, tt))])
                    else:
                        o = xo[grp % 2]
                        c.op("dve", lambda e: e.tensor_copy(out=o.t[:, q, :], in_=ps.t[:]), R=[ps.R()], W=[o.R()])
                        if q == 3:
                            c.dma("pool", S["xbc"][s, (grp - 4) * 512:(grp - 3) * 512, tt * NT:(tt + 1) * NT].rearrange("(q p) t -> p q t", p=128), o.t[:],
                                  R=[o.R()], W=[self.wr(("xbc", s))])

    def ssm_s2(self, s, si, cw, cb):
        c, K, S = self.c, self.K, self.S
        T = self.T
        NT = 512
        ntile = T // NT
        NBUF = 4
        with ExitStack() as es:
            cin = [c.sb(es, [128, NT + 4], F32, "cin") for _ in range(NBUF)]
            acc = [c.sb(es, [128, NT], F32, "cacc") for _ in range(NBUF)]
            ptm = [c.sb(es, [128, NT], F32, "cptm") for _ in range(NBUF)]
            fm = c.sb(es, [128, 32, NT], BF16, "fm")
            xtm = c.sb(es, [128, 4, 2048], BF16, "cxtm")
            btm = c.sb(es, [128, 4, 1024], BF16, "cbtm")
            its = [(tt, cc) for tt in range(ntile) for cc in range(32)]

            def load(n):
                tt, cc = its[n]
                t0 = tt * NT
                ci = cin[n % NBUF]
                lo = 2 if tt == 0 else 0
                hi = NT + 2 if tt == ntile - 1 else NT + 4
                if tt == 0:
                    c.op("pool", lambda e: e.memset(ci.t[:, 0:2], 0.0), W=[ci.R()])
                if tt == ntile - 1:
                    c.op("pool", lambda e: e.memset(ci.t[:, NT + 2:NT + 4], 0.0), W=[ci.R()])
                c.dma("sp", ci.t[:, lo:hi], S["xbc"][s, cc * 128:(cc + 1) * 128, t0 - 2 + lo:t0 - 2 + hi], R=[self.wr(("xbc", s))], W=[ci.R()])

            def ident(n):
                tt_, cc_ = its[n]
                ci_ = cin[n % NBUF]
                ac_ = acc[n % NBUF]
                c.op("act", lambda e: e.activation(out=ac_.t[:], in_=ci_.t[:, 2:NT + 2], func=AF.Identity, scale=cw.t[:, cc_ * 5 + 2:cc_ * 5 + 3], bias=cb.t[:, cc_:cc_ + 1]),
                     R=[ci_.R(), cw.R(), cb.R()], W=[ac_.R()])

            for n in range(min(NBUF - 1, len(its))):
                load(n)
            ident(0)
            for n, (tt, cc) in enumerate(its):
                t0 = tt * NT
                if n + NBUF - 1 < len(its):
                    load(n + NBUF - 1)
                if n + 1 < len(its):
                    ident(n + 1)
                ci = cin[n % NBUF]
                ac = acc[n % NBUF]
                pt = ptm[n % NBUF]
                wcol = lambda j: cw.t[:, cc * 5 + j:cc * 5 + j + 1]
                for jn, j in enumerate((0, 1, 3, 4)):
                    src = ac if jn == 0 else pt
                    c.op("dve", lambda e: e.scalar_tensor_tensor(out=pt.t[:], in0=ci.t[:, j:j + NT], scalar=wcol(j), in1=src.t[:], op0=ALU.mult, op1=ALU.add),
                         R=[ci.R(), cw.R(), src.R()], W=[pt.R()])
                c.op("act", lambda e: e.activation(out=fm.t[:, cc, :], in_=pt.t[:], func=AF.Silu), R=[pt.R()], W=[fm.R(cc)])
                if cc < 24:
                    half = cc % 2
                    pb = self.psb
                    for a in range(4):
                        c.op("pe", lambda e: e.transpose(pb.t[:, half * 512 + a * 128:half * 512 + (a + 1) * 128], fm.t[:, cc, a * 128:(a + 1) * 128], K["identb"].t[:]),
                             R=[fm.R(cc), K["identb"].R()], W=[pb.R(half)], inc=(a == 3))
                    if cc < 16:
                        dst, dr = xtm.t[:, :, cc * 128:(cc + 1) * 128], xtm.R()
                    else:
                        dst, dr = btm.t[:, :, (cc - 16) * 128:(cc - 15) * 128], btm.R()
                    c.op("act", lambda e: e.copy(out=dst, in_=pb.t[:, half * 512:(half + 1) * 512].rearrange("p (a f) -> p a f", a=4)),
                         R=[pb.R(half)], W=[dr])
                if cc == 31:
                    allfm = [fm.R(q) for q in range(32)]
                    c.dma("pool", S["xsT"][s][:, t0:t0 + NT].rearrange("(cc p) t -> p cc t", p=128), fm.t[:, 0:16, :], R=allfm[0:16], W=[self.wr(("xsT", s, tt))])
                    c.dma("pool", S["BT"][s][:, t0:t0 + NT].rearrange("(cc p) t -> p cc t", p=128), fm.t[:, 16:24, :], R=allfm[16:24], W=[self.wr(("BT", s, tt))])
                    c.dma("pool", S["CT"][s][:, t0:t0 + NT].rearrange("(cc p) t -> p cc t", p=128), fm.t[:, 24:32, :], R=allfm[24:32], W=[self.wr(("CT", s, tt))])
                    c.dma("pool", S["xs_tm"][s, t0:t0 + NT, :].rearrange("(a p) f -> p a f", p=128), xtm.t[:], R=[xtm.R()], W=[self.wr(("xs_tm", s, tt))])
                    c.dma("pool", S["B_tm"][s, t0:t0 + NT, :].rearrange("(a p) f -> p a f", p=128), btm.t[:], R=[btm.R()], W=[self.wr(("B_tm", s, tt))])

    def ssm_s3(self, s, li, si, DT, Arow, Dcol, ngc, tri):
        c = self.c
        base_banks = c.psum_banks
        c.psum_banks = base_banks + (self.pacc[0:1] if PCS_DEDICATED else self.pacc)
        c.psum_next = 0
        try:
            self._ssm_s3(s, li, si, DT, Arow, Dcol, ngc, tri)
        finally:
            c.psum_banks = base_banks
            c.psum_next = 0

    def _ssm_s3(self, s, li, si, DT, Arow, Dcol, ngc, tri):
        c, K, S = self.c, self.K, self.S
        T = self.T
        NT = 512
        ntile = T // NT
        with ExitStack() as es:
            St = c.sb(es, [128, 8, 256], F32, "St")
            Sb = c.sb(es, [128, 8, 256], BF16, "Sb")
            xtm = c.sb(es, [128, 4, 2048], BF16, "xtm")
            btm = c.sb(es, [128, 4, 1024], BF16, "btm")
            BTt = c.sb(es, [128, 8, NT], BF16, "BTt")
            CTt = c.sb(es, [128, 8, NT], BF16, "CTt")
            yacc = c.sb(es, [128, 16, NT], F32, "yacc")
            CH = []
            for _pb in range(NCH):
                CH.append((c.sb(es, [128, 32], F32, "atm"), c.sb(es, [128, 32], F32, "ncs"), c.sb(es, [128, 32], F32, "d1"),
                           c.sb(es, [128, 32], F32, "wend"), c.sb(es, [128, 32], F32, "dec"), c.sb(es, [96, 128], BF16, "cs3"),
                           c.sb(es, [128, 32], F32, "nb"), None))
            r1 = c.sb(es, [32, 128], F32, "r1")
            r2 = c.sb(es, [32, 128], F32, "r2")
            midt = c.sb(es, [32, 128], BF16, "midt")
            lot = c.sb(es, [32, 128], BF16, "lot")
            lndt = c.sb(es, [128, 32], F32, "lndt")
            XW = [c.sb(es, [128, 256], BF16, "xwg") for _ in range(2)]
            Sel3 = c.sb(es, [96, 32 * 128], BF16, "sel3")
            c.dma("pool", Sel3.t[:], self.I["sel3"][:, :], W=[Sel3.R()])
            cbm = [c.sb(es, [128, 128], F32, "cbm") for _ in range(2)]
            ECS = [c.sb(es, [128, 512], F32, "ECS") for _ in range(2)]
            Lt = [c.sb(es, [128, 512], F32, "Lt") for _ in range(2)]
            Mt = [c.sb(es, [128, 512], BF16, "Mt") for _ in range(2)]
            Cs = [c.sb(es, [128, 512], BF16, "Cs") for _ in range(2)]
            stmp = [c.sb(es, [128, 256], F32, "stmp") for _ in range(2)]
            zt = c.sb(es, [128, 16, NT], BF16, "zt")
            xf = c.sb(es, [128, 16, NT], BF16, "xf")
            xr = c.sb(es, [128, 8, NT], F32, "xr")
            wso = [c.sb(es, [128, 2048], BF16, "wso") for _ in range(2)]
            ntmp = self.norm_tmp(es)
            rsb = c.sb(es, [128, 512], F32, "rsb")

            def prologue_pieces(d, tt, a, pb):
                ch = tt * 4 + a
                dt = DT.t[:, ch, d * 32:(d + 1) * 32]
                trm = tri[d]
                atm, ncs, d1, wend, dec, cs3, nb, _ = CH[pb]
                hold = {}

                def p0():
                    c.op("dve", lambda e: e.tensor_tensor(out=atm.t[:], in0=dt, in1=Arow.t[:, d * 32:(d + 1) * 32], op=ALU.mult),
                         R=[DT.R(), Arow.R()], W=[atm.R()])
                    pcs = hold["pcs"] = self.pacc[1] if PCS_DEDICATED else c.psum()
                    c.op("pe", lambda e: e.matmul(pcs.t[:, 0:32], trm.t[:], atm.t[:], start=True, stop=True), R=[trm.R(), atm.R()], W=[pcs.R()], inc=False)
                    c.op("pe", lambda e: e.matmul(pcs.t[:, 32:64], K["ones"].t[:], atm.t[:], start=True, stop=True), R=[K["ones"].R(), atm.R()], W=[pcs.R()], inc=False)
                    c.op("pe", lambda e: e.matmul(pcs.t[0:32, 64:192], atm.t[:], trm.t[:], start=True, stop=True), R=[trm.R(), atm.R()], W=[pcs.R()])

                def p1():
                    pcs = hold["pcs"]
                    c.op("dve", lambda e: e.tensor_scalar(out=ncs.t[:], in0=pcs.t[:, 0:32], scalar1=-1.0, scalar2=None, op0=ALU.mult), R=[pcs.R()], W=[ncs.R()])
                    c.op("dve", lambda e: e.tensor_tensor(out=d1.t[:], in0=pcs.t[:, 32:64], in1=ncs.t[:], op=ALU.add), R=[pcs.R(), ncs.R()], W=[d1.R()])
                    c.op("act", lambda e: e.activation(out=lndt.t[:], in_=dt, func=AF.Ln), R=[DT.R()], W=[lndt.R()])

                def p2():
                    pcs = hold["pcs"]
                    c.op("act", lambda e: e.activation(out=d1.t[:], in_=d1.t[:], func=AF.Exp), W=[d1.R()])
                    c.op("act", lambda e: e.activation(out=dec.t[:], in_=pcs.t[:, 32:64], func=AF.Exp), R=[pcs.R()], W=[dec.R()])
                    c.op("act", lambda e: e.copy(out=cs3.t[0:32, :], in_=pcs.t[0:32, 64:192]), R=[pcs.R()], W=[cs3.R()])
                    c.op("dve", lambda e: e.tensor_tensor(out=nb.t[:], in0=lndt.t[:], in1=ncs.t[:], op=ALU.add), R=[lndt.R(), ncs.R()], W=[nb.R()])

                def p3():
                    pcs = hold["pcs"]
                    c.op("dve", lambda e: e.tensor_tensor(out=wend.t[:], in0=d1.t[:], in1=dt, op=ALU.mult), R=[d1.R(), DT.R()], W=[wend.R()])
                    c.op("dve", lambda e: e.tensor_tensor(out=r1.t[:], in0=pcs.t[0:32, 64:192], in1=cs3.t[0:32, :], op=ALU.subtract), R=[pcs.R(), cs3.R()], W=[r1.R()])

                def p4():
                    c.op("act", lambda e: e.copy(out=midt.t[:], in_=r1.t[:]), R=[r1.R()], W=[midt.R()])

                def p5():
                    c.op("dve", lambda e: e.tensor_tensor(out=r2.t[:], in0=r1.t[:], in1=midt.t[:], op=ALU.subtract), R=[r1.R(), midt.R()], W=[r2.R()])
                    c.op("dve", lambda e: e.tensor_copy(out=cs3.t[32:64, :], in_=midt.t[:]), R=[midt.R()], W=[cs3.R()])

                def p6():
                    c.op("act", lambda e: e.copy(out=lot.t[:], in_=r2.t[:]), R=[r2.R()], W=[lot.R()])

                def p7():
                    c.op("dve", lambda e: e.tensor_copy(out=cs3.t[64:96, :], in_=lot.t[:]), R=[lot.R()], W=[cs3.R()])

                return [p0, p1, p2, p3, p4, p5, p6, p7]

            PS = {}

            def st1(d, tt, a, pb, g):
                asl = slice(a * 128, (a + 1) * 128)
                cs3 = CH[pb][5]
                pcb = c.psum()
                c.op("pe", lambda e: e.matmul(pcb.t[:, 0:128], BTt.t[:, g, asl], CTt.t[:, g, asl], start=True, stop=True), R=[BTt.R(a), CTt.R(a)], W=[pcb.R()])
                pcr = c.psum()
                for rr in range(4):
                    r = g * 4 + rr
                    c.op("pe", lambda e: e.matmul(pcr.t[:, rr * 128:(rr + 1) * 128], Sel3.t[:, r * 128:(r + 1) * 128], cs3.t[:], start=True, stop=True),
                         R=[Sel3.R(), cs3.R()], W=[pcr.R()], inc=(rr == 3))
                PS[(a, g)] = [pcb, pcr, None]

            def st2(d, tt, a, pb, g):
                trm = tri[d]
                ncs = CH[pb][6]
                k = g % 2
                pcb, pcr, _ = PS[(a, g)]
                c.op("act", lambda e: e.activation(out=ECS[k].t[:], in_=pcr.t[:], func=AF.Exp), R=[pcr.R()], W=[ECS[k].R()])
                for rr in range(4):
                    r = g * 4 + rr
                    c.op("act", lambda e: e.activation(out=Lt[k].t[:, rr * 128:(rr + 1) * 128], in_=pcr.t[:, rr * 128:(rr + 1) * 128], func=AF.Exp, bias=ncs.t[:, r:r + 1]),
                         R=[pcr.R(), ncs.R()], W=[Lt[k].R()])
                c.op("dve", lambda e: e.tensor_tensor(out=cbm[k].t[:], in0=pcb.t[:, 0:128], in1=trm.t[:], op=ALU.mult), R=[pcb.R(), trm.R()], W=[cbm[k].R()])

            def st3(d, tt, a, pb, g):
                asl = slice(a * 128, (a + 1) * 128)
                k = g % 2
                c.op("dve", lambda e: e.scalar_tensor_tensor(out=Mt[k].t[:].rearrange("p (r i) -> p r i", r=4), in0=Lt[k].t[:].rearrange("p (r i) -> p r i", r=4), scalar=1e30,
                                                             in1=cbm[k].t[:].unsqueeze(1).to_broadcast([128, 4, 128]), op0=ALU.min, op1=ALU.mult),
                     R=[Lt[k].R(), cbm[k].R()], W=[Mt[k].R()])
                c.op("pool", lambda e: e.tensor_tensor(out=Cs[k].t[:].rearrange("p (r i) -> p r i", r=4), in0=ECS[k].t[:].rearrange("p (r i) -> p r i", r=4),
                                                      in1=CTt.t[:, g, asl].unsqueeze(1).to_broadcast([128, 4, 128]), op=ALU.mult),
                     R=[ECS[k].R(), CTt.R(a)], W=[Cs[k].R()])
                wend = CH[pb][3]
                c.op("pool", lambda e: e.tensor_tensor(out=XW[k].t[:].rearrange("p (r d) -> p r d", r=4), in0=xtm.t[:, a, g * 256:(g + 1) * 256].rearrange("p (r d) -> p r d", r=4),
                                                      in1=wend.t[:, g * 4:(g + 1) * 4].unsqueeze(2).to_broadcast([128, 4, 64]), op=ALU.mult),
                     R=[xtm.R(a), wend.R()], W=[XW[k].R()])

            def st4(d, tt, a, pb, g):
                k = g % 2
                py = c.psum()
                PS[(a, g)][2] = py
                for rr in range(4):
                    r = g * 4 + rr
                    half = rr % 2
                    osl = py.t[half * 64:(half + 1) * 64, (rr // 2) * 128:(rr // 2 + 1) * 128]
                    c.op("pe", lambda e: e.matmul(osl, xtm.t[:, a, r * 64:(r + 1) * 64], Mt[k].t[:, rr * 128:(rr + 1) * 128], start=True, stop=False),
                         R=[xtm.R(a), Mt[k].R()], W=[py.R()], inc=False)
                    c.op("pe", lambda e: e.matmul(osl, Sb.t[:, g, rr * 64:(rr + 1) * 64], Cs[k].t[:, rr * 128:(rr + 1) * 128], start=False, stop=True),
                         R=[Sb.R(g), Cs[k].R()], W=[py.R()], inc=(rr == 3))
                pst = c.psum()
                PS[(a, g)].append(pst)
                c.op("pe", lambda e: e.matmul(pst.t[:, 0:256], btm.t[:, a, g * 128:(g + 1) * 128], XW[k].t[:], start=True, stop=True),
                     R=[btm.R(a), XW[k].R()], W=[pst.R()])

            def st5(d, tt, a, pb, g):
                asl = slice(a * 128, (a + 1) * 128)
                dec = CH[pb][4]
                k = g % 2
                _ps = PS.pop((a, g))
                py, pst = _ps[2], _ps[3]
                ydst = yacc.t[:, 2 * g:2 * g + 2, asl]
                ysrc = py.t[:, 0:256].rearrange("p (c i) -> p c i", c=2)
                if d == 0:
                    c.op("act", lambda e: e.copy(out=ydst, in_=ysrc), R=[py.R()], W=[yacc.R(g)])
                else:
                    c.op("dve", lambda e: e.tensor_tensor(out=ydst, in0=ysrc, in1=ydst, op=ALU.add), R=[py.R()], W=[yacc.R(g)])
                st = stmp[k]
                c.op("dve", lambda e: e.tensor_tensor(out=st.t[:].rearrange("p (r d) -> p r d", r=4), in0=St.t[:, g, :].rearrange("p (r d) -> p r d", r=4),
                                                     in1=dec.t[:, g * 4:(g + 1) * 4].unsqueeze(2).to_broadcast([128, 4, 64]), op=ALU.mult),
                     R=[St.R(g), dec.R()], W=[st.R()])
                c.op("dve", lambda e: e.tensor_tensor(out=St.t[:, g, :], in0=pst.t[:, 0:256], in1=st.t[:], op=ALU.add), R=[pst.R(), st.R()], W=[St.R(g)])
                if d == 0:
                    c.op("dve", lambda e: e.tensor_copy(out=Sb.t[:, g, :], in_=St.t[:, g, :]), R=[St.R(g)], W=[Sb.R(g)])
                else:
                    c.op("act", lambda e: e.copy(out=Sb.t[:, g, :], in_=St.t[:, g, :]), R=[St.R(g)], W=[Sb.R(g)])

            def run_tile(d, tt):
                aorder = list(range(4)) if d == 0 else [3, 2, 1, 0]
                items = [(ai_, a, g) for ai_, a in enumerate(aorder) for g in range(8)]
                n = len(items)
                base = self._chunk_ctr
                self._chunk_ctr += 4
                for p in prologue_pieces(d, tt, aorder[0], base % NCH):
                    p()
                stages = [st1, st2, st3, st4, st5]
                pend = []
                for t in range(n + 4):
                    if t < n:
                        ai_, a, g = items[t]
                        if g == PRO_SLOT and ai_ + 1 < 4:
                            pend = prologue_pieces(d, tt, aorder[ai_ + 1], (base + ai_ + 1) % NCH)
                        for _ in range(PRO_RATE):
                            if pend:
                                pend.pop(0)()
                    for si_, fn in enumerate(stages):
                        j = t - si_
                        if 0 <= j < n:
                            ai_, a, g = items[j]
                            fn(d, tt, a, (base + ai_) % NCH, g)
                assert not pend

            def load_tile(tt, aorder):
                t0 = tt * NT
                for a in aorder:
                    ta = t0 + a * 128
                    c.dma("sp", xtm.t[:, a, :], S["xs_tm"][s, ta:ta + 128, :], R=[self.wr(("xs_tm", s, tt))], W=[xtm.R(a)])
                    c.dma("sp", btm.t[:, a, :], S["B_tm"][s, ta:ta + 128, :], R=[self.wr(("B_tm", s, tt))], W=[btm.R(a)])
                    c.dma("sp", BTt.t[:, :, a * 128:(a + 1) * 128], S["BT"][s][:, ta:ta + 128].rearrange("(cc p) t -> p cc t", p=128), R=[self.wr(("BT", s, tt))], W=[BTt.R(a)])
                    c.dma("sp", CTt.t[:, :, a * 128:(a + 1) * 128], S["CT"][s][:, ta:ta + 128].rearrange("(cc p) t -> p cc t", p=128), R=[self.wr(("CT", s, tt))], W=[CTt.R(a)])

            self._chunk_ctr = 0
            for d in range(2):
                c.op("dve", lambda e: e.memset(St.t[:], 0.0), W=[St.R(g) for g in range(8)])
                c.op("dve", lambda e: e.memset(Sb.t[:], 0.0), W=[Sb.R(g) for g in range(8)])
                order = range(ntile) if d == 0 else range(ntile - 1, -1, -1)
                order = list(order)
                aord = list(range(4)) if d == 0 else [3, 2, 1, 0]
                load_tile(order[0], aord)
                for oi, tt in enumerate(order):
                    t0 = tt * NT
                    if d == 1:
                        c.dma("sp", yacc.t[:], S["yf"][s][:, t0:t0 + NT].rearrange("(cc p) t -> p cc t", p=128), R=[self.wr(("yf", s, tt))], W=[yacc.R(g) for g in range(8)])
                        c.dma("sp", xf.t[:], S["xsT"][s][:, t0:t0 + NT].rearrange("(cc p) t -> p cc t", p=128), R=[self.wr(("xsT", s, tt))], W=[xf.R(q_) for q_ in range(16)])
                        c.dma("sp", zt.t[:], S["zs"][s][:, t0:t0 + NT].rearrange("(cc p) t -> p cc t", p=128), R=[self.wr(("zs", s, tt))], W=[zt.R()])
                        c.dma("sp", xr.t[:], self.XT[s][:, t0:t0 + NT].rearrange("(kc p) t -> p kc t", p=128), R=self.xres_range(s, t0, NT), W=[xr.R()])
                        for mo in range(2):
                            c.dma("sp", wso[mo].t[:], S["w_sout"][si, mo].rearrange("p kc j -> p (kc j)"), R=[self.wr((("w_sout", si), mo))], W=[wso[mo].R()])
                    run_tile(d, tt)
                    if d == 0:
                        c.dma("pool", S["yf"][s][:, t0:t0 + NT].rearrange("(cc p) t -> p cc t", p=128), yacc.t[:], R=[yacc.R(g) for g in range(8)], W=[self.wr(("yf", s, tt))])
                        if oi + 1 < len(order):
                            load_tile(order[oi + 1], aord)
                        continue
                    if oi + 1 < len(order):
                        load_tile(order[oi + 1], aord)
                    sq, ln, rs = ntmp
                    rs2 = [rs, rsb]

                    def epiA(g):
                        ps = c.psum()
                        for q in range(2):
                            cc = 2 * g + q
                            c.op("dve", lambda e: e.scalar_tensor_tensor(out=yacc.t[:, cc, :], in0=xf.t[:, cc, :], scalar=Dcol.t[:, cc:cc + 1], in1=yacc.t[:, cc, :], op0=ALU.mult, op1=ALU.add),
                                 R=[xf.R(cc), Dcol.R()], W=[yacc.R(g)])
                            c.op("pool", lambda e: e.tensor_tensor(out=yacc.t[:, cc, :], in0=yacc.t[:, cc, :], in1=zt.t[:, cc, :], op=ALU.mult), R=[zt.R()], W=[yacc.R(g)])
                            c.op("act", lambda e: e.activation(out=sq[q].t[:], in_=yacc.t[:, cc, :], func=AF.Square), R=[yacc.R(g)], W=[sq[q].R()])
                            c.op("pe", lambda e: e.matmul(ps.t[:], K["onesb"].t[:], sq[q].t[:], start=(q == 0), stop=(q == 1)), R=[K["onesb"].R(), sq[q].R()], W=[ps.R()])
                        c.op("act", lambda e: e.activation(out=ln.t[:], in_=ps.t[:], func=AF.Ln, scale=1.0 / 256, bias=self.eps_col()), R=[ps.R(), K["eps"].R()], W=[ln.R()])
                        c.op("act", lambda e: e.activation(out=rs2[g % 2].t[:], in_=ln.t[:], func=AF.Exp, scale=-0.5), R=[ln.R()], W=[rs2[g % 2].R()])

                    def epiB(g):
                        for q in range(2):
                            cc = 2 * g + q
                            c.op("dve", lambda e: e.scalar_tensor_tensor(out=xf.t[:, cc, :], in0=yacc.t[:, cc, :], scalar=ngc.t[:, cc:cc + 1], in1=rs2[g % 2].t[:], op0=ALU.mult, op1=ALU.mult),
                                 R=[yacc.R(g), ngc.R(), rs2[g % 2].R()], W=[xf.R(cc)])

                    epiA(0)
                    for g in range(8):
                        if g + 1 < 8:
                            epiA(g + 1)
                        epiB(g)
                    for mo in range(8):
                        wb = wso[mo % 2]
                        ps = c.psum()
                        for kc in range(16):
                            c.op("pe", lambda e: e.matmul(ps.t[:], wb.t[:, kc * 128:(kc + 1) * 128], xf.t[:, kc, :], start=(kc == 0), stop=(kc == 15)),
                                 R=[wb.R(), xf.R(kc)], W=[ps.R()], inc=(kc == 15))
                        if mo + 2 < 8:
                            c.dma("sp", wb.t[:], S["w_sout"][si, mo + 2].rearrange("p kc j -> p (kc j)"), R=[self.wr((("w_sout", si), mo + 2))], W=[wb.R()])
                        c.op("dve", lambda e: e.tensor_tensor(out=xr.t[:, mo, :], in0=ps.t[:], in1=xr.t[:, mo, :], op=ALU.add), R=[ps.R()], W=[xr.R()])
                    c.dma("pool", self.XT[s][:, t0:t0 + NT].rearrange("(kc p) t -> p kc t", p=128), xr.t[:], R=[xr.R()], W=self.xres_range(s, t0, NT))

    def qk_post(self, ps, gcol, out_ap, bias_col, cs, sn, tm):
        c, K = self.c, self.K
        qraw, sqf, ln, rs, qn, t1, t2 = tm
        c.op("act", lambda e: e.copy(out=qraw.t[:], in_=ps.t[:]), R=[ps.R()], W=[qraw.R()])
        c.op("act", lambda e: e.activation(out=sqf.t[:], in_=ps.t[:], func=AF.Square), R=[ps.R()], W=[sqf.R()])
        p2 = c.psum()
        c.op("pe", lambda e: e.matmul(p2.t[:], K["blk"].t[:], sqf.t[:], start=True, stop=True), R=[K["blk"].R(), sqf.R()], W=[p2.R()])
        c.op("act", lambda e: e.activation(out=ln.t[:], in_=p2.t[:], func=AF.Ln, scale=1.0 / 64, bias=self.eps_col()),
             R=[p2.R(), K["eps"].R()], W=[ln.R()])
        c.op("act", lambda e: e.activation(out=rs.t[:], in_=ln.t[:], func=AF.Exp, scale=-0.5, bias=bias_col),
             R=[ln.R(), K["lnq"].R()], W=[rs.R()])
        c.op("dve", lambda e: e.scalar_tensor_tensor(out=qn.t[:], in0=qraw.t[:], scalar=gcol, in1=rs.t[:], op0=ALU.mult, op1=ALU.mult),
             R=[qraw.R(), rs.R(), K["qkg"].R()], W=[qn.R()])
        p3 = c.psum()
        c.op("pe", lambda e: e.matmul(p3.t[:], K["Rm"].t[:], qn.t[:], start=True, stop=True), R=[K["Rm"].R(), qn.R()], W=[p3.R()])
        c.op("pool", lambda e: e.tensor_tensor(out=t1.t[:], in0=qn.t[:], in1=cs.t[:], op=ALU.mult), R=[qn.R(), cs.R()], W=[t1.R()])
        c.op("dve", lambda e: e.tensor_tensor(out=t2.t[:], in0=p3.t[:], in1=sn.t[:], op=ALU.mult), R=[p3.R(), sn.R()], W=[t2.R()])
        if callable(out_ap):
            out_ap(t1, t2)
        else:
            c.op("pool", lambda e: e.tensor_tensor(out=out_ap[0], in0=t1.t[:], in1=t2.t[:], op=ALU.add), R=[t1.R(), t2.R()], W=[out_ap[1]])

    def attn(self, li, ai):
        c, K, S, I = self.c, self.K, self.S, self.I
        T = self.T
        NT = 512
        ntile = T // NT
        nkc = T // 128
        gbase = (li * 3 + 1) * 8
        c.barrier()
        with ExitStack() as es:
            KT = c.sb(es, [128, 2, T], BF16, "KT")
            Vx = c.sb(es, [128, nkc, 4, 128], BF16, "Vx")
            xt = c.sb(es, [128, 8, NT], F32, "ax")
            ht = c.sb(es, [128, 8, NT], BF16, "ah")
            cs = c.sb(es, [128, NT], F32, "cos")
            sn = c.sb(es, [128, NT], F32, "sin")
            tms = [[c.sb(es, [128, 512], F32, "qk") for _ in range(7)] for _ in range(2)]
            tmi = [0]

            def next_tm():
                tmi[0] += 1
                return tms[tmi[0] % 2]
            ntmp = self.norm_tmp(es)
            K["Rm"] = c.sb(es, [128, 128], F32, "Rm")
            K["qkg"] = c.sb(es, [128, 2], F32, "qkg")
            qkall = c.sb(es, [128, 2 * max(self.n_att, 1)], F32, "qkall")
            K["lnq"] = c.sb(es, [128, 2], F32, "lnq")
            c.dma("sp", K["Rm"].t[:], I["Rm"][:, :], W=[K["Rm"].R()])
            na_ = max(self.n_att, 1)
            c.dma("sp", qkall.t[:, 0:na_], I["attn_qg_col"][:, :], W=[qkall.R()])
            c.dma("sp", qkall.t[:, na_:2 * na_], I["attn_kg_col"][:, :], W=[qkall.R()])
            c.op("dve", lambda e: e.tensor_copy(out=K["qkg"].t[:, 0:1], in_=qkall.t[:, ai:ai + 1]), R=[qkall.R()], W=[K["qkg"].R()])
            c.op("dve", lambda e: e.tensor_copy(out=K["qkg"].t[:, 1:2], in_=qkall.t[:, na_ + ai:na_ + ai + 1]), R=[qkall.R()], W=[K["qkg"].R()])
            c.op("dve", lambda e: e.memset(K["lnq"].t[:, 0:1], float(np.log(0.125))), W=[K["lnq"].R()])
            c.op("dve", lambda e: e.memset(K["lnq"].t[:, 1:2], 0.0), W=[K["lnq"].R()])
            c.op("dve", lambda e: e.memset(Vx.t[:, :, :, 64:128], 1.0), W=[Vx.R()])
            gcol = lambda kc: K["norm_g"].t[:, gbase + kc:gbase + kc + 1]

            def load_tile(s, tt):
                c.dma("sp", xt.t[:], self.XT[s][:, tt * NT:(tt + 1) * NT].rearrange("(kc p) t -> p kc t", p=128),
                      R=self.xres_range(s, tt * NT, NT), W=[xt.R()])
                c.dma("sp", cs.t[:], I["cosT"][:, tt * NT:(tt + 1) * NT], W=[cs.R()])
                c.dma("sp", sn.t[:], I["sinT"][:, tt * NT:(tt + 1) * NT], W=[sn.R()])
                self.rmsnorm(es, xt, ht, gcol, NT, tmp=ntmp)

            for s in range(self.NSEQ):
                with ExitStack() as e1:
                    wk = c.sb(e1, [128, 2, 1024], BF16, "wk")
                    wv = c.sb(e1, [128, 8, 256], BF16, "wv")
                    c.dma("sp", wk.t[:], S["w_k"][ai].rearrange("m p kc j -> p m (kc j)"),
                          R=[self.wr((("w_k", ai), m)) for m in range(2)], W=[wk.R()])
                    c.dma("sp", wv.t[:], S["w_v"][ai], R=[self.wr(("w_v", ai))], W=[wv.R()])
                    for tt in range(ntile):
                        load_tile(s, tt)
                        for kv in range(2):
                            ps = c.psum()
                            for kc in range(8):
                                c.op("pe", lambda e: e.matmul(ps.t[:], wk.t[:, kv, kc * 128:(kc + 1) * 128], ht.t[:, kc, :], start=(kc == 0), stop=(kc == 7)),
                                     R=[wk.R(), ht.R()], W=[ps.R()], inc=(kc == 7))
                            self.qk_post(ps, K["qkg"].t[:, 1:2], (KT.t[:, kv, tt * NT:(tt + 1) * NT], KT.R()), K["lnq"].t[:, 1:2], cs, sn, next_tm())
                        for a in range(NT // 128):
                            ps = c.psum()
                            for kc in range(8):
                                c.op("pe", lambda e: e.matmul(ps.t[:, 0:256], ht.t[:, kc, a * 128:(a + 1) * 128], wv.t[:, kc, :], start=(kc == 0), stop=(kc == 7)),
                                     R=[wv.R(), ht.R()], W=[ps.R()], inc=(kc == 7))
                            c.op("dve", lambda e: e.tensor_copy(out=Vx.t[:, tt * (NT // 128) + a, :, 0:64], in_=ps.t[:, 0:256].rearrange("p (g d) -> p g d", g=4)),
                                 R=[ps.R()], W=[Vx.R()])
                    c.barrier()
                with ExitStack() as e2:
                    wq = c.sb(e2, [128, 8, 1024], BF16, "wq")
                    wob = [c.sb(e2, [128, 1024], BF16, "wob") for _ in range(2)]
                    QT2 = [c.sb(e2, [128, 16, NT], BF16, "QP") for _ in range(2)]
                    for QT_ in QT2:
                        c.op("pool", lambda e: e.memset(QT_.t[:], 0.0), W=[QT_.R(h) for h in range(16)])
                    OT = c.sb(e2, [128, 8, NT], BF16, "OT")
                    PT = [c.sb(e2, [128, 512], BF16, "PT") for _ in range(3)]
                    rec = [c.sb(e2, [128, 512], F32, "rec") for _ in range(2)]
                    xr2 = [c.sb(e2, [128, 512], F32, "xr2") for _ in range(2)]
                    c.dma("sp", wq.t[:], S["w_q"][ai].rearrange("m p kc j -> p m (kc j)"),
                          R=[self.wr((("w_q", ai), m)) for m in range(8)], W=[wq.R()])
                    banks = c.psum_banks
                    loop_banks = banks[0:3]
                    prep_banks = banks[3:5]
                    rot = {"loop": 0, "prep": 0}

                    def use(which):
                        c.psum_banks = loop_banks if which == "loop" else prep_banks
                        c.psum_next = rot[which]

                    def save(which):
                        rot[which] = c.psum_next

                    def prep_pieces(tt, QTt):
                        P = []

                        def p_load():
                            c.dma("sp", xt.t[:], self.XT[s][:, tt * NT:(tt + 1) * NT].rearrange("(kc p) t -> p kc t", p=128), W=[xt.R()])
                            c.dma("sp", cs.t[:], I["cosT"][:, tt * NT:(tt + 1) * NT], W=[cs.R()])
                            c.dma("sp", sn.t[:], I["sinT"][:, tt * NT:(tt + 1) * NT], W=[sn.R()])
                        P.append(p_load)
                        P.append(lambda: self.rmsnorm(es, xt, ht, gcol, NT, tmp=ntmp))
                        for m in range(8):
                            H = {}
                            tm = tms[m % 2]
                            qraw, sqf, ln, rs, qn, t1, t2 = tm

                            def pa(m=m, H=H):
                                ps = H["ps"] = c.psum()
                                for kc in range(8):
                                    c.op("pe", lambda e: e.matmul(ps.t[:], wq.t[:, m, kc * 128:(kc + 1) * 128], ht.t[:, kc, :], start=(kc == 0), stop=(kc == 7)),
                                         R=[wq.R(), ht.R()], W=[ps.R()], inc=(kc == 7))

                            def pa2(H=H, qraw=qraw, sqf=sqf):
                                ps = H["ps"]
                                c.op("act", lambda e: e.copy(out=qraw.t[:], in_=ps.t[:]), R=[ps.R()], W=[qraw.R()])
                                c.op("act", lambda e: e.activation(out=sqf.t[:], in_=ps.t[:], func=AF.Square), R=[ps.R()], W=[sqf.R()])

                            def pb(H=H, sqf=sqf):
                                p2 = H["p2"] = c.psum()
                                c.op("pe", lambda e: e.matmul(p2.t[:], K["blk"].t[:], sqf.t[:], start=True, stop=True), R=[K["blk"].R(), sqf.R()], W=[p2.R()])

                            def pc(H=H, ln=ln, rs=rs, qn=qn, qraw=qraw):
                                p2 = H["p2"]
                                c.op("act", lambda e: e.activation(out=ln.t[:], in_=p2.t[:], func=AF.Ln, scale=1.0 / 64, bias=self.eps_col()),
                                     R=[p2.R(), K["eps"].R()], W=[ln.R()])
                                c.op("act", lambda e: e.activation(out=rs.t[:], in_=ln.t[:], func=AF.Exp, scale=-0.5, bias=K["lnq"].t[:, 0:1]),
                                     R=[ln.R(), K["lnq"].R()], W=[rs.R()])
                                c.op("dve", lambda e: e.scalar_tensor_tensor(out=qn.t[:], in0=qraw.t[:], scalar=K["qkg"].t[:, 0:1], in1=rs.t[:], op0=ALU.mult, op1=ALU.mult),
                                     R=[qraw.R(), rs.R(), K["qkg"].R()], W=[qn.R()])

                            def pd(H=H, qn=qn, t1=t1):
                                p3 = H["p3"] = c.psum()
                                c.op("pe", lambda e: e.matmul(p3.t[:], K["Rm"].t[:], qn.t[:], start=True, stop=True), R=[K["Rm"].R(), qn.R()], W=[p3.R()])
                                c.op("pool", lambda e: e.tensor_tensor(out=t1.t[:], in0=qn.t[:], in1=cs.t[:], op=ALU.mult), R=[qn.R(), cs.R()], W=[t1.R()])

                            def pe_(m=m, H=H, t1=t1, t2=t2):
                                p3 = H["p3"]
                                c.op("dve", lambda e: e.tensor_tensor(out=t2.t[:], in0=p3.t[:], in1=sn.t[:], op=ALU.mult), R=[p3.R(), sn.R()], W=[t2.R()])
                                for hh in range(2):
                                    h = 2 * m + hh
                                    kh = (h // 4) % 2
                                    c.op("pool", lambda e: e.tensor_tensor(out=QTt.t[kh * 64:(kh + 1) * 64, h, :], in0=t1.t[hh * 64:(hh + 1) * 64, :],
                                                                          in1=t2.t[hh * 64:(hh + 1) * 64, :], op=ALU.add),
                                         R=[t1.R(), t2.R()], W=[QTt.R(h)])
                            P += [pa, pa2, pb, pc, pd, pe_]
                        return P

                    def run_piece(p):
                        save("loop")
                        use("prep")
                        p()
                        save("prep")
                        use("loop")

                    use("loop")
                    for p in prep_pieces(0, QT2[0]):
                        run_piece(p)
                    for tt in range(ntile):
                        QT = QT2[tt % 2]
                        pend = prep_pieces(tt + 1, QT2[(tt + 1) % 2]) if tt + 1 < ntile else []
                        every = max(1, (16 * nkc) // (len(pend) + 1)) if pend else 0
                        for mo in range(2):
                            c.dma("sp", wob[mo].t[:], S["w_aout"][ai, mo].rearrange("p kc j -> p (kc j)"), R=[self.wr((("w_aout", ai), mo))], W=[wob[mo].R()])
                        c.dma("sp", xr2[0].t[:], self.XT[s][0:128, tt * NT:(tt + 1) * NT], W=[xr2[0].R()])
                        it = 0
                        for h in range(16):
                            m, half, kv = h // 2, h % 2, h // 4
                            hs = slice(half * 64, (half + 1) * 64)
                            po = self.pacc[h % 2]
                            pS = {}

                            def qk(kc):
                                p = c.psum()
                                pS[kc] = p
                                c.op("pe", lambda e: e.matmul(p.t[:], KT.t[:, kv // 2, kc * 128:(kc + 1) * 128], QT.t[:, h, :], start=True, stop=True),
                                     R=[KT.R(), QT.R(h)], W=[p.R()])

                            qk(0)
                            if nkc > 1:
                                qk(1)
                            for kc in range(nkc):
                                p = pS.pop(kc)
                                pt = PT[kc % 3]
                                c.op("act", lambda e: e.activation(out=pt.t[:], in_=p.t[:], func=AF.Exp), R=[p.R()], W=[pt.R()])
                                if kc + 2 < nkc:
                                    qk(kc + 2)
                                c.op("pe", lambda e: e.matmul(po.t[:], Vx.t[:, kc, kv, :], pt.t[:], start=(kc == 0), stop=(kc == nkc - 1)),
                                     R=[Vx.R(), pt.R()], W=[po.R()])
                                it += 1
                                if pend and it % every == 0:
                                    run_piece(pend.pop(0))
                            rc = rec[h % 2]
                            c.op("dve", lambda e: e.reciprocal(out=rc.t[0:64, :], in_=po.t[64:128, :]), R=[po.R()], W=[rc.R()])
                            c.op("dve", lambda e: e.tensor_tensor(out=OT.t[hs, m, :], in0=po.t[0:64, :], in1=rc.t[0:64, :], op=ALU.mult),
                                 R=[po.R(), rc.R()], W=[OT.R()])
                        while pend:
                            run_piece(pend.pop(0))
                        for mo in range(8):
                            wb = wob[mo % 2]
                            xr = xr2[mo % 2]
                            if mo + 1 < 8:
                                c.dma("sp", xr2[(mo + 1) % 2].t[:], self.XT[s][(mo + 1) * 128:(mo + 2) * 128, tt * NT:(tt + 1) * NT], W=[xr2[(mo + 1) % 2].R()])
                            ps = c.psum()
                            for kc in range(8):
                                c.op("pe", lambda e: e.matmul(ps.t[:], wb.t[:, kc * 128:(kc + 1) * 128], OT.t[:, kc, :], start=(kc == 0), stop=(kc == 7)),
                                     R=[wb.R(), OT.R()], W=[ps.R()], inc=(kc == 7))
                            if mo + 2 < 8:
                                c.dma("sp", wb.t[:], S["w_aout"][ai, mo + 2].rearrange("p kc j -> p (kc j)"), R=[self.wr((("w_aout", ai), mo + 2))], W=[wb.R()])
                            c.op("dve", lambda e: e.tensor_tensor(out=xr.t[:], in0=ps.t[:], in1=xr.t[:], op=ALU.add), R=[ps.R()], W=[xr.R()])
                            c.dma("pool", self.XT[s][mo * 128:(mo + 1) * 128, tt * NT:(tt + 1) * NT], xr.t[:], R=[xr.R()], W=[self.wr(("XTc", s, tt, mo))])
                    save("loop")
                    c.psum_banks = banks
                    c.psum_next = 0
                    c.barrier()


def host_inputs(T, NSEQ, layers, x_slots, inp):
    L = len(layers)
    f32 = np.float32
    m = {}
    m["x"] = np.ascontiguousarray(x_slots, dtype=f32)
    ng = np.asarray(inp["norm_g"], f32)[:L]
    m["norm_g"] = np.ascontiguousarray(ng.reshape(L, 3, 8, 128).transpose(3, 0, 1, 2).reshape(128, L * 24))
    for k in ("ffn_w_gate", "ffn_w_up", "ffn_w_down"):
        m[k] = np.ascontiguousarray(np.asarray(inp[k], f32)[:L])
    ns = max(sum(1 for l in layers if l[1] == "ssm"), 1)
    na = max(sum(1 for l in layers if l[1] == "attn"), 1)
    m["ssm_w_in"] = np.ascontiguousarray(np.asarray(inp["ssm_w_in"], f32)[:ns])
    cw = np.asarray(inp["ssm_conv_w"], f32)[:ns]
    m["ssm_conv_w"] = np.ascontiguousarray(cw.reshape(ns, 5, 32, 128).transpose(3, 0, 2, 1).reshape(128, ns * 32 * 5))
    cb = np.asarray(inp["ssm_conv_b"], f32)[:ns]
    m["ssm_conv_b"] = np.ascontiguousarray(cb.reshape(ns, 32, 128).transpose(2, 0, 1).reshape(128, ns * 32))
    dtb = np.asarray(inp["ssm_dt_bias"], f32)[:ns].reshape(ns, 64)
    m["ssm_dtb_col"] = np.ascontiguousarray(dtb.T)
    m["ssm_dtb_row"] = np.ascontiguousarray(dtb)
    m["ssm_alog_row"] = np.ascontiguousarray(np.asarray(inp["ssm_A_log"], f32)[:ns].reshape(ns, 64))
    Dv = np.asarray(inp["ssm_D"], f32)[:ns]
    m["ssm_D_col"] = np.ascontiguousarray(np.repeat(Dv, 64, axis=1).reshape(ns, 16, 128).transpose(2, 0, 1).reshape(128, ns * 16))
    ngm = np.asarray(inp["ssm_norm_g"], f32)[:ns]
    m["ssm_ng_col"] = np.ascontiguousarray(ngm.reshape(ns, 16, 128).transpose(2, 0, 1).reshape(128, ns * 16))
    m["ssm_w_out"] = np.ascontiguousarray(np.asarray(inp["ssm_w_out"], f32)[:ns])
    m["attn_w_qkv"] = np.ascontiguousarray(np.asarray(inp["attn_w_qkv"], f32)[:na])
    qg = np.asarray(inp["attn_q_norm"], f32)[:na]
    kg = np.asarray(inp["attn_k_norm"], f32)[:na]
    m["attn_qg_col"] = np.ascontiguousarray(np.concatenate([qg, qg], 1).T)
    m["attn_kg_col"] = np.ascontiguousarray(np.concatenate([kg, kg], 1).T)
    m["attn_w_out"] = np.ascontiguousarray(np.asarray(inp["attn_w_out"], f32)[:na])
    m["final_g"] = np.ascontiguousarray(np.asarray(inp["final_norm"], f32).reshape(8, 128).T)
    m["ident"] = np.eye(128, dtype=f32)
    cosT, sinT, Rm = rope_tables(T)
    m["cosT"], m["sinT"], m["Rm"] = cosT, sinT, Rm
    jj, ii = np.meshgrid(np.arange(128), np.arange(128), indexing="ij")
    m["tri"] = (jj <= ii).astype(f32)
    m["triT"] = (jj >= ii).astype(f32)
    sel = np.zeros((32, 32, 128), f32)
    for r in range(32):
        sel[r, r, :] = 1.0
    m["sel"] = sel.reshape(32, 32 * 128)
    m["sel3"] = np.ascontiguousarray(np.concatenate([sel, sel, sel], 0).reshape(96, 32 * 128))
    return m


FULL_LAYERS = [(True, "ssm", True), (True, "attn", True), (True, "ssm", True), (True, "attn", True)]
_CACHE = {}


def run(T, NSEQ, layers, slots_per_core, inp, final=True):
    key = (T, NSEQ, tuple(layers), final)
    if key not in _CACHE:
        _CACHE[key] = Builder(T, NSEQ, layers, final).build()
    nc = _CACHE[key]
    in_maps = [host_inputs(T, NSEQ, layers, xs, inp) for xs in slots_per_core]
    res = run_bass_kernel_spmd(nc, in_maps, core_ids=list(range(len(slots_per_core))))
    return [r["y"] for r in res.results]


def kernel(**inputs):
    xp = np.asarray(inputs["x_prompt"], np.float32)
    xs = np.asarray(inputs["x_sample"], np.float32)
    seqs = [xp[i] for i in range(xp.shape[0])] + [xs[i] for i in range(xs.shape[0])]
    T = xp.shape[1]
    slots = []
    for cidx in range(8):
        a = seqs[cidx]
        b = seqs[8 + cidx] if 8 + cidx < len(seqs) else seqs[cidx]
        slots.append(np.stack([a, b], 0))
    ys = run(T, 2, FULL_LAYERS, slots, inputs)
    out = [None] * len(seqs)
    for cidx in range(8):
        out[cidx] = ys[cidx][0]
        if 8 + cidx < len(seqs):
            out[8 + cidx] = ys[cidx][1]
    nP = xp.shape[0]
    y_prompt = np.stack(out[:nP], 0).astype(np.float32)
    y_sample = np.stack(out[nP:], 0).astype(np.float32)
    return (y_prompt, y_sample)
```

```python
import numpy as np
from contextlib import ExitStack
import concourse.bass as bass
import concourse.mybir as mybir
from concourse.bass_utils import run_bass_kernel_spmd

F32 = mybir.dt.float32
BF16 = mybir.dt.bfloat16
ALU = mybir.AluOpType
AF = mybir.ActivationFunctionType

D_MODEL = 1024
D_FF = 2816
D_INNER = 2048
SSM_HEADS = 32
SSM_GROUPS = 8
D_STATE = 128
CONV_DIM = 4096
SSM_IN_DIM = 6208
N_HEADS = 16
N_KV = 4
HD = 64
EPS = 1e-6
GRID_W = 64
ROPE_THETA = 10000.0
NDMASEM = 12
SSM_SKIP = set()
PRO_SLOT = 0
PRO_RATE = 1
NCH = 3
PCS_DEDICATED = True


class Res:
    __slots__ = ("w", "r", "excl")

    def __init__(self, excl=False):
        self.w = None
        self.r = {}
        self.excl = excl


class Tile:
    def __init__(self, t, excl=False):
        self.t = t
        self.res = {}
        self.excl = excl

    def R(self, key=None):
        if self.excl:
            key = None
        r = self.res.get(key)
        if r is None:
            r = self.res[key] = Res(self.excl)
        return r


class PsumHandle:
    def __init__(self, bank):
        self.bank = bank

    @property
    def t(self):
        assert self.bank.owner is self, "stale PSUM handle"
        return self.bank.t

    def R(self, key=None):
        assert self.bank.owner is self, "stale PSUM handle"
        return self.bank.R(key)


class EW:
    def __init__(self, name, eng, sem):
        self.name = name
        self.eng = eng
        self.sem = sem
        self.count = 0
        self.waited = {}
        self.pending = False


class Ctx:
    def __init__(self, nc, es):
        self.nc = nc
        self.es = es
        self.E = {}
        for name, eng in (("pe", nc.tensor), ("act", nc.scalar), ("dve", nc.vector),
                          ("pool", nc.gpsimd), ("sp", nc.sync)):
            sem = es.enter_context(nc.semaphore("s_" + name))
            self.E[name] = EW(name, eng, sem)
        self.dsem = {}
        for q in ("sp", "pool"):
            self.dsem[q] = [[es.enter_context(nc.semaphore(f"d_{q}{i}")), 0] for i in range(NDMASEM)]
        self.dnext = {"sp": 0, "pool": 0}
        self.semid = {}
        self.psum_banks = []
        self.psum_next = 0
        self.uid = 0

    def _wait(self, E, dep):
        if dep is None:
            return
        sem, val = dep
        if sem is E.sem and val > E.count:
            return
        k = id(sem)
        if E.waited.get(k, 0) >= val:
            return
        E.eng.wait_ge(sem, val)
        E.waited[k] = val

    def _hazards(self, E, R, W):
        for r in R:
            self._wait(E, r.w)
            if r.excl:
                for d in r.r.values():
                    if d[0] is not E.sem:
                        self._wait(E, d)
        for w in W:
            self._wait(E, w.w)
            for d in w.r.values():
                self._wait(E, d)

    def _record(self, dep, R, W):
        k = id(dep[0])
        for r in R:
            old = r.r.get(k)
            if old is None or old[1] < dep[1]:
                r.r[k] = dep
        for w in W:
            w.w = dep
            w.r = {}

    def op(self, en, fn, R=(), W=(), inc=True):
        E = self.E[en]
        self._hazards(E, R, W)
        ins = fn(E.eng)
        dep = (E.sem, E.count + 1)
        if inc:
            ins.then_inc(E.sem, 1)
            E.count += 1
            E.pending = False
        else:
            E.pending = True
        self._record(dep, R, W)
        return ins

    def dma(self, q, out, in_, R=(), W=()):
        E = self.E[q]
        self._hazards(E, R, W)
        i = self.dnext[q]
        self.dnext[q] = (i + 1) % NDMASEM
        ent = self.dsem[q][i]
        if ent[1] > 0:
            self._wait(E, (ent[0], ent[1]))
        ent[1] += 16
        E.eng.dma_start(out=out, in_=in_).then_inc(ent[0], 16)
        dep = (ent[0], ent[1])
        self._record(dep, R, W)

    def barrier(self):
        deps = []
        for E in self.E.values():
            assert not E.pending
            if E.count:
                deps.append((E.sem, E.count))
        for q in self.dsem:
            for sem, val in self.dsem[q]:
                if val:
                    deps.append((sem, val))
        for E in self.E.values():
            for d in deps:
                if d[0] is not E.sem:
                    self._wait(E, d)

    def sb(self, es, shape, dt, name=None):
        self.uid += 1
        t = es.enter_context(self.nc.sbuf_tensor(f"{name or 't'}_{self.uid}", list(shape), dt))
        return Tile(t)

    def psum(self):
        b = self.psum_banks[self.psum_next]
        self.psum_next = (self.psum_next + 1) % len(self.psum_banks)
        h = PsumHandle(b)
        b.owner = h
        return h


def rope_tables(T):
    rows = T // GRID_W
    r_idx, c_idx = np.meshgrid(np.arange(rows), np.arange(GRID_W), indexing="ij")
    r_idx = r_idx.reshape(-1).astype(np.float32)
    c_idx = c_idx.reshape(-1).astype(np.float32)
    inv_freq = (np.float32(ROPE_THETA) ** (-np.arange(0, 32, 2, dtype=np.float32) / np.float32(32))).astype(np.float32)
    ang = np.stack([r_idx[:, None] * inv_freq, c_idx[:, None] * inv_freq], axis=1).astype(np.float32)
    cos = np.cos(ang).astype(np.float32)
    sin = np.sin(ang).astype(np.float32)
    cosT = np.zeros((64, T), np.float32)
    sinT = np.zeros((64, T), np.float32)
    for a in range(2):
        for s in range(2):
            cosT[a * 32 + s * 16:a * 32 + s * 16 + 16, :] = cos[:, a, :].T
            sinT[a * 32 + s * 16:a * 32 + s * 16 + 16, :] = sin[:, a, :].T
    cosT = np.concatenate([cosT, cosT], 0)
    sinT = np.concatenate([sinT, sinT], 0)
    Rm = np.zeros((128, 128), np.float32)
    for hh in range(2):
        for a in range(2):
            for i in range(16):
                d0 = hh * 64 + a * 32 + i
                d1 = d0 + 16
                Rm[d1, d0] = -1.0
                Rm[d0, d1] = 1.0
    return np.ascontiguousarray(cosT), np.ascontiguousarray(sinT), Rm


class Builder:
    def __init__(self, T, NSEQ, layers, final=True):
        self.T = T
        self.NSEQ = NSEQ
        self.layers = layers
        self.final = final
        self.n_ffn = len(layers)
        self.n_ssm = sum(1 for l in layers if l[1] == "ssm")
        self.n_att = sum(1 for l in layers if l[1] == "attn")

    def build(self):
        T, NSEQ = self.T, self.NSEQ
        L = len(self.layers)
        nc = bass.Bass("TRN2", target_bir_lowering=False)
        self.nc = nc
        din = lambda n, s: nc.dram_tensor(n, list(s), F32, kind="ExternalInput").ap()
        dsc = lambda n, s, dt=F32: nc.dram_tensor(n, list(s), dt, kind="Internal").ap()
        I = self.I = {}
        I["x"] = din("x", (NSEQ, T, D_MODEL))
        I["norm_g"] = din("norm_g", (128, L * 3 * 8))
        I["ffn_w_gate"] = din("ffn_w_gate", (L, 2, D_MODEL, D_FF))
        I["ffn_w_up"] = din("ffn_w_up", (L, 2, D_MODEL, D_FF))
        I["ffn_w_down"] = din("ffn_w_down", (L, 2, D_FF, D_MODEL))
        ns, na = max(self.n_ssm, 1), max(self.n_att, 1)
        I["ssm_w_in"] = din("ssm_w_in", (ns, D_MODEL, SSM_IN_DIM))
        I["ssm_conv_w"] = din("ssm_conv_w", (128, ns * 32 * 5))
        I["ssm_conv_b"] = din("ssm_conv_b", (128, ns * 32))
        I["ssm_dtb_col"] = din("ssm_dtb_col", (64, ns))
        I["ssm_dtb_row"] = din("ssm_dtb_row", (ns, 64))
        I["ssm_alog_row"] = din("ssm_alog_row", (ns, 64))
        I["ssm_D_col"] = din("ssm_D_col", (128, ns * 16))
        I["ssm_ng_col"] = din("ssm_ng_col", (128, ns * 16))
        I["ssm_w_out"] = din("ssm_w_out", (ns, D_INNER, D_MODEL))
        I["attn_w_qkv"] = din("attn_w_qkv", (na, D_MODEL, 1536))
        I["attn_qg_col"] = din("attn_qg_col", (128, na))
        I["attn_kg_col"] = din("attn_kg_col", (128, na))
        I["attn_w_out"] = din("attn_w_out", (na, D_MODEL, D_MODEL))
        I["final_g"] = din("final_g", (128, 8))
        I["ident"] = din("ident", (128, 128))
        I["cosT"] = din("cosT", (128, T))
        I["sinT"] = din("sinT", (128, T))
        I["Rm"] = din("Rm", (128, 128))
        I["tri"] = din("tri", (128, 128))
        I["triT"] = din("triT", (128, 128))
        I["sel"] = din("sel", (32, 32 * 128))
        I["sel3"] = din("sel3", (96, 32 * 128))
        self.y = nc.dram_tensor("y", [NSEQ, T, D_MODEL], F32, kind="ExternalOutput").ap()
        self.XT = dsc("XT", (NSEQ, D_MODEL, T))
        S = self.S = {}
        S["wg"] = dsc("wg_b", (L, 2, 22, 128, 8, 128), BF16)
        S["wu"] = dsc("wu_b", (L, 2, 22, 128, 8, 128), BF16)
        S["wd"] = dsc("wd_b", (L, 2, 8, 128, 22, 128), BF16)
        S["w_in"] = dsc("w_in_b", (ns, 48, 128, 8, 128), BF16)
        S["w_dt"] = dsc("w_dt_b", (ns, 128, 8, 64), BF16)
        S["w_sout"] = dsc("w_sout_b", (ns, 8, 128, 16, 128), BF16)
        S["w_q"] = dsc("w_q_b", (na, 8, 128, 8, 128), BF16)
        S["w_k"] = dsc("w_k_b", (na, 2, 128, 8, 128), BF16)
        S["w_v"] = dsc("w_v_b", (na, 128, 8, 256), BF16)
        S["w_aout"] = dsc("w_aout_b", (na, 8, 128, 8, 128), BF16)
        if self.n_ssm:
            S["zs"] = dsc("zs", (NSEQ, D_INNER, T), BF16)
            S["xbc"] = dsc("xbc", (NSEQ, CONV_DIM, T), F32)
            S["xsT"] = dsc("xsT", (NSEQ, D_INNER, T), BF16)
            S["xs_tm"] = dsc("xs_tm", (NSEQ, T, D_INNER), BF16)
            S["B_tm"] = dsc("B_tm", (NSEQ, T, 1024), BF16)
            S["BT"] = dsc("BT", (NSEQ, 1024, T), BF16)
            S["CT"] = dsc("CT", (NSEQ, 1024, T), BF16)
            S["yf"] = dsc("yf", (NSEQ, D_INNER, T), F32)

        with ExitStack() as es:
            c = self.c = Ctx(nc, es)
            psbig = es.enter_context(nc.psum_tensor("psbig", [128, 5 * 512], F32))
            self.psbig = psbig
            for i in range(5):
                c.psum_banks.append(Tile(psbig[:, i * 512:(i + 1) * 512], excl=True))
            self.pacc = [Tile(es.enter_context(nc.psum_tensor(f"pacc{i}", [128, 512], F32)), excl=True) for i in range(2)]
            self.psb = Tile(es.enter_context(nc.psum_tensor("psb", [128, 1024], BF16)))
            K = self.K = {}
            K["ident"] = c.sb(es, [128, 128], F32, "ident")
            K["identb"] = c.sb(es, [128, 128], BF16, "identb")
            K["onesb"] = c.sb(es, [128, 128], BF16, "onesb")
            K["ones"] = c.sb(es, [128, 128], F32, "ones")
            K["blk"] = c.sb(es, [128, 128], F32, "blk")
            K["norm_g"] = c.sb(es, [128, L * 3 * 8], F32, "normg")
            K["final_g"] = c.sb(es, [128, 8], F32, "finalg")
            c.dma("sp", K["ident"].t[:], I["ident"][:, :], W=[K["ident"].R()])
            c.dma("sp", K["norm_g"].t[:], I["norm_g"][:, :], W=[K["norm_g"].R()])
            c.dma("sp", K["final_g"].t[:], I["final_g"][:, :], W=[K["final_g"].R()])
            c.op("dve", lambda e: e.tensor_copy(out=K["identb"].t[:], in_=K["ident"].t[:]),
                 R=[K["ident"].R()], W=[K["identb"].R()])
            c.op("dve", lambda e: e.memset(K["onesb"].t[:], 1.0), W=[K["onesb"].R()])
            c.op("dve", lambda e: e.memset(K["ones"].t[:], 1.0), W=[K["ones"].R()])
            c.op("dve", lambda e: e.memset(K["blk"].t[:], 0.0), W=[K["blk"].R()])
            c.op("dve", lambda e: e.memset(K["blk"].t[0:64, 0:64], 1.0), W=[K["blk"].R()])
            c.op("dve", lambda e: e.memset(K["blk"].t[64:128, 64:128], 1.0), W=[K["blk"].R()])
            K["blkb"] = c.sb(es, [128, 128], BF16, "blkb")
            c.op("dve", lambda e: e.tensor_copy(out=K["blkb"].t[:], in_=K["blk"].t[:]), R=[K["blk"].R()], W=[K["blkb"].R()])

            self.wres = {}
            self.convert_weights()
            self.transpose_in()
            c.barrier()
            fi = si = ai = 0
            for li, (f1, mixer, f2) in enumerate(self.layers):
                if f1:
                    self.ffn(li, 0)
                if mixer == "ssm":
                    self.ssm(li, si)
                    si += 1
                elif mixer == "attn":
                    self.attn(li, ai)
                    ai += 1
                if f2:
                    self.ffn(li, 1)
            self.final_out()
            c.barrier()
        return nc

    def wr(self, key):
        r = self.wres.get(key)
        if r is None:
            r = self.wres[key] = Res()
        return r

    def convert_weights(self):
        c, I, S = self.c, self.I, self.S

        def blocked(dst, src, nm, key):
            for m in range(nm):
                c.dma("pool", dst[m], src[:, m * 128:(m + 1) * 128].rearrange("(kc p) j -> p kc j", p=128),
                      W=[self.wr((key, m))])

        si = ai = 0
        for li, (f1, mixer, f2) in enumerate(self.layers):
            for j, on in ((0, f1), (1, f2)):
                if not on:
                    continue
                blocked(S["wg"][li, j], I["ffn_w_gate"][li, j], 22, ("wg", li, j))
                blocked(S["wu"][li, j], I["ffn_w_up"][li, j], 22, ("wu", li, j))
                blocked(S["wd"][li, j], I["ffn_w_down"][li, j], 8, ("wd", li, j))
            if mixer == "ssm":
                blocked(S["w_in"][si], I["ssm_w_in"][si, :, 0:6144], 48, ("w_in", si))
                c.dma("pool", S["w_dt"][si], I["ssm_w_in"][si, :, 6144:6208].rearrange("(kc p) j -> p kc j", p=128),
                      W=[self.wr(("w_dt", si))])
                blocked(S["w_sout"][si], I["ssm_w_out"][si], 8, ("w_sout", si))
                si += 1
            elif mixer == "attn":
                blocked(S["w_q"][ai], I["attn_w_qkv"][ai, :, 0:1024], 8, ("w_q", ai))
                blocked(S["w_k"][ai], I["attn_w_qkv"][ai, :, 1024:1280], 2, ("w_k", ai))
                c.dma("pool", S["w_v"][ai], I["attn_w_qkv"][ai, :, 1280:1536].rearrange("(kc p) j -> p kc j", p=128),
                      W=[self.wr(("w_v", ai))])
                blocked(S["w_aout"][ai], I["attn_w_out"][ai], 8, ("w_aout", ai))
                ai += 1

    def xres(self, s, i):
        return self.wr(("XT", s, i))

    def xres_range(self, s, t0, n):
        return [self.xres(s, i) for i in range(t0 // 512, (t0 + n + 511) // 512)]

    def rmsnorm(self, es, xt, ht, gcol, NT, nkc=8, tmp=None):
        c, K = self.c, self.K
        sq, ln, rs = tmp
        for st in range(NT // 512):
            sl = slice(st * 512, (st + 1) * 512)
            ps = c.psum()
            for kc in range(nkc):
                c.op("act", lambda e: e.activation(out=sq[kc % 2].t[:], in_=xt.t[:, kc, sl], func=AF.Square),
                     R=[xt.R()], W=[sq[kc % 2].R()])
                c.op("pe", lambda e: e.matmul(ps.t[:], K["onesb"].t[:], sq[kc % 2].t[:], start=(kc == 0), stop=(kc == nkc - 1)),
                     R=[K["onesb"].R(), sq[kc % 2].R()], W=[ps.R()])
            c.op("act", lambda e: e.activation(out=ln.t[:], in_=ps.t[:], func=AF.Ln, scale=1.0 / (nkc * 128), bias=self.eps_col()),
                 R=[ps.R(), self.K["eps"].R()], W=[ln.R()])
            c.op("act", lambda e: e.activation(out=rs.t[:], in_=ln.t[:], func=AF.Exp, scale=-0.5),
                 R=[ln.R()], W=[rs.R()])
            for kc in range(nkc):
                c.op("dve", lambda e: e.scalar_tensor_tensor(out=ht.t[:, kc, sl], in0=xt.t[:, kc, sl], scalar=gcol(kc),
                                                             in1=rs.t[:], op0=ALU.mult, op1=ALU.mult),
                     R=[xt.R(), rs.R(), self.K["norm_g"].R(), self.K["final_g"].R()], W=[ht.R()])

    def eps_col(self):
        return self.K["eps"].t[:, 0:1]

    def norm_tmp(self, es):
        c = self.c
        sq = [c.sb(es, [128, 512], BF16, "sq") for _ in range(2)]
        ln = c.sb(es, [128, 512], F32, "ln")
        rs = c.sb(es, [128, 512], F32, "rs")
        return sq, ln, rs

    def transpose_in(self):
        c, I, K = self.c, self.I, self.K
        T = self.T
        with ExitStack() as es:
            K["eps"] = c.sb(self.c.es, [128, 1], F32, "eps")
            c.op("dve", lambda e: e.memset(K["eps"].t[:], EPS), W=[K["eps"].R()])
            xin = [c.sb(es, [128, 4, 1024], F32, "xin") for _ in range(2)]
            xo = [c.sb(es, [128, 8, 512], F32, "xo") for _ in range(2)]
            n = 0
            for s in range(self.NSEQ):
                for tt in range(T // 512):
                    b = n % 2
                    n += 1
                    c.dma("sp", xin[b].t[:], I["x"][s, tt * 512:(tt + 1) * 512, :].rearrange("(a p) f -> p a f", p=128),
                          W=[xin[b].R()])
                    for kc in range(8):
                        ps = c.psum()
                        for a in range(4):
                            c.op("pe", lambda e: e.transpose(ps.t[:, a * 128:(a + 1) * 128], xin[b].t[:, a, kc * 128:(kc + 1) * 128], K["ident"].t[:]),
                                 R=[xin[b].R(), K["ident"].R()], W=[ps.R()], inc=(a == 3))
                        eng = "act" if kc % 2 else "dve"
                        if eng == "act":
                            c.op("act", lambda e: e.copy(out=xo[b].t[:, kc, :], in_=ps.t[:]), R=[ps.R()], W=[xo[b].R()])
                        else:
                            c.op("dve", lambda e: e.tensor_copy(out=xo[b].t[:, kc, :], in_=ps.t[:]), R=[ps.R()], W=[xo[b].R()])
                    c.dma("pool", self.XT[s][:, tt * 512:(tt + 1) * 512].rearrange("(kc p) t -> p kc t", p=128), xo[b].t[:],
                          R=[xo[b].R()], W=[self.xres(s, tt)])

    def final_out(self):
        c, K = self.c, self.K
        T = self.T
        c.barrier()
        with ExitStack() as es:
            xt = [c.sb(es, [128, 8, 512], F32, "fx") for _ in range(2)]
            hn = [c.sb(es, [128, 8, 512], F32, "fh") for _ in range(2)]
            yo = [c.sb(es, [128, 4, 1024], F32, "fy") for _ in range(2)]
            tmp = self.norm_tmp(es)
            n = 0
            for s in range(self.NSEQ):
                for tt in range(T // 512):
                    b = n % 2
                    n += 1
                    c.dma("sp", xt[b].t[:], self.XT[s][:, tt * 512:(tt + 1) * 512].rearrange("(kc p) t -> p kc t", p=128),
                          R=[self.xres(s, tt)], W=[xt[b].R()])
                    if self.final:
                        self.rmsnorm(es, xt[b], hn[b], lambda kc: K["final_g"].t[:, kc:kc + 1], 512, tmp=tmp)
                        src = hn[b]
                    else:
                        src = xt[b]
                    for a in range(4):
                        for half in range(2):
                            ps = c.psum()
                            for q in range(4):
                                kc = half * 4 + q
                                c.op("pe", lambda e: e.transpose(ps.t[:, q * 128:(q + 1) * 128], src.t[:, kc, a * 128:(a + 1) * 128], K["ident"].t[:]),
                                     R=[src.R(), K["ident"].R()], W=[ps.R()], inc=(q == 3))
                            if half:
                                c.op("act", lambda e: e.copy(out=yo[b].t[:, a, half * 512:(half + 1) * 512], in_=ps.t[:]), R=[ps.R()], W=[yo[b].R()])
                            else:
                                c.op("dve", lambda e: e.tensor_copy(out=yo[b].t[:, a, half * 512:(half + 1) * 512], in_=ps.t[:]), R=[ps.R()], W=[yo[b].R()])
                    c.dma("pool", self.y[s, tt * 512:(tt + 1) * 512, :].rearrange("(a p) f -> p a f", p=128), yo[b].t[:],
                          R=[yo[b].R()], W=[self.wr(("y", s, tt))])

    def ffn(self, li, j):
        c, K, S = self.c, self.K, self.S
        T = self.T
        NT = 1024 if T >= 1024 else 512
        nst = NT // 512
        gbase = (li * 3 + (0 if j == 0 else 2)) * 8
        c.barrier()
        with ExitStack() as es:
            xt = [c.sb(es, [128, 8, NT], F32, "x") for _ in range(2)]
            ht = [c.sb(es, [128, 8, NT], BF16, "h") for _ in range(2)]
            act = c.sb(es, [128, 22, NT], BF16, "act")
            NB = 4
            wbuf = [c.sb(es, [128, 22 * 128], BF16, "w") for _ in range(NB)]
            sg = [c.sb(es, [128, 512], F32, "sg") for _ in range(2)]
            tmp = self.norm_tmp(es)
            tiles = [(s, i) for s in range(self.NSEQ) for i in range(T // NT)]
            stream = []
            for _ in tiles:
                for m in range(22):
                    stream.append(("gu", m))
                for m in range(8):
                    stream.append(("d", m))
            state = {"issued": 0}

            def issue():
                k = state["issued"]
                if k >= len(stream):
                    return
                kind, m = stream[k]
                wb = wbuf[k % NB]
                if kind == "gu":
                    c.dma("sp", wb.t[:, 0:1024], S["wg"][li, j, m].rearrange("p kc j -> p (kc j)"),
                          R=[self.wr((("wg", li, j), m))], W=[wb.R()])
                    c.dma("sp", wb.t[:, 1024:2048], S["wu"][li, j, m].rearrange("p kc j -> p (kc j)"),
                          R=[self.wr((("wu", li, j), m))], W=[wb.R()])
                else:
                    c.dma("sp", wb.t[:, :], S["wd"][li, j, m].rearrange("p kc j -> p (kc j)"),
                          R=[self.wr((("wd", li, j), m))], W=[wb.R()])
                state["issued"] = k + 1

            used = {"n": 0}

            def nextw():
                k = used["n"]
                used["n"] += 1
                while state["issued"] < min(k + NB, len(stream)):
                    issue()
                return wbuf[k % NB]

            def load_norm(idx):
                s, i = tiles[idx]
                b = idx % 2
                c.dma("sp", xt[b].t[:], self.XT[s][:, i * NT:(i + 1) * NT].rearrange("(kc p) t -> p kc t", p=128),
                      R=self.xres_range(s, i * NT, NT), W=[xt[b].R()])
                self.rmsnorm(es, xt[b], ht[b], lambda kc: K["norm_g"].t[:, gbase + kc:gbase + kc + 1], NT, tmp=tmp)

            load_norm(0)
            for idx, (s, i) in enumerate(tiles):
                b = idx % 2
                h = ht[b]
                for m in range(22):
                    wb = nextw()
                    for st in range(nst):
                        sl = slice(st * 512, (st + 1) * 512)
                        pg = c.psum()
                        pu = c.psum()
                        for kc in range(8):
                            c.op("pe", lambda e: e.matmul(pg.t[:], wb.t[:, kc * 128:(kc + 1) * 128], h.t[:, kc, sl], start=(kc == 0), stop=(kc == 7)),
                                 R=[wb.R(), h.R()], W=[pg.R()], inc=(kc == 7))
                        for kc in range(8):
                            c.op("pe", lambda e: e.matmul(pu.t[:], wb.t[:, 1024 + kc * 128:1024 + (kc + 1) * 128], h.t[:, kc, sl], start=(kc == 0), stop=(kc == 7)),
                                 R=[wb.R(), h.R()], W=[pu.R()], inc=(kc == 7))
                        sgt = sg[(m * nst + st) % 2]
                        c.op("act", lambda e: e.activation(out=sgt.t[:], in_=pg.t[:], func=AF.Silu), R=[pg.R()], W=[sgt.R()])
                        c.op("dve", lambda e: e.tensor_tensor(out=act.t[:, m, sl], in0=sgt.t[:], in1=pu.t[:], op=ALU.mult),
                             R=[sgt.R(), pu.R()], W=[act.R(("w", st))])
                if idx + 1 < len(tiles):
                    load_norm(idx + 1)
                for m in range(8):
                    wb = nextw()
                    for st in range(nst):
                        sl = slice(st * 512, (st + 1) * 512)
                        pd = c.psum()
                        for kc in range(22):
                            c.op("pe", lambda e: e.matmul(pd.t[:], wb.t[:, kc * 128:(kc + 1) * 128], act.t[:, kc, sl], start=(kc == 0), stop=(kc == 21)),
                                 R=[wb.R(), act.R(("w", st))], W=[pd.R()], inc=(kc == 21))
                        c.op("dve", lambda e: e.scalar_tensor_tensor(out=xt[b].t[:, m, sl], in0=pd.t[:], scalar=0.5, in1=xt[b].t[:, m, sl],
                                                                     op0=ALU.mult, op1=ALU.add),
                             R=[pd.R()], W=[xt[b].R()])
                c.dma("pool", self.XT[s][:, i * NT:(i + 1) * NT].rearrange("(kc p) t -> p kc t", p=128), xt[b].t[:],
                      R=[xt[b].R()], W=self.xres_range(s, i * NT, NT))

    def ssm(self, li, si):
        c, K, S, I = self.c, self.K, self.S, self.I
        T = self.T
        NT = 512
        ntile = T // NT
        nch = T // 128
        gbase = (li * 3 + 1) * 8
        gcol = lambda kc: K["norm_g"].t[:, gbase + kc:gbase + kc + 1]
        c.barrier()
        with ExitStack() as es:
            DT = c.sb(es, [128, nch, 64], F32, "DT")
            cw = c.sb(es, [128, 160], F32, "cw")
            cb = c.sb(es, [128, 32], F32, "cb")
            dtb = c.sb(es, [128, 64], F32, "dtb")
            Arow = c.sb(es, [128, 64], F32, "Arow")
            Dcol = c.sb(es, [128, 16], F32, "Dcol")
            ngc = c.sb(es, [128, 16], F32, "ngc")
            tri = [c.sb(es, [128, 128], F32, "tri") for _ in range(2)]
            one1 = c.sb(es, [128, 1], F32, "one1")
            c.dma("sp", cw.t[:], I["ssm_conv_w"][:, si * 160:(si + 1) * 160], W=[cw.R()])
            c.dma("sp", cb.t[:], I["ssm_conv_b"][:, si * 32:(si + 1) * 32], W=[cb.R()])
            c.dma("sp", dtb.t[:], I["ssm_dtb_row"][si:si + 1, :].to_broadcast([128, 64]), W=[dtb.R()])
            c.dma("sp", Arow.t[:], I["ssm_alog_row"][si:si + 1, :].to_broadcast([128, 64]), W=[Arow.R()])
            c.dma("sp", Dcol.t[:], I["ssm_D_col"][:, si * 16:(si + 1) * 16], W=[Dcol.R()])
            c.dma("sp", ngc.t[:], I["ssm_ng_col"][:, si * 16:(si + 1) * 16], W=[ngc.R()])
            c.dma("sp", tri[0].t[:], I["tri"][:, :], W=[tri[0].R()])
            c.dma("sp", tri[1].t[:], I["triT"][:, :], W=[tri[1].R()])
            c.op("dve", lambda e: e.memset(one1.t[:], 1.0), W=[one1.R()])
            c.op("act", lambda e: e.activation(out=Arow.t[:], in_=Arow.t[:], func=AF.Exp), R=[], W=[Arow.R()])
            c.op("dve", lambda e: e.tensor_scalar(out=Arow.t[:], in0=Arow.t[:], scalar1=-1.0, scalar2=None, op0=ALU.mult), W=[Arow.R()])
            for s in range(self.NSEQ):
                if 1 not in SSM_SKIP:
                    self.ssm_s1(es, s, si, gcol, DT, dtb, one1)
                c.barrier()
                if 2 not in SSM_SKIP:
                    self.ssm_s2(s, si, cw, cb)
                c.barrier()
                if 3 not in SSM_SKIP:
                    self.ssm_s3(s, li, si, DT, Arow, Dcol, ngc, tri)
                c.barrier()

    def ssm_s1(self, es0, s, si, gcol, DT, dtb, one1):
        c, K, S = self.c, self.K, self.S
        T = self.T
        NT = 512
        ntile = T // NT
        with ExitStack() as es:
            Win = c.sb(es, [128, 48, 1024], BF16, "Win")
            xt = c.sb(es, [128, 8, NT], F32, "sx")
            hv = [c.sb(es, [128, 8, NT], BF16, "shv") for _ in range(2)]
            ntmp = self.norm_tmp(es)
            wdt = c.sb(es, [128, 8, 64], BF16, "wdt")
            zo = [c.sb(es, [128, 4, 512], BF16, "zo") for _ in range(2)]
            xo = [c.sb(es, [128, 4, 512], F32, "xo") for _ in range(2)]
            t64 = [c.sb(es, [128, 64], F32, "t64") for _ in range(2)]
            c.dma("sp", wdt.t[:], S["w_dt"][si], R=[self.wr(("w_dt", si))], W=[wdt.R()])

            def load_norm(tt):
                c.dma("sp", xt.t[:], self.XT[s][:, tt * NT:(tt + 1) * NT].rearrange("(kc p) t -> p kc t", p=128),
                      R=self.xres_range(s, tt * NT, NT), W=[xt.R()])
                self.rmsnorm(es, xt, hv[tt % 2], gcol, NT, tmp=ntmp)

            load_norm(0)
            for q6 in range(6):
                c.dma("sp", Win.t[:, q6 * 8:(q6 + 1) * 8, :], S["w_in"][si, q6 * 8:(q6 + 1) * 8].rearrange("m p kc j -> p m (kc j)"),
                      R=[self.wr((("w_in", si), mm)) for mm in range(q6 * 8, (q6 + 1) * 8)], W=[Win.R(q6)])
            n = 0
            for tt in range(ntile):
                h = hv[tt % 2]
                if tt + 1 < ntile:
                    load_norm(tt + 1)
                for a in range(4):
                    ch = tt * 4 + a
                    ps = c.psum()
                    for kc in range(8):
                        c.op("pe", lambda e: e.matmul(ps.t[:, 0:64], h.t[:, kc, a * 128:(a + 1) * 128], wdt.t[:, kc, :], start=(kc == 0), stop=(kc == 7)),
                             R=[h.R(), wdt.R()], W=[ps.R()], inc=(kc == 7))
                    t = t64[ch % 2]
                    c.op("dve", lambda e: e.tensor_tensor(out=t.t[:], in0=ps.t[:, 0:64], in1=dtb.t[:], op=ALU.add), R=[ps.R(), dtb.R()], W=[t.R()])
                    c.op("act", lambda e: e.activation(out=t.t[:], in_=t.t[:], func=AF.Exp), W=[t.R()])
                    c.op("act", lambda e: e.activation(out=DT.t[:, ch, :], in_=t.t[:], func=AF.Ln, bias=one1.t[:, 0:1]), R=[t.R(), one1.R()], W=[DT.R()])
                for m in range(48):
                    ps = c.psum()
                    for kc in range(8):
                        c.op("pe", lambda e: e.matmul(ps.t[:], Win.t[:, m, kc * 128:(kc + 1) * 128], h.t[:, kc, :], start=(kc == 0), stop=(kc == 7)),
                             R=[Win.R(m // 8), h.R()], W=[ps.R()], inc=(kc == 7))
                    q, grp = m % 4, m // 4
                    if m < 16:
                        o = zo[grp % 2]
                        c.op("act", lambda e: e.activation(out=o.t[:, q, :], in_=ps.t[:], func=AF.Silu), R=[ps.R()], W=[o.R()])
                        if q == 3:
                            c.dma("pool", S["zs"][s, grp * 512:(grp + 1) * 512, tt * NT:(tt + 1) * NT].rearrange("(q p) t -> p q t", p=128), o.t[:],
                                  R=[o.R()], W=[self.wr(("zs", s, tt))])
                    else:
                        o = xo[grp % 2]
                        c.op("dve", lambda e: e.tensor_copy(out=o.t[:, q, :], in_=ps.t[:]), R=[ps.R()], W=[o.R()])
                        if q == 3:
                            c.dma("pool", S["xbc"][s, (grp - 4) * 512:(grp - 3) * 512, tt * NT:(tt + 1) * NT].rearrange("(q p) t -> p q t", p=128), o.t[:],
                                  R=[o.R()], W=[self.wr(("xbc", s))])

    def ssm_s2(self, s, si, cw, cb):
        c, K, S = self.c, self.K, self.S
        T = self.T
        NT = 512
        ntile = T // NT
        NBUF = 4
        with ExitStack() as es:
            cin = [c.sb(es, [128, NT + 4], F32, "cin") for _ in range(NBUF)]
            acc = [c.sb(es, [128, NT], F32, "cacc") for _ in range(NBUF)]
            ptm = [c.sb(es, [128, NT], F32, "cptm") for _ in range(NBUF)]
            fm = c.sb(es, [128, 32, NT], BF16, "fm")
            xtm = c.sb(es, [128, 4, 2048], BF16, "cxtm")
            btm = c.sb(es, [128, 4, 1024], BF16, "cbtm")
            its = [(tt, cc) for tt in range(ntile) for cc in range(32)]

            def load(n):
                tt, cc = its[n]
                t0 = tt * NT
                ci = cin[n % NBUF]
                lo = 2 if tt == 0 else 0
                hi = NT + 2 if tt == ntile - 1 else NT + 4
                if tt == 0:
                    c.op("pool", lambda e: e.memset(ci.t[:, 0:2], 0.0), W=[ci.R()])
                if tt == ntile - 1:
                    c.op("pool", lambda e: e.memset(ci.t[:, NT + 2:NT + 4], 0.0), W=[ci.R()])
                c.dma("sp", ci.t[:, lo:hi], S["xbc"][s, cc * 128:(cc + 1) * 128, t0 - 2 + lo:t0 - 2 + hi], R=[self.wr(("xbc", s))], W=[ci.R()])

            def ident(n):
                tt_, cc_ = its[n]
                ci_ = cin[n % NBUF]
                ac_ = acc[n % NBUF]
                c.op("act", lambda e: e.activation(out=ac_.t[:], in_=ci_.t[:, 2:NT + 2], func=AF.Identity, scale=cw.t[:, cc_ * 5 + 2:cc_ * 5 + 3], bias=cb.t[:, cc_:cc_ + 1]),
                     R=[ci_.R(), cw.R(), cb.R()], W=[ac_.R()])

            for n in range(min(NBUF - 1, len(its))):
                load(n)
            ident(0)
            for n, (tt, cc) in enumerate(its):
                t0 = tt * NT
                if n + NBUF - 1 < len(its):
                    load(n + NBUF - 1)
                if n + 1 < len(its):
                    ident(n + 1)
                ci = cin[n % NBUF]
                ac = acc[n % NBUF]
                pt = ptm[n % NBUF]
                wcol = lambda j: cw.t[:, cc * 5 + j:cc * 5 + j + 1]
                for jn, j in enumerate((0, 1, 3, 4)):
                    src = ac if jn == 0 else pt
                    c.op("dve", lambda e: e.scalar_tensor_tensor(out=pt.t[:], in0=ci.t[:, j:j + NT], scalar=wcol(j), in1=src.t[:], op0=ALU.mult, op1=ALU.add),
                         R=[ci.R(), cw.R(), src.R()], W=[pt.R()])
                c.op("act", lambda e: e.activation(out=fm.t[:, cc, :], in_=pt.t[:], func=AF.Silu), R=[pt.R()], W=[fm.R(cc)])
                if cc < 24:
                    half = cc % 2
                    pb = self.psb
                    for a in range(4):
                        c.op("pe", lambda e: e.transpose(pb.t[:, half * 512 + a * 128:half * 512 + (a + 1) * 128], fm.t[:, cc, a * 128:(a + 1) * 128], K["identb"].t[:]),
                             R=[fm.R(cc), K["identb"].R()], W=[pb.R(half)], inc=(a == 3))
                    if cc < 16:
                        dst, dr = xtm.t[:, :, cc * 128:(cc + 1) * 128], xtm.R()
                    else:
                        dst, dr = btm.t[:, :, (cc - 16) * 128:(cc - 15) * 128], btm.R()
                    c.op("act", lambda e: e.copy(out=dst, in_=pb.t[:, half * 512:(half + 1) * 512].rearrange("p (a f) -> p a f", a=4)),
                         R=[pb.R(half)], W=[dr])
                if cc == 31:
                    allfm = [fm.R(q) for q in range(32)]
                    c.dma("pool", S["xsT"][s][:, t0:t0 + NT].rearrange("(cc p) t -> p cc t", p=128), fm.t[:, 0:16, :], R=allfm[0:16], W=[self.wr(("xsT", s, tt))])
                    c.dma("pool", S["BT"][s][:, t0:t0 + NT].rearrange("(cc p) t -> p cc t", p=128), fm.t[:, 16:24, :], R=allfm[16:24], W=[self.wr(("BT", s, tt))])
                    c.dma("pool", S["CT"][s][:, t0:t0 + NT].rearrange("(cc p) t -> p cc t", p=128), fm.t[:, 24:32, :], R=allfm[24:32], W=[self.wr(("CT", s, tt))])
                    c.dma("pool", S["xs_tm"][s, t0:t0 + NT, :].rearrange("(a p) f -> p a f", p=128), xtm.t[:], R=[xtm.R()], W=[self.wr(("xs_tm", s, tt))])
                    c.dma("pool", S["B_tm"][s, t0:t0 + NT, :].rearrange("(a p) f -> p a f", p=128), btm.t[:], R=[btm.R()], W=[self.wr(("B_tm", s, tt))])

    def ssm_s3(self, s, li, si, DT, Arow, Dcol, ngc, tri):
        c = self.c
        base_banks = c.psum_banks
        c.psum_banks = base_banks + (self.pacc[0:1] if PCS_DEDICATED else self.pacc)
        c.psum_next = 0
        try:
            self._ssm_s3(s, li, si, DT, Arow, Dcol, ngc, tri)
        finally:
            c.psum_banks = base_banks
            c.psum_next = 0

    def _ssm_s3(self, s, li, si, DT, Arow, Dcol, ngc, tri):
        c, K, S = self.c, self.K, self.S
        T = self.T
        NT = 512
        ntile = T // NT
        with ExitStack() as es:
            St = c.sb(es, [128, 8, 256], F32, "St")
            Sb = c.sb(es, [128, 8, 256], BF16, "Sb")
            xtm = c.sb(es, [128, 4, 2048], BF16, "xtm")
            btm = c.sb(es, [128, 4, 1024], BF16, "btm")
            BTt = c.sb(es, [128, 8, NT], BF16, "BTt")
            CTt = c.sb(es, [128, 8, NT], BF16, "CTt")
            yacc = c.sb(es, [128, 16, NT], F32, "yacc")
            CH = []
            for _pb in range(NCH):
                CH.append((c.sb(es, [128, 32], F32, "atm"), c.sb(es, [128, 32], F32, "ncs"), c.sb(es, [128, 32], F32, "d1"),
                           c.sb(es, [128, 32], F32, "wend"), c.sb(es, [128, 32], F32, "dec"), c.sb(es, [96, 128], BF16, "cs3"),
                           c.sb(es, [128, 32], F32, "nb"), None))
            r1 = c.sb(es, [32, 128], F32, "r1")
            r2 = c.sb(es, [32, 128], F32, "r2")
            midt = c.sb(es, [32, 128], BF16, "midt")
            lot = c.sb(es, [32, 128], BF16, "lot")
            lndt = c.sb(es, [128, 32], F32, "lndt")
            XW = [c.sb(es, [128, 256], BF16, "xwg") for _ in range(2)]
            Sel3 = c.sb(es, [96, 32 * 128], BF16, "sel3")
            c.dma("pool", Sel3.t[:], self.I["sel3"][:, :], W=[Sel3.R()])
            cbm = [c.sb(es, [128, 128], F32, "cbm") for _ in range(2)]
            ECS = [c.sb(es, [128, 512], F32, "ECS") for _ in range(2)]
            Lt = [c.sb(es, [128, 512], F32, "Lt") for _ in range(2)]
            Mt = [c.sb(es, [128, 512], BF16, "Mt") for _ in range(2)]
            Cs = [c.sb(es, [128, 512], BF16, "Cs") for _ in range(2)]
            stmp = [c.sb(es, [128, 256], F32, "stmp") for _ in range(2)]
            zt = c.sb(es, [128, 16, NT], BF16, "zt")
            xf = c.sb(es, [128, 16, NT], BF16, "xf")
            xr = c.sb(es, [128, 8, NT], F32, "xr")
            wso = [c.sb(es, [128, 2048], BF16, "wso") for _ in range(2)]
            ntmp = self.norm_tmp(es)
            rsb = c.sb(es, [128, 512], F32, "rsb")

            def prologue_pieces(d, tt, a, pb):
                ch = tt * 4 + a
                dt = DT.t[:, ch, d * 32:(d + 1) * 32]
                trm = tri[d]
                atm, ncs, d1, wend, dec, cs3, nb, _ = CH[pb]
                hold = {}

                def p0():
                    c.op("dve", lambda e: e.tensor_tensor(out=atm.t[:], in0=dt, in1=Arow.t[:, d * 32:(d + 1) * 32], op=ALU.mult),
                         R=[DT.R(), Arow.R()], W=[atm.R()])
                    pcs = hold["pcs"] = self.pacc[1] if PCS_DEDICATED else c.psum()
                    c.op("pe", lambda e: e.matmul(pcs.t[:, 0:32], trm.t[:], atm.t[:], start=True, stop=True), R=[trm.R(), atm.R()], W=[pcs.R()], inc=False)
                    c.op("pe", lambda e: e.matmul(pcs.t[:, 32:64], K["ones"].t[:], atm.t[:], start=True, stop=True), R=[K["ones"].R(), atm.R()], W=[pcs.R()], inc=False)
                    c.op("pe", lambda e: e.matmul(pcs.t[0:32, 64:192], atm.t[:], trm.t[:], start=True, stop=True), R=[trm.R(), atm.R()], W=[pcs.R()])

                def p1():
                    pcs = hold["pcs"]
                    c.op("dve", lambda e: e.tensor_scalar(out=ncs.t[:], in0=pcs.t[:, 0:32], scalar1=-1.0, scalar2=None, op0=ALU.mult), R=[pcs.R()], W=[ncs.R()])
                    c.op("dve", lambda e: e.tensor_tensor(out=d1.t[:], in0=pcs.t[:, 32:64], in1=ncs.t[:], op=ALU.add), R=[pcs.R(), ncs.R()], W=[d1.R()])
                    c.op("act", lambda e: e.activation(out=lndt.t[:], in_=dt, func=AF.Ln), R=[DT.R()], W=[lndt.R()])

                def p2():
                    pcs = hold["pcs"]
                    c.op("act", lambda e: e.activation(out=d1.t[:], in_=d1.t[:], func=AF.Exp), W=[d1.R()])
                    c.op("act", lambda e: e.activation(out=dec.t[:], in_=pcs.t[:, 32:64], func=AF.Exp), R=[pcs.R()], W=[dec.R()])
                    c.op("act", lambda e: e.copy(out=cs3.t[0:32, :], in_=pcs.t[0:32, 64:192]), R=[pcs.R()], W=[cs3.R()])
                    c.op("dve", lambda e: e.tensor_tensor(out=nb.t[:], in0=lndt.t[:], in1=ncs.t[:], op=ALU.add), R=[lndt.R(), ncs.R()], W=[nb.R()])

                def p3():
                    pcs = hold["pcs"]
                    c.op("dve", lambda e: e.tensor_tensor(out=wend.t[:], in0=d1.t[:], in1=dt, op=ALU.mult), R=[d1.R(), DT.R()], W=[wend.R()])
                    c.op("dve", lambda e: e.tensor_tensor(out=r1.t[:], in0=pcs.t[0:32, 64:192], in1=cs3.t[0:32, :], op=ALU.subtract), R=[pcs.R(), cs3.R()], W=[r1.R()])

                def p4():
                    c.op("act", lambda e: e.copy(out=midt.t[:], in_=r1.t[:]), R=[r1.R()], W=[midt.R()])

                def p5():
                    c.op("dve", lambda e: e.tensor_tensor(out=r2.t[:], in0=r1.t[:], in1=midt.t[:], op=ALU.subtract), R=[r1.R(), midt.R()], W=[r2.R()])
                    c.op("dve", lambda e: e.tensor_copy(out=cs3.t[32:64, :], in_=midt.t[:]), R=[midt.R()], W=[cs3.R()])

                def p6():
                    c.op("act", lambda e: e.copy(out=lot.t[:], in_=r2.t[:]), R=[r2.R()], W=[lot.R()])

                def p7():
                    c.op("dve", lambda e: e.tensor_copy(out=cs3.t[64:96, :], in_=lot.t[:]), R=[lot.R()], W=[cs3.R()])

                return [p0, p1, p2, p3, p4, p5, p6, p7]

            PS = {}

            def st1(d, tt, a, pb, g):
                asl = slice(a * 128, (a + 1) * 128)
                cs3 = CH[pb][5]
                pcb = c.psum()
                c.op("pe", lambda e: e.matmul(pcb.t[:, 0:128], BTt.t[:, g, asl], CTt.t[:, g, asl], start=True, stop=True), R=[BTt.R(a), CTt.R(a)], W=[pcb.R()])
                pcr = c.psum()
                for rr in range(4):
                    r = g * 4 + rr
                    c.op("pe", lambda e: e.matmul(pcr.t[:, rr * 128:(rr + 1) * 128], Sel3.t[:, r * 128:(r + 1) * 128], cs3.t[:], start=True, stop=True),
                         R=[Sel3.R(), cs3.R()], W=[pcr.R()], inc=(rr == 3))
                PS[(a, g)] = [pcb, pcr, None]

            def st2(d, tt, a, pb, g):
                trm = tri[d]
                ncs = CH[pb][6]
                k = g % 2
                pcb, pcr, _ = PS[(a, g)]
                c.op("act", lambda e: e.activation(out=ECS[k].t[:], in_=pcr.t[:], func=AF.Exp), R=[pcr.R()], W=[ECS[k].R()])
                for rr in range(4):
                    r = g * 4 + rr
                    c.op("act", lambda e: e.activation(out=Lt[k].t[:, rr * 128:(rr + 1) * 128], in_=pcr.t[:, rr * 128:(rr + 1) * 128], func=AF.Exp, bias=ncs.t[:, r:r + 1]),
                         R=[pcr.R(), ncs.R()], W=[Lt[k].R()])
                c.op("dve", lambda e: e.tensor_tensor(out=cbm[k].t[:], in0=pcb.t[:, 0:128], in1=trm.t[:], op=ALU.mult), R=[pcb.R(), trm.R()], W=[cbm[k].R()])

            def st3(d, tt, a, pb, g):
                asl = slice(a * 128, (a + 1) * 128)
                k = g % 2
                c.op("dve", lambda e: e.scalar_tensor_tensor(out=Mt[k].t[:].rearrange("p (r i) -> p r i", r=4), in0=Lt[k].t[:].rearrange("p (r i) -> p r i", r=4), scalar=1e30,
                                                             in1=cbm[k].t[:].unsqueeze(1).to_broadcast([128, 4, 128]), op0=ALU.min, op1=ALU.mult),
                     R=[Lt[k].R(), cbm[k].R()], W=[Mt[k].R()])
                c.op("pool", lambda e: e.tensor_tensor(out=Cs[k].t[:].rearrange("p (r i) -> p r i", r=4), in0=ECS[k].t[:].rearrange("p (r i) -> p r i", r=4),
                                                      in1=CTt.t[:, g, asl].unsqueeze(1).to_broadcast([128, 4, 128]), op=ALU.mult),
                     R=[ECS[k].R(), CTt.R(a)], W=[Cs[k].R()])
                wend = CH[pb][3]
                c.op("pool", lambda e: e.tensor_tensor(out=XW[k].t[:].rearrange("p (r d) -> p r d", r=4), in0=xtm.t[:, a, g * 256:(g + 1) * 256].rearrange("p (r d) -> p r d", r=4),
                                                      in1=wend.t[:, g * 4:(g + 1) * 4].unsqueeze(2).to_broadcast([128, 4, 64]), op=ALU.mult),
                     R=[xtm.R(a), wend.R()], W=[XW[k].R()])

            def st4(d, tt, a, pb, g):
                k = g % 2
                py = c.psum()
                PS[(a, g)][2] = py
                for rr in range(4):
                    r = g * 4 + rr
                    half = rr % 2
                    osl = py.t[half * 64:(half + 1) * 64, (rr // 2) * 128:(rr // 2 + 1) * 128]
                    c.op("pe", lambda e: e.matmul(osl, xtm.t[:, a, r * 64:(r + 1) * 64], Mt[k].t[:, rr * 128:(rr + 1) * 128], start=True, stop=False),
                         R=[xtm.R(a), Mt[k].R()], W=[py.R()], inc=False)
                    c.op("pe", lambda e: e.matmul(osl, Sb.t[:, g, rr * 64:(rr + 1) * 64], Cs[k].t[:, rr * 128:(rr + 1) * 128], start=False, stop=True),
                         R=[Sb.R(g), Cs[k].R()], W=[py.R()], inc=(rr == 3))
                pst = c.psum()
                PS[(a, g)].append(pst)
                c.op("pe", lambda e: e.matmul(pst.t[:, 0:256], btm.t[:, a, g * 128:(g + 1) * 128], XW[k].t[:], start=True, stop=True),
                     R=[btm.R(a), XW[k].R()], W=[pst.R()])

            def st5(d, tt, a, pb, g):
                asl = slice(a * 128, (a + 1) * 128)
                dec = CH[pb][4]
                k = g % 2
                _ps = PS.pop((a, g))
                py, pst = _ps[2], _ps[3]
                ydst = yacc.t[:, 2 * g:2 * g + 2, asl]
                ysrc = py.t[:, 0:256].rearrange("p (c i) -> p c i", c=2)
                if d == 0:
                    c.op("act", lambda e: e.copy(out=ydst, in_=ysrc), R=[py.R()], W=[yacc.R(g)])
                else:
                    c.op("dve", lambda e: e.tensor_tensor(out=ydst, in0=ysrc, in1=ydst, op=ALU.add), R=[py.R()], W=[yacc.R(g)])
                st = stmp[k]
                c.op("dve", lambda e: e.tensor_tensor(out=st.t[:].rearrange("p (r d) -> p r d", r=4), in0=St.t[:, g, :].rearrange("p (r d) -> p r d", r=4),
                                                     in1=dec.t[:, g * 4:(g + 1) * 4].unsqueeze(2).to_broadcast([128, 4, 64]), op=ALU.mult),
                     R=[St.R(g), dec.R()], W=[st.R()])
                c.op("dve", lambda e: e.tensor_tensor(out=St.t[:, g, :], in0=pst.t[:, 0:256], in1=st.t[:], op=ALU.add), R=[pst.R(), st.R()], W=[St.R(g)])
                if d == 0:
                    c.op("dve", lambda e: e.tensor_copy(out=Sb.t[:, g, :], in_=St.t[:, g, :]), R=[St.R(g)], W=[Sb.R(g)])
                else:
                    c.op("act", lambda e: e.copy(out=Sb.t[:, g, :], in_=St.t[:, g, :]), R=[St.R(g)], W=[Sb.R(g)])

            def run_tile(d, tt):
                aorder = list(range(4)) if d == 0 else [3, 2, 1, 0]
                items = [(ai_, a, g) for ai_, a in enumerate(aorder) for g in range(8)]
                n = len(items)
                base = self._chunk_ctr
                self._chunk_ctr += 4
                for p in prologue_pieces(d, tt, aorder[0], base % NCH):
                    p()
                stages = [st1, st2, st3, st4, st5]
                pend = []
                for t in range(n + 4):
                    if t < n:
                        ai_, a, g = items[t]
                        if g == PRO_SLOT and ai_ + 1 < 4:
                            pend = prologue_pieces(d, tt, aorder[ai_ + 1], (base + ai_ + 1) % NCH)
                        for _ in range(PRO_RATE):
                            if pend:
                                pend.pop(0)()
                    for si_, fn in enumerate(stages):
                        j = t - si_
                        if 0 <= j < n:
                            ai_, a, g = items[j]
                            fn(d, tt, a, (base + ai_) % NCH, g)
                assert not pend

            def load_tile(tt, aorder):
                t0 = tt * NT
                for a in aorder:
                    ta = t0 + a * 128
                    c.dma("sp", xtm.t[:, a, :], S["xs_tm"][s, ta:ta + 128, :], R=[self.wr(("xs_tm", s, tt))], W=[xtm.R(a)])
                    c.dma("sp", btm.t[:, a, :], S["B_tm"][s, ta:ta + 128, :], R=[self.wr(("B_tm", s, tt))], W=[btm.R(a)])
                    c.dma("sp", BTt.t[:, :, a * 128:(a + 1) * 128], S["BT"][s][:, ta:ta + 128].rearrange("(cc p) t -> p cc t", p=128), R=[self.wr(("BT", s, tt))], W=[BTt.R(a)])
                    c.dma("sp", CTt.t[:, :, a * 128:(a + 1) * 128], S["CT"][s][:, ta:ta + 128].rearrange("(cc p) t -> p cc t", p=128), R=[self.wr(("CT", s, tt))], W=[CTt.R(a)])

            self._chunk_ctr = 0
            for d in range(2):
                c.op("dve", lambda e: e.memset(St.t[:], 0.0), W=[St.R(g) for g in range(8)])
                c.op("dve", lambda e: e.memset(Sb.t[:], 0.0), W=[Sb.R(g) for g in range(8)])
                order = range(ntile) if d == 0 else range(ntile - 1, -1, -1)
                order = list(order)
                aord = list(range(4)) if d == 0 else [3, 2, 1, 0]
                load_tile(order[0], aord)
                for oi, tt in enumerate(order):
                    t0 = tt * NT
                    if d == 1:
                        c.dma("sp", yacc.t[:], S["yf"][s][:, t0:t0 + NT].rearrange("(cc p) t -> p cc t", p=128), R=[self.wr(("yf", s, tt))], W=[yacc.R(g) for g in range(8)])
                        c.dma("sp", xf.t[:], S["xsT"][s][:, t0:t0 + NT].rearrange("(cc p) t -> p cc t", p=128), R=[self.wr(("xsT", s, tt))], W=[xf.R(q_) for q_ in range(16)])
                        c.dma("sp", zt.t[:], S["zs"][s][:, t0:t0 + NT].rearrange("(cc p) t -> p cc t", p=128), R=[self.wr(("zs", s, tt))], W=[zt.R()])
                        c.dma("sp", xr.t[:], self.XT[s][:, t0:t0 + NT].rearrange("(kc p) t -> p kc t", p=128), R=self.xres_range(s, t0, NT), W=[xr.R()])
                        for mo in range(2):
                            c.dma("sp", wso[mo].t[:], S["w_sout"][si, mo].rearrange("p kc j -> p (kc j)"), R=[self.wr((("w_sout", si), mo))], W=[wso[mo].R()])
                    run_tile(d, tt)
                    if d == 0:
                        c.dma("pool", S["yf"][s][:, t0:t0 + NT].rearrange("(cc p) t -> p cc t", p=128), yacc.t[:], R=[yacc.R(g) for g in range(8)], W=[self.wr(("yf", s, tt))])
                        if oi + 1 < len(order):
                            load_tile(order[oi + 1], aord)
                        continue
                    if oi + 1 < len(order):
                        load_tile(order[oi + 1], aord)
                    sq, ln, rs = ntmp
                    rs2 = [rs, rsb]

                    def epiA(g):
                        ps = c.psum()
                        for q in range(2):
                            cc = 2 * g + q
                            c.op("dve", lambda e: e.scalar_tensor_tensor(out=yacc.t[:, cc, :], in0=xf.t[:, cc, :], scalar=Dcol.t[:, cc:cc + 1], in1=yacc.t[:, cc, :], op0=ALU.mult, op1=ALU.add),
                                 R=[xf.R(cc), Dcol.R()], W=[yacc.R(g)])
                            c.op("pool", lambda e: e.tensor_tensor(out=yacc.t[:, cc, :], in0=yacc.t[:, cc, :], in1=zt.t[:, cc, :], op=ALU.mult), R=[zt.R()], W=[yacc.R(g)])
                            c.op("act", lambda e: e.activation(out=sq[q].t[:], in_=yacc.t[:, cc, :], func=AF.Square), R=[yacc.R(g)], W=[sq[q].R()])
                            c.op("pe", lambda e: e.matmul(ps.t[:], K["onesb"].t[:], sq[q].t[:], start=(q == 0), stop=(q == 1)), R=[K["onesb"].R(), sq[q].R()], W=[ps.R()])
                        c.op("act", lambda e: e.activation(out=ln.t[:], in_=ps.t[:], func=AF.Ln, scale=1.0 / 256, bias=self.eps_col()), R=[ps.R(), K["eps"].R()], W=[ln.R()])
                        c.op("act", lambda e: e.activation(out=rs2[g % 2].t[:], in_=ln.t[:], func=AF.Exp, scale=-0.5), R=[ln.R()], W=[rs2[g % 2].R()])

                    def epiB(g):
                        for q in range(2):
                            cc = 2 * g + q
                            c.op("dve", lambda e: e.scalar_tensor_tensor(out=xf.t[:, cc, :], in0=yacc.t[:, cc, :], scalar=ngc.t[:, cc:cc + 1], in1=rs2[g % 2].t[:], op0=ALU.mult, op1=ALU.mult),
                                 R=[yacc.R(g), ngc.R(), rs2[g % 2].R()], W=[xf.R(cc)])

                    epiA(0)
                    for g in range(8):
                        if g + 1 < 8:
                            epiA(g + 1)
                        epiB(g)
                    for mo in range(8):
                        wb = wso[mo % 2]
                        ps = c.psum()
                        for kc in range(16):
                            c.op("pe", lambda e: e.matmul(ps.t[:], wb.t[:, kc * 128:(kc + 1) * 128], xf.t[:, kc, :], start=(kc == 0), stop=(kc == 15)),
                                 R=[wb.R(), xf.R(kc)], W=[ps.R()], inc=(kc == 15))
                        if mo + 2 < 8:
                            c.dma("sp", wb.t[:], S["w_sout"][si, mo + 2].rearrange("p kc j -> p (kc j)"), R=[self.wr((("w_sout", si), mo + 2))], W=[wb.R()])
                        c.op("dve", lambda e: e.tensor_tensor(out=xr.t[:, mo, :], in0=ps.t[:], in1=xr.t[:, mo, :], op=ALU.add), R=[ps.R()], W=[xr.R()])
                    c.dma("pool", self.XT[s][:, t0:t0 + NT].rearrange("(kc p) t -> p kc t", p=128), xr.t[:], R=[xr.R()], W=self.xres_range(s, t0, NT))

    def qk_post(self, ps, gcol, out_ap, bias_col, cs, sn, tm):
        c, K = self.c, self.K
        qraw, sqf, ln, rs, qn, t1, t2, sqb, qhi, qlo = tm
        c.op("act", lambda e: e.copy(out=qraw.t[:], in_=ps.t[:]), R=[ps.R()], W=[qraw.R()])
        c.op("act", lambda e: e.activation(out=sqb.t[:], in_=ps.t[:], func=AF.Square), R=[ps.R()], W=[sqb.R()])
        p2 = c.psum()
        c.op("pe", lambda e: e.matmul(p2.t[:], K["blkb"].t[:], sqb.t[:], start=True, stop=True), R=[K["blkb"].R(), sqb.R()], W=[p2.R()])
        c.op("act", lambda e: e.activation(out=ln.t[:], in_=p2.t[:], func=AF.Ln, scale=1.0 / 64, bias=self.eps_col()),
             R=[p2.R(), K["eps"].R()], W=[ln.R()])
        c.op("act", lambda e: e.activation(out=rs.t[:], in_=ln.t[:], func=AF.Exp, scale=-0.5, bias=bias_col),
             R=[ln.R(), K["lnq"].R()], W=[rs.R()])
        c.op("dve", lambda e: e.scalar_tensor_tensor(out=qn.t[:], in0=qraw.t[:], scalar=gcol, in1=rs.t[:], op0=ALU.mult, op1=ALU.mult),
             R=[qraw.R(), rs.R(), K["qkg"].R()], W=[qn.R()])
        c.op("dve", lambda e: e.tensor_copy(out=qhi.t[:], in_=qn.t[:]), R=[qn.R()], W=[qhi.R()])
        c.op("dve", lambda e: e.tensor_tensor(out=qlo.t[:], in0=qn.t[:], in1=qhi.t[:], op=ALU.subtract), R=[qn.R(), qhi.R()], W=[qlo.R()])
        p3 = c.psum()
        c.op("pe", lambda e: e.matmul(p3.t[:], K["Rmb"].t[:], qhi.t[:], start=True, stop=False), R=[K["Rmb"].R(), qhi.R()], W=[p3.R()], inc=False)
        c.op("pe", lambda e: e.matmul(p3.t[:], K["Rmb"].t[:], qlo.t[:], start=False, stop=True), R=[K["Rmb"].R(), qlo.R()], W=[p3.R()])
        c.op("pool", lambda e: e.tensor_tensor(out=t1.t[:], in0=qn.t[:], in1=cs.t[:], op=ALU.mult), R=[qn.R(), cs.R()], W=[t1.R()])
        c.op("dve", lambda e: e.tensor_tensor(out=t2.t[:], in0=p3.t[:], in1=sn.t[:], op=ALU.mult), R=[p3.R(), sn.R()], W=[t2.R()])
        if callable(out_ap):
            out_ap(t1, t2)
        else:
            c.op("pool", lambda e: e.tensor_tensor(out=out_ap[0], in0=t1.t[:], in1=t2.t[:], op=ALU.add), R=[t1.R(), t2.R()], W=[out_ap[1]])

    def attn(self, li, ai):
        c, K, S, I = self.c, self.K, self.S, self.I
        T = self.T
        NT = 512
        ntile = T // NT
        nkc = T // 128
        gbase = (li * 3 + 1) * 8
        c.barrier()
        with ExitStack() as es:
            KT = c.sb(es, [128, 2, T], BF16, "KT")
            Vx = c.sb(es, [128, nkc, 4, 128], BF16, "Vx")
            xt = c.sb(es, [128, 8, NT], F32, "ax")
            ht = c.sb(es, [128, 8, NT], BF16, "ah")
            cs = c.sb(es, [128, NT], F32, "cos")
            sn = c.sb(es, [128, NT], F32, "sin")
            tms = [[c.sb(es, [128, 512], F32, "qk") for _ in range(7)] + [c.sb(es, [128, 512], BF16, "qkb") for _ in range(3)] for _ in range(2)]
            tmi = [0]

            def next_tm():
                tmi[0] += 1
                return tms[tmi[0] % 2]
            ntmp = self.norm_tmp(es)
            K["Rm"] = c.sb(es, [128, 128], F32, "Rm")
            K["qkg"] = c.sb(es, [128, 2], F32, "qkg")
            qkall = c.sb(es, [128, 2 * max(self.n_att, 1)], F32, "qkall")
            K["lnq"] = c.sb(es, [128, 2], F32, "lnq")
            c.dma("sp", K["Rm"].t[:], I["Rm"][:, :], W=[K["Rm"].R()])
            K["Rmb"] = c.sb(es, [128, 128], BF16, "Rmb")
            c.op("dve", lambda e: e.tensor_copy(out=K["Rmb"].t[:], in_=K["Rm"].t[:]), R=[K["Rm"].R()], W=[K["Rmb"].R()])
            na_ = max(self.n_att, 1)
            c.dma("sp", qkall.t[:, 0:na_], I["attn_qg_col"][:, :], W=[qkall.R()])
            c.dma("sp", qkall.t[:, na_:2 * na_], I["attn_kg_col"][:, :], W=[qkall.R()])
            c.op("dve", lambda e: e.tensor_copy(out=K["qkg"].t[:, 0:1], in_=qkall.t[:, ai:ai + 1]), R=[qkall.R()], W=[K["qkg"].R()])
            c.op("dve", lambda e: e.tensor_copy(out=K["qkg"].t[:, 1:2], in_=qkall.t[:, na_ + ai:na_ + ai + 1]), R=[qkall.R()], W=[K["qkg"].R()])
            c.op("dve", lambda e: e.memset(K["lnq"].t[:, 0:1], float(np.log(0.125))), W=[K["lnq"].R()])
            c.op("dve", lambda e: e.memset(K["lnq"].t[:, 1:2], 0.0), W=[K["lnq"].R()])
            c.op("dve", lambda e: e.memset(Vx.t[:, :, :, 64:128], 1.0), W=[Vx.R()])
            gcol = lambda kc: K["norm_g"].t[:, gbase + kc:gbase + kc + 1]

            def load_tile(s, tt):
                c.dma("sp", xt.t[:], self.XT[s][:, tt * NT:(tt + 1) * NT].rearrange("(kc p) t -> p kc t", p=128),
                      R=self.xres_range(s, tt * NT, NT), W=[xt.R()])
                c.dma("sp", cs.t[:], I["cosT"][:, tt * NT:(tt + 1) * NT], W=[cs.R()])
                c.dma("sp", sn.t[:], I["sinT"][:, tt * NT:(tt + 1) * NT], W=[sn.R()])
                self.rmsnorm(es, xt, ht, gcol, NT, tmp=ntmp)

            for s in range(self.NSEQ):
                with ExitStack() as e1:
                    wk = c.sb(e1, [128, 2, 1024], BF16, "wk")
                    wv = c.sb(e1, [128, 8, 256], BF16, "wv")
                    c.dma("sp", wk.t[:], S["w_k"][ai].rearrange("m p kc j -> p m (kc j)"),
                          R=[self.wr((("w_k", ai), m)) for m in range(2)], W=[wk.R()])
                    c.dma("sp", wv.t[:], S["w_v"][ai], R=[self.wr(("w_v", ai))], W=[wv.R()])
                    for tt in range(ntile):
                        load_tile(s, tt)
                        for kv in range(2):
                            ps = c.psum()
                            for kc in range(8):
                                c.op("pe", lambda e: e.matmul(ps.t[:], wk.t[:, kv, kc * 128:(kc + 1) * 128], ht.t[:, kc, :], start=(kc == 0), stop=(kc == 7)),
                                     R=[wk.R(), ht.R()], W=[ps.R()], inc=(kc == 7))
                            self.qk_post(ps, K["qkg"].t[:, 1:2], (KT.t[:, kv, tt * NT:(tt + 1) * NT], KT.R()), K["lnq"].t[:, 1:2], cs, sn, next_tm())
                        for a in range(NT // 128):
                            ps = c.psum()
                            for kc in range(8):
                                c.op("pe", lambda e: e.matmul(ps.t[:, 0:256], ht.t[:, kc, a * 128:(a + 1) * 128], wv.t[:, kc, :], start=(kc == 0), stop=(kc == 7)),
                                     R=[wv.R(), ht.R()], W=[ps.R()], inc=(kc == 7))
                            c.op("dve", lambda e: e.tensor_copy(out=Vx.t[:, tt * (NT // 128) + a, :, 0:64], in_=ps.t[:, 0:256].rearrange("p (g d) -> p g d", g=4)),
                                 R=[ps.R()], W=[Vx.R()])
                    c.barrier()
                with ExitStack() as e2:
                    wq = c.sb(e2, [128, 8, 1024], BF16, "wq")
                    wob = [c.sb(e2, [128, 1024], BF16, "wob") for _ in range(2)]
                    QT2 = [c.sb(e2, [128, 16, NT], BF16, "QP") for _ in range(2)]
                    for QT_ in QT2:
                        c.op("pool", lambda e: e.memset(QT_.t[:], 0.0), W=[QT_.R(h) for h in range(16)])
                    OT = c.sb(e2, [128, 8, NT], BF16, "OT")
                    PT = [c.sb(e2, [128, 512], BF16, "PT") for _ in range(3)]
                    rec = [c.sb(e2, [128, 512], F32, "rec") for _ in range(2)]
                    xr2 = [c.sb(e2, [128, 512], F32, "xr2") for _ in range(2)]
                    c.dma("sp", wq.t[:], S["w_q"][ai].rearrange("m p kc j -> p m (kc j)"),
                          R=[self.wr((("w_q", ai), m)) for m in range(8)], W=[wq.R()])
                    banks = c.psum_banks
                    loop_banks = banks[0:3]
                    prep_banks = banks[3:5]
                    rot = {"loop": 0, "prep": 0}

                    def use(which):
                        c.psum_banks = loop_banks if which == "loop" else prep_banks
                        c.psum_next = rot[which]

                    def save(which):
                        rot[which] = c.psum_next

                    def prep_pieces(tt, QTt):
                        P = []

                        def p_load():
                            c.dma("sp", xt.t[:], self.XT[s][:, tt * NT:(tt + 1) * NT].rearrange("(kc p) t -> p kc t", p=128), W=[xt.R()])
                            c.dma("sp", cs.t[:], I["cosT"][:, tt * NT:(tt + 1) * NT], W=[cs.R()])
                            c.dma("sp", sn.t[:], I["sinT"][:, tt * NT:(tt + 1) * NT], W=[sn.R()])
                        P.append(p_load)
                        P.append(lambda: self.rmsnorm(es, xt, ht, gcol, NT, tmp=ntmp))
                        for m in range(8):
                            H = {}
                            tm = tms[m % 2]
                            qraw, sqf, ln, rs, qn, t1, t2, sqb, qhi, qlo = tm

                            def pa(m=m, H=H):
                                ps = H["ps"] = c.psum()
                                for kc in range(8):
                                    c.op("pe", lambda e: e.matmul(ps.t[:], wq.t[:, m, kc * 128:(kc + 1) * 128], ht.t[:, kc, :], start=(kc == 0), stop=(kc == 7)),
                                         R=[wq.R(), ht.R()], W=[ps.R()], inc=(kc == 7))

                            def pa2(H=H, qraw=qraw, sqf=sqb):
                                ps = H["ps"]
                                c.op("act", lambda e: e.copy(out=qraw.t[:], in_=ps.t[:]), R=[ps.R()], W=[qraw.R()])
                                c.op("act", lambda e: e.activation(out=sqf.t[:], in_=ps.t[:], func=AF.Square), R=[ps.R()], W=[sqf.R()])

                            def pb(H=H, sqf=sqb):
                                p2 = H["p2"] = c.psum()
                                c.op("pe", lambda e: e.matmul(p2.t[:], K["blkb"].t[:], sqf.t[:], start=True, stop=True), R=[K["blkb"].R(), sqf.R()], W=[p2.R()])

                            def pc(H=H, ln=ln, rs=rs, qn=qn, qraw=qraw, qhi=qhi, qlo=qlo):
                                p2 = H["p2"]
                                c.op("act", lambda e: e.activation(out=ln.t[:], in_=p2.t[:], func=AF.Ln, scale=1.0 / 64, bias=self.eps_col()),
                                     R=[p2.R(), K["eps"].R()], W=[ln.R()])
                                c.op("act", lambda e: e.activation(out=rs.t[:], in_=ln.t[:], func=AF.Exp, scale=-0.5, bias=K["lnq"].t[:, 0:1]),
                                     R=[ln.R(), K["lnq"].R()], W=[rs.R()])
                                c.op("dve", lambda e: e.scalar_tensor_tensor(out=qn.t[:], in0=qraw.t[:], scalar=K["qkg"].t[:, 0:1], in1=rs.t[:], op0=ALU.mult, op1=ALU.mult),
                                     R=[qraw.R(), rs.R(), K["qkg"].R()], W=[qn.R()])
                                c.op("dve", lambda e: e.tensor_copy(out=qhi.t[:], in_=qn.t[:]), R=[qn.R()], W=[qhi.R()])
                                c.op("dve", lambda e: e.tensor_tensor(out=qlo.t[:], in0=qn.t[:], in1=qhi.t[:], op=ALU.subtract), R=[qn.R(), qhi.R()], W=[qlo.R()])

                            def pd(H=H, qn=qn, t1=t1, qhi=qhi, qlo=qlo):
                                p3 = H["p3"] = c.psum()
                                c.op("pe", lambda e: e.matmul(p3.t[:], K["Rmb"].t[:], qhi.t[:], start=True, stop=False), R=[K["Rmb"].R(), qhi.R()], W=[p3.R()], inc=False)
                                c.op("pe", lambda e: e.matmul(p3.t[:], K["Rmb"].t[:], qlo.t[:], start=False, stop=True), R=[K["Rmb"].R(), qlo.R()], W=[p3.R()])
                                c.op("pool", lambda e: e.tensor_tensor(out=t1.t[:], in0=qn.t[:], in1=cs.t[:], op=ALU.mult), R=[qn.R(), cs.R()], W=[t1.R()])

                            def pe_(m=m, H=H, t1=t1, t2=t2):
                                p3 = H["p3"]
                                c.op("dve", lambda e: e.tensor_tensor(out=t2.t[:], in0=p3.t[:], in1=sn.t[:], op=ALU.mult), R=[p3.R(), sn.R()], W=[t2.R()])
                                for hh in range(2):
                                    h = 2 * m + hh
                                    kh = (h // 4) % 2
                                    c.op("pool", lambda e: e.tensor_tensor(out=QTt.t[kh * 64:(kh + 1) * 64, h, :], in0=t1.t[hh * 64:(hh + 1) * 64, :],
                                                                          in1=t2.t[hh * 64:(hh + 1) * 64, :], op=ALU.add),
                                         R=[t1.R(), t2.R()], W=[QTt.R(h)])
                            P += [pa, pa2, pb, pc, pd, pe_]
                        return P

                    def run_piece(p):
                        save("loop")
                        use("prep")
                        p()
                        save("prep")
                        use("loop")

                    use("loop")
                    for p in prep_pieces(0, QT2[0]):
                        run_piece(p)
                    for tt in range(ntile):
                        QT = QT2[tt % 2]
                        pend = prep_pieces(tt + 1, QT2[(tt + 1) % 2]) if tt + 1 < ntile else []
                        every = max(1, (16 * nkc) // (len(pend) + 1)) if pend else 0
                        for mo in range(2):
                            c.dma("sp", wob[mo].t[:], S["w_aout"][ai, mo].rearrange("p kc j -> p (kc j)"), R=[self.wr((("w_aout", ai), mo))], W=[wob[mo].R()])
                        c.dma("sp", xr2[0].t[:], self.XT[s][0:128, tt * NT:(tt + 1) * NT], W=[xr2[0].R()])
                        it = 0
                        for h in range(16):
                            m, half, kv = h // 2, h % 2, h // 4
                            hs = slice(half * 64, (half + 1) * 64)
                            po = self.pacc[h % 2]
                            pS = {}

                            def qk(kc):
                                p = c.psum()
                                pS[kc] = p
                                c.op("pe", lambda e: e.matmul(p.t[:], KT.t[:, kv // 2, kc * 128:(kc + 1) * 128], QT.t[:, h, :], start=True, stop=True),
                                     R=[KT.R(), QT.R(h)], W=[p.R()])

                            qk(0)
                            if nkc > 1:
                                qk(1)
                            for kc in range(nkc):
                                p = pS.pop(kc)
                                pt = PT[kc % 3]
                                c.op("act", lambda e: e.activation(out=pt.t[:], in_=p.t[:], func=AF.Exp), R=[p.R()], W=[pt.R()])
                                if kc + 2 < nkc:
                                    qk(kc + 2)
                                c.op("pe", lambda e: e.matmul(po.t[:], Vx.t[:, kc, kv, :], pt.t[:], start=(kc == 0), stop=(kc == nkc - 1)),
                                     R=[Vx.R(), pt.R()], W=[po.R()])
                                it += 1
                                if pend and it % every == 0:
                                    run_piece(pend.pop(0))
                            rc = rec[h % 2]
                            c.op("dve", lambda e: e.reciprocal(out=rc.t[0:64, :], in_=po.t[64:128, :]), R=[po.R()], W=[rc.R()])
                            c.op("dve", lambda e: e.tensor_tensor(out=OT.t[hs, m, :], in0=po.t[0:64, :], in1=rc.t[0:64, :], op=ALU.mult),
                                 R=[po.R(), rc.R()], W=[OT.R()])
                        while pend:
                            run_piece(pend.pop(0))
                        for mo in range(8):
                            wb = wob[mo % 2]
                            xr = xr2[mo % 2]
                            if mo + 1 < 8:
                                c.dma("sp", xr2[(mo + 1) % 2].t[:], self.XT[s][(mo + 1) * 128:(mo + 2) * 128, tt * NT:(tt + 1) * NT], W=[xr2[(mo + 1) % 2].R()])
                            ps = c.psum()
                            for kc in range(8):
                                c.op("pe", lambda e: e.matmul(ps.t[:], wb.t[:, kc * 128:(kc + 1) * 128], OT.t[:, kc, :], start=(kc == 0), stop=(kc == 7)),
                                     R=[wb.R(), OT.R()], W=[ps.R()], inc=(kc == 7))
                            if mo + 2 < 8:
                                c.dma("sp", wb.t[:], S["w_aout"][ai, mo + 2].rearrange("p kc j -> p (kc j)"), R=[self.wr((("w_aout", ai), mo + 2))], W=[wb.R()])
                            c.op("dve", lambda e: e.tensor_tensor(out=xr.t[:], in0=ps.t[:], in1=xr.t[:], op=ALU.add), R=[ps.R()], W=[xr.R()])
                            c.dma("pool", self.XT[s][mo * 128:(mo + 1) * 128, tt * NT:(tt + 1) * NT], xr.t[:], R=[xr.R()], W=[self.wr(("XTc", s, tt, mo))])
                    save("loop")
                    c.psum_banks = banks
                    c.psum_next = 0
                    c.barrier()


def host_inputs(T, NSEQ, layers, x_slots, inp):
    L = len(layers)
    f32 = np.float32
    m = {}
    m["x"] = np.ascontiguousarray(x_slots, dtype=f32)
    ng = np.asarray(inp["norm_g"], f32)[:L]
    m["norm_g"] = np.ascontiguousarray(ng.reshape(L, 3, 8, 128).transpose(3, 0, 1, 2).reshape(128, L * 24))
    for k in ("ffn_w_gate", "ffn_w_up", "ffn_w_down"):
        m[k] = np.ascontiguousarray(np.asarray(inp[k], f32)[:L])
    ns = max(sum(1 for l in layers if l[1] == "ssm"), 1)
    na = max(sum(1 for l in layers if l[1] == "attn"), 1)
    m["ssm_w_in"] = np.ascontiguousarray(np.asarray(inp["ssm_w_in"], f32)[:ns])
    cw = np.asarray(inp["ssm_conv_w"], f32)[:ns]
    m["ssm_conv_w"] = np.ascontiguousarray(cw.reshape(ns, 5, 32, 128).transpose(3, 0, 2, 1).reshape(128, ns * 32 * 5))
    cb = np.asarray(inp["ssm_conv_b"], f32)[:ns]
    m["ssm_conv_b"] = np.ascontiguousarray(cb.reshape(ns, 32, 128).transpose(2, 0, 1).reshape(128, ns * 32))
    dtb = np.asarray(inp["ssm_dt_bias"], f32)[:ns].reshape(ns, 64)
    m["ssm_dtb_col"] = np.ascontiguousarray(dtb.T)
    m["ssm_dtb_row"] = np.ascontiguousarray(dtb)
    m["ssm_alog_row"] = np.ascontiguousarray(np.asarray(inp["ssm_A_log"], f32)[:ns].reshape(ns, 64))
    Dv = np.asarray(inp["ssm_D"], f32)[:ns]
    m["ssm_D_col"] = np.ascontiguousarray(np.repeat(Dv, 64, axis=1).reshape(ns, 16, 128).transpose(2, 0, 1).reshape(128, ns * 16))
    ngm = np.asarray(inp["ssm_norm_g"], f32)[:ns]
    m["ssm_ng_col"] = np.ascontiguousarray(ngm.reshape(ns, 16, 128).transpose(2, 0, 1).reshape(128, ns * 16))
    m["ssm_w_out"] = np.ascontiguousarray(np.asarray(inp["ssm_w_out"], f32)[:ns])
    m["attn_w_qkv"] = np.ascontiguousarray(np.asarray(inp["attn_w_qkv"], f32)[:na])
    qg = np.asarray(inp["attn_q_norm"], f32)[:na]
    kg = np.asarray(inp["attn_k_norm"], f32)[:na]
    m["attn_qg_col"] = np.ascontiguousarray(np.concatenate([qg, qg], 1).T)
    m["attn_kg_col"] = np.ascontiguousarray(np.concatenate([kg, kg], 1).T)
    m["attn_w_out"] = np.ascontiguousarray(np.asarray(inp["attn_w_out"], f32)[:na])
    m["final_g"] = np.ascontiguousarray(np.asarray(inp["final_norm"], f32).reshape(8, 128).T)
    m["ident"] = np.eye(128, dtype=f32)
    cosT, sinT, Rm = rope_tables(T)
    m["cosT"], m["sinT"], m["Rm"] = cosT, sinT, Rm
    jj, ii = np.meshgrid(np.arange(128), np.arange(128), indexing="ij")
    m["tri"] = (jj <= ii).astype(f32)
    m["triT"] = (jj >= ii).astype(f32)
    sel = np.zeros((32, 32, 128), f32)
    for r in range(32):
        sel[r, r, :] = 1.0
    m["sel"] = sel.reshape(32, 32 * 128)
    m["sel3"] = np.ascontiguousarray(np.concatenate([sel, sel, sel], 0).reshape(96, 32 * 128))
    return m


FULL_LAYERS = [(True, "ssm", True), (True, "attn", True), (True, "ssm", True), (True, "attn", True)]
_CACHE = {}


def run(T, NSEQ, layers, slots_per_core, inp, final=True):
    key = (T, NSEQ, tuple(layers), final)
    if key not in _CACHE:
        _CACHE[key] = Builder(T, NSEQ, layers, final).build()
    nc = _CACHE[key]
    in_maps = [host_inputs(T, NSEQ, layers, xs, inp) for xs in slots_per_core]
    res = run_bass_kernel_spmd(nc, in_maps, core_ids=list(range(len(slots_per_core))))
    return [r["y"] for r in res.results]


def kernel(**inputs):
    xp = np.asarray(inputs["x_prompt"], np.float32)
    xs = np.asarray(inputs["x_sample"], np.float32)
    seqs = [xp[i] for i in range(xp.shape[0])] + [xs[i] for i in range(xs.shape[0])]
    T = xp.shape[1]
    slots = []
    for cidx in range(8):
        a = seqs[cidx]
        b = seqs[8 + cidx] if 8 + cidx < len(seqs) else seqs[cidx]
        slots.append(np.stack([a, b], 0))
    ys = run(T, 2, FULL_LAYERS, slots, inputs)
    out = [None] * len(seqs)
    for cidx in range(8):
        out[cidx] = ys[cidx][0]
        if 8 + cidx < len(seqs):
            out[8 + cidx] = ys[cidx][1]
    nP = xp.shape[0]
    y_prompt = np.stack(out[:nP], 0).astype(np.float32)
    y_sample = np.stack(out[nP:], 0).astype(np.float32)
    return (y_prompt, y_sample)
```

```python
import numpy as np
from contextlib import ExitStack
import concourse.bass as bass
import concourse.mybir as mybir
from concourse.bass_utils import run_bass_kernel_spmd

F32 = mybir.dt.float32
BF16 = mybir.dt.bfloat16
ALU = mybir.AluOpType
AF = mybir.ActivationFunctionType

D_MODEL = 1024
D_FF = 2816
D_INNER = 2048
SSM_HEADS = 32
SSM_GROUPS = 8
D_STATE = 128
CONV_DIM = 4096
SSM_IN_DIM = 6208
N_HEADS = 16
N_KV = 4
HD = 64
EPS = 1e-6
GRID_W = 64
ROPE_THETA = 10000.0
NDMASEM = 12
SSM_SKIP = set()
PRO_SLOT = 0
PRO_RATE = 1
NCH = 3
PCS_DEDICATED = True


class Res:
    __slots__ = ("w", "r", "excl")

    def __init__(self, excl=False):
        self.w = None
        self.r = {}
        self.excl = excl


class Tile:
    def __init__(self, t, excl=False):
        self.t = t
        self.res = {}
        self.excl = excl

    def R(self, key=None):
        if self.excl:
            key = None
        r = self.res.get(key)
        if r is None:
            r = self.res[key] = Res(self.excl)
        return r


class PsumHandle:
    def __init__(self, bank):
        self.bank = bank

    @property
    def t(self):
        assert self.bank.owner is self, "stale PSUM handle"
        return self.bank.t

    def R(self, key=None):
        assert self.bank.owner is self, "stale PSUM handle"
        return self.bank.R(key)


class EW:
    def __init__(self, name, eng, sem):
        self.name = name
        self.eng = eng
        self.sem = sem
        self.count = 0
        self.waited = {}
        self.pending = False


class Ctx:
    def __init__(self, nc, es):
        self.nc = nc
        self.es = es
        self.E = {}
        for name, eng in (("pe", nc.tensor), ("act", nc.scalar), ("dve", nc.vector),
                          ("pool", nc.gpsimd), ("sp", nc.sync)):
            sem = es.enter_context(nc.semaphore("s_" + name))
            self.E[name] = EW(name, eng, sem)
        self.dsem = {}
        for q in ("sp", "pool"):
            self.dsem[q] = [[es.enter_context(nc.semaphore(f"d_{q}{i}")), 0] for i in range(NDMASEM)]
        self.dnext = {"sp": 0, "pool": 0}
        self.semid = {}
        self.psum_banks = []
        self.psum_next = 0
        self.uid = 0

    def _wait(self, E, dep):
        if dep is None:
            return
        sem, val = dep
        if sem is E.sem and val > E.count:
            return
        k = id(sem)
        if E.waited.get(k, 0) >= val:
            return
        E.eng.wait_ge(sem, val)
        E.waited[k] = val

    def _hazards(self, E, R, W):
        for r in R:
            self._wait(E, r.w)
            if r.excl:
                for d in r.r.values():
                    if d[0] is not E.sem:
                        self._wait(E, d)
        for w in W:
            self._wait(E, w.w)
            for d in w.r.values():
                self._wait(E, d)

    def _record(self, dep, R, W):
        k = id(dep[0])
        for r in R:
            old = r.r.get(k)
            if old is None or old[1] < dep[1]:
                r.r[k] = dep
        for w in W:
            w.w = dep
            w.r = {}

    def op(self, en, fn, R=(), W=(), inc=True):
        E = self.E[en]
        self._hazards(E, R, W)
        ins = fn(E.eng)
        dep = (E.sem, E.count + 1)
        if inc:
            ins.then_inc(E.sem, 1)
            E.count += 1
            E.pending = False
        else:
            E.pending = True
        self._record(dep, R, W)
        return ins

    def dma(self, q, out, in_, R=(), W=()):
        E = self.E[q]
        self._hazards(E, R, W)
        i = self.dnext[q]
        self.dnext[q] = (i + 1) % NDMASEM
        ent = self.dsem[q][i]
        if ent[1] > 0:
            self._wait(E, (ent[0], ent[1]))
        ent[1] += 16
        E.eng.dma_start(out=out, in_=in_).then_inc(ent[0], 16)
        dep = (ent[0], ent[1])
        self._record(dep, R, W)

    def barrier(self):
        deps = []
        for E in self.E.values():
            assert not E.pending
            if E.count:
                deps.append((E.sem, E.count))
        for q in self.dsem:
            for sem, val in self.dsem[q]:
                if val:
                    deps.append((sem, val))
        for E in self.E.values():
            for d in deps:
                if d[0] is not E.sem:
                    self._wait(E, d)

    def sb(self, es, shape, dt, name=None):
        self.uid += 1
        t = es.enter_context(self.nc.sbuf_tensor(f"{name or 't'}_{self.uid}", list(shape), dt))
        return Tile(t)

    def psum(self):
        b = self.psum_banks[self.psum_next]
        self.psum_next = (self.psum_next + 1) % len(self.psum_banks)
        h = PsumHandle(b)
        b.owner = h
        return h


def rope_tables(T):
    rows = T // GRID_W
    r_idx, c_idx = np.meshgrid(np.arange(rows), np.arange(GRID_W), indexing="ij")
    r_idx = r_idx.reshape(-1).astype(np.float32)
    c_idx = c_idx.reshape(-1).astype(np.float32)
    inv_freq = (np.float32(ROPE_THETA) ** (-np.arange(0, 32, 2, dtype=np.float32) / np.float32(32))).astype(np.float32)
    ang = np.stack([r_idx[:, None] * inv_freq, c_idx[:, None] * inv_freq], axis=1).astype(np.float32)
    cos = np.cos(ang).astype(np.float32)
    sin = np.sin(ang).astype(np.float32)
    cosT = np.zeros((64, T), np.float32)
    sinT = np.zeros((64, T), np.float32)
    for a in range(2):
        for s in range(2):
            cosT[a * 32 + s * 16:a * 32 + s * 16 + 16, :] = cos[:, a, :].T
            sinT[a * 32 + s * 16:a * 32 + s * 16 + 16, :] = sin[:, a, :].T
    cosT = np.concatenate([cosT, cosT], 0)
    sinT = np.concatenate([sinT, sinT], 0)
    Rm = np.zeros((128, 128), np.float32)
    for hh in range(2):
        for a in range(2):
            for i in range(16):
                d0 = hh * 64 + a * 32 + i
                d1 = d0 + 16
                Rm[d1, d0] = -1.0
                Rm[d0, d1] = 1.0
    return np.ascontiguousarray(cosT), np.ascontiguousarray(sinT), Rm


class Builder:
    def __init__(self, T, NSEQ, layers, final=True):
        self.T = T
        self.NSEQ = NSEQ
        self.layers = layers
        self.final = final
        self.n_ffn = len(layers)
        self.n_ssm = sum(1 for l in layers if l[1] == "ssm")
        self.n_att = sum(1 for l in layers if l[1] == "attn")

    def build(self):
        T, NSEQ = self.T, self.NSEQ
        L = len(self.layers)
        nc = bass.Bass("TRN2", target_bir_lowering=False)
        self.nc = nc
        din = lambda n, s: nc.dram_tensor(n, list(s), F32, kind="ExternalInput").ap()
        dsc = lambda n, s, dt=F32: nc.dram_tensor(n, list(s), dt, kind="Internal").ap()
        I = self.I = {}
        I["x"] = din("x", (NSEQ, T, D_MODEL))
        I["norm_g"] = din("norm_g", (128, L * 3 * 8))
        I["ffn_w_gate"] = din("ffn_w_gate", (L, 2, D_MODEL, D_FF))
        I["ffn_w_up"] = din("ffn_w_up", (L, 2, D_MODEL, D_FF))
        I["ffn_w_down"] = din("ffn_w_down", (L, 2, D_FF, D_MODEL))
        ns, na = max(self.n_ssm, 1), max(self.n_att, 1)
        I["ssm_w_in"] = din("ssm_w_in", (ns, D_MODEL, SSM_IN_DIM))
        I["ssm_conv_w"] = din("ssm_conv_w", (128, ns * 32 * 5))
        I["ssm_conv_b"] = din("ssm_conv_b", (128, ns * 32))
        I["ssm_dtb_col"] = din("ssm_dtb_col", (64, ns))
        I["ssm_dtb_row"] = din("ssm_dtb_row", (ns, 64))
        I["ssm_alog_row"] = din("ssm_alog_row", (ns, 64))
        I["ssm_D_col"] = din("ssm_D_col", (128, ns * 16))
        I["ssm_ng_col"] = din("ssm_ng_col", (128, ns * 16))
        I["ssm_w_out"] = din("ssm_w_out", (ns, D_INNER, D_MODEL))
        I["attn_w_qkv"] = din("attn_w_qkv", (na, D_MODEL, 1536))
        I["attn_qg_col"] = din("attn_qg_col", (128, na))
        I["attn_kg_col"] = din("attn_kg_col", (128, na))
        I["attn_w_out"] = din("attn_w_out", (na, D_MODEL, D_MODEL))
        I["final_g"] = din("final_g", (128, 8))
        I["ident"] = din("ident", (128, 128))
        I["cosT"] = din("cosT", (128, T))
        I["sinT"] = din("sinT", (128, T))
        I["Rm"] = din("Rm", (128, 128))
        I["tri"] = din("tri", (128, 128))
        I["triT"] = din("triT", (128, 128))
        I["sel"] = din("sel", (32, 32 * 128))
        I["sel3"] = din("sel3", (96, 32 * 128))
        self.y = nc.dram_tensor("y", [NSEQ, T, D_MODEL], F32, kind="ExternalOutput").ap()
        self.XT = dsc("XT", (NSEQ, D_MODEL, T))
        S = self.S = {}
        S["wg"] = dsc("wg_b", (L, 2, 22, 128, 8, 128), BF16)
        S["wu"] = dsc("wu_b", (L, 2, 22, 128, 8, 128), BF16)
        S["wd"] = dsc("wd_b", (L, 2, 8, 128, 22, 128), BF16)
        S["w_in"] = dsc("w_in_b", (ns, 48, 128, 8, 128), BF16)
        S["w_dt"] = dsc("w_dt_b", (ns, 128, 8, 64), BF16)
        S["w_sout"] = dsc("w_sout_b", (ns, 8, 128, 16, 128), BF16)
        S["w_q"] = dsc("w_q_b", (na, 8, 128, 8, 128), BF16)
        S["w_k"] = dsc("w_k_b", (na, 2, 128, 8, 128), BF16)
        S["w_v"] = dsc("w_v_b", (na, 128, 8, 256), BF16)
        S["w_aout"] = dsc("w_aout_b", (na, 8, 128, 8, 128), BF16)
        if self.n_ssm:
            S["zs"] = dsc("zs", (NSEQ, D_INNER, T), BF16)
            S["xbc"] = dsc("xbc", (NSEQ, CONV_DIM, T), F32)
            S["xsT"] = dsc("xsT", (NSEQ, D_INNER, T), BF16)
            S["xs_tm"] = dsc("xs_tm", (NSEQ, T, D_INNER), BF16)
            S["B_tm"] = dsc("B_tm", (NSEQ, T, 1024), BF16)
            S["BT"] = dsc("BT", (NSEQ, 1024, T), BF16)
            S["CT"] = dsc("CT", (NSEQ, 1024, T), BF16)
            S["yf"] = dsc("yf", (NSEQ, D_INNER, T), F32)

        with ExitStack() as es:
            c = self.c = Ctx(nc, es)
            psbig = es.enter_context(nc.psum_tensor("psbig", [128, 5 * 512], F32))
            self.psbig = psbig
            for i in range(5):
                c.psum_banks.append(Tile(psbig[:, i * 512:(i + 1) * 512], excl=True))
            self.pacc = [Tile(es.enter_context(nc.psum_tensor(f"pacc{i}", [128, 512], F32)), excl=True) for i in range(2)]
            self.psb = Tile(es.enter_context(nc.psum_tensor("psb", [128, 1024], BF16)))
            K = self.K = {}
            K["ident"] = c.sb(es, [128, 128], F32, "ident")
            K["identb"] = c.sb(es, [128, 128], BF16, "identb")
            K["onesb"] = c.sb(es, [128, 128], BF16, "onesb")
            K["ones"] = c.sb(es, [128, 128], F32, "ones")
            K["blk"] = c.sb(es, [128, 128], F32, "blk")
            K["norm_g"] = c.sb(es, [128, L * 3 * 8], F32, "normg")
            K["final_g"] = c.sb(es, [128, 8], F32, "finalg")
            c.dma("sp", K["ident"].t[:], I["ident"][:, :], W=[K["ident"].R()])
            c.dma("sp", K["norm_g"].t[:], I["norm_g"][:, :], W=[K["norm_g"].R()])
            c.dma("sp", K["final_g"].t[:], I["final_g"][:, :], W=[K["final_g"].R()])
            c.op("dve", lambda e: e.tensor_copy(out=K["identb"].t[:], in_=K["ident"].t[:]),
                 R=[K["ident"].R()], W=[K["identb"].R()])
            c.op("dve", lambda e: e.memset(K["onesb"].t[:], 1.0), W=[K["onesb"].R()])
            c.op("dve", lambda e: e.memset(K["ones"].t[:], 1.0), W=[K["ones"].R()])
            c.op("dve", lambda e: e.memset(K["blk"].t[:], 0.0), W=[K["blk"].R()])
            c.op("dve", lambda e: e.memset(K["blk"].t[0:64, 0:64], 1.0), W=[K["blk"].R()])
            c.op("dve", lambda e: e.memset(K["blk"].t[64:128, 64:128], 1.0), W=[K["blk"].R()])

            self.wres = {}
            self.convert_weights()
            self.transpose_in()
            c.barrier()
            fi = si = ai = 0
            for li, (f1, mixer, f2) in enumerate(self.layers):
                if f1:
                    self.ffn(li, 0)
                if mixer == "ssm":
                    self.ssm(li, si)
                    si += 1
                elif mixer == "attn":
                    self.attn(li, ai)
                    ai += 1
                if f2:
                    self.ffn(li, 1)
            self.final_out()
            c.barrier()
        return nc

    def wr(self, key):
        r = self.wres.get(key)
        if r is None:
            r = self.wres[key] = Res()
        return r

    def convert_weights(self):
        c, I, S = self.c, self.I, self.S

        def blocked(dst, src, nm, key):
            for m in range(nm):
                c.dma("pool", dst[m], src[:, m * 128:(m + 1) * 128].rearrange("(kc p) j -> p kc j", p=128),
                      W=[self.wr((key, m))])

        si = ai = 0
        for li, (f1, mixer, f2) in enumerate(self.layers):
            for j, on in ((0, f1), (1, f2)):
                if not on:
                    continue
                blocked(S["wg"][li, j], I["ffn_w_gate"][li, j], 22, ("wg", li, j))
                blocked(S["wu"][li, j], I["ffn_w_up"][li, j], 22, ("wu", li, j))
                blocked(S["wd"][li, j], I["ffn_w_down"][li, j], 8, ("wd", li, j))
            if mixer == "ssm":
                blocked(S["w_in"][si], I["ssm_w_in"][si, :, 0:6144], 48, ("w_in", si))
                c.dma("pool", S["w_dt"][si], I["ssm_w_in"][si, :, 6144:6208].rearrange("(kc p) j -> p kc j", p=128),
                      W=[self.wr(("w_dt", si))])
                blocked(S["w_sout"][si], I["ssm_w_out"][si], 8, ("w_sout", si))
                si += 1
            elif mixer == "attn":
                blocked(S["w_q"][ai], I["attn_w_qkv"][ai, :, 0:1024], 8, ("w_q", ai))
                blocked(S["w_k"][ai], I["attn_w_qkv"][ai, :, 1024:1280], 2, ("w_k", ai))
                c.dma("pool", S["w_v"][ai], I["attn_w_qkv"][ai, :, 1280:1536].rearrange("(kc p) j -> p kc j", p=128),
                      W=[self.wr(("w_v", ai))])
                blocked(S["w_aout"][ai], I["attn_w_out"][ai], 8, ("w_aout", ai))
                ai += 1

    def xres(self, s, i):
        return self.wr(("XT", s, i))

    def xres_range(self, s, t0, n):
        return [self.xres(s, i) for i in range(t0 // 512, (t0 + n + 511) // 512)]

    def rmsnorm(self, es, xt, ht, gcol, NT, nkc=8, tmp=None):
        c, K = self.c, self.K
        sq, ln, rs = tmp
        for st in range(NT // 512):
            sl = slice(st * 512, (st + 1) * 512)
            ps = c.psum()
            for kc in range(nkc):
                c.op("act", lambda e: e.activation(out=sq[kc % 2].t[:], in_=xt.t[:, kc, sl], func=AF.Square),
                     R=[xt.R()], W=[sq[kc % 2].R()])
                c.op("pe", lambda e: e.matmul(ps.t[:], K["onesb"].t[:], sq[kc % 2].t[:], start=(kc == 0), stop=(kc == nkc - 1)),
                     R=[K["onesb"].R(), sq[kc % 2].R()], W=[ps.R()])
            c.op("act", lambda e: e.activation(out=ln.t[:], in_=ps.t[:], func=AF.Ln, scale=1.0 / (nkc * 128), bias=self.eps_col()),
                 R=[ps.R(), self.K["eps"].R()], W=[ln.R()])
            c.op("act", lambda e: e.activation(out=rs.t[:], in_=ln.t[:], func=AF.Exp, scale=-0.5),
                 R=[ln.R()], W=[rs.R()])
            for kc in range(nkc):
                c.op("dve", lambda e: e.scalar_tensor_tensor(out=ht.t[:, kc, sl], in0=xt.t[:, kc, sl], scalar=gcol(kc),
                                                             in1=rs.t[:], op0=ALU.mult, op1=ALU.mult),
                     R=[xt.R(), rs.R(), self.K["norm_g"].R(), self.K["final_g"].R()], W=[ht.R()])

    def eps_col(self):
        return self.K["eps"].t[:, 0:1]

    def norm_tmp(self, es):
        c = self.c
        sq = [c.sb(es, [128, 512], BF16, "sq") for _ in range(2)]
        ln = c.sb(es, [128, 512], F32, "ln")
        rs = c.sb(es, [128, 512], F32, "rs")
        return sq, ln, rs

    def transpose_in(self):
        c, I, K = self.c, self.I, self.K
        T = self.T
        with ExitStack() as es:
            K["eps"] = c.sb(self.c.es, [128, 1], F32, "eps")
            c.op("dve", lambda e: e.memset(K["eps"].t[:], EPS), W=[K["eps"].R()])
            xin = [c.sb(es, [128, 4, 1024], F32, "xin") for _ in range(2)]
            xo = [c.sb(es, [128, 8, 512], F32, "xo") for _ in range(2)]
            n = 0
            for s in range(self.NSEQ):
                for tt in range(T // 512):
                    b = n % 2
                    n += 1
                    c.dma("sp", xin[b].t[:], I["x"][s, tt * 512:(tt + 1) * 512, :].rearrange("(a p) f -> p a f", p=128),
                          W=[xin[b].R()])
                    for kc in range(8):
                        ps = c.psum()
                        for a in range(4):
                            c.op("pe", lambda e: e.transpose(ps.t[:, a * 128:(a + 1) * 128], xin[b].t[:, a, kc * 128:(kc + 1) * 128], K["ident"].t[:]),
                                 R=[xin[b].R(), K["ident"].R()], W=[ps.R()], inc=(a == 3))
                        eng = "act" if kc % 2 else "dve"
                        if eng == "act":
                            c.op("act", lambda e: e.copy(out=xo[b].t[:, kc, :], in_=ps.t[:]), R=[ps.R()], W=[xo[b].R()])
                        else:
                            c.op("dve", lambda e: e.tensor_copy(out=xo[b].t[:, kc, :], in_=ps.t[:]), R=[ps.R()], W=[xo[b].R()])
                    c.dma("pool", self.XT[s][:, tt * 512:(tt + 1) * 512].rearrange("(kc p) t -> p kc t", p=128), xo[b].t[:],
                          R=[xo[b].R()], W=[self.xres(s, tt)])

    def final_out(self):
        c, K = self.c, self.K
        T = self.T
        c.barrier()
        with ExitStack() as es:
            xt = [c.sb(es, [128, 8, 512], F32, "fx") for _ in range(2)]
            hn = [c.sb(es, [128, 8, 512], F32, "fh") for _ in range(2)]
            yo = [c.sb(es, [128, 4, 1024], F32, "fy") for _ in range(2)]
            tmp = self.norm_tmp(es)
            n = 0
            for s in range(self.NSEQ):
                for tt in range(T // 512):
                    b = n % 2
                    n += 1
                    c.dma("sp", xt[b].t[:], self.XT[s][:, tt * 512:(tt + 1) * 512].rearrange("(kc p) t -> p kc t", p=128),
                          R=[self.xres(s, tt)], W=[xt[b].R()])
                    if self.final:
                        self.rmsnorm(es, xt[b], hn[b], lambda kc: K["final_g"].t[:, kc:kc + 1], 512, tmp=tmp)
                        src = hn[b]
                    else:
                        src = xt[b]
                    for a in range(4):
                        for half in range(2):
                            ps = c.psum()
                            for q in range(4):
                                kc = half * 4 + q
                                c.op("pe", lambda e: e.transpose(ps.t[:, q * 128:(q + 1) * 128], src.t[:, kc, a * 128:(a + 1) * 128], K["ident"].t[:]),
                                     R=[src.R(), K["ident"].R()], W=[ps.R()], inc=(q == 3))
                            if half:
                                c.op("act", lambda e: e.copy(out=yo[b].t[:, a, half * 512:(half + 1) * 512], in_=ps.t[:]), R=[ps.R()], W=[yo[b].R()])
                            else:
                                c.op("dve", lambda e: e.tensor_copy(out=yo[b].t[:, a, half * 512:(half + 1) * 512], in_=ps.t[:]), R=[ps.R()], W=[yo[b].R()])
                    c.dma("pool", self.y[s, tt * 512:(tt + 1) * 512, :].rearrange("(a p) f -> p a f", p=128), yo[b].t[:],
                          R=[yo[b].R()], W=[self.wr(("y", s, tt))])

    def ffn(self, li, j):
        c, K, S = self.c, self.K, self.S
        T = self.T
        NT = 1024 if T >= 1024 else 512
        nst = NT // 512
        gbase = (li * 3 + (0 if j == 0 else 2)) * 8
        c.barrier()
        with ExitStack() as es:
            xt = [c.sb(es, [128, 8, NT], F32, "x") for _ in range(2)]
            ht = [c.sb(es, [128, 8, NT], BF16, "h") for _ in range(2)]
            act = c.sb(es, [128, 22, NT], BF16, "act")
            NB = 4
            wbuf = [c.sb(es, [128, 22 * 128], BF16, "w") for _ in range(NB)]
            sg = [c.sb(es, [128, 512], F32, "sg") for _ in range(2)]
            tmp = self.norm_tmp(es)
            tiles = [(s, i) for s in range(self.NSEQ) for i in range(T // NT)]
            stream = []
            for _ in tiles:
                for m in range(22):
                    stream.append(("gu", m))
                for m in range(8):
                    stream.append(("d", m))
            state = {"issued": 0}

            def issue():
                k = state["issued"]
                if k >= len(stream):
                    return
                kind, m = stream[k]
                wb = wbuf[k % NB]
                if kind == "gu":
                    c.dma("sp", wb.t[:, 0:1024], S["wg"][li, j, m].rearrange("p kc j -> p (kc j)"),
                          R=[self.wr((("wg", li, j), m))], W=[wb.R()])
                    c.dma("sp", wb.t[:, 1024:2048], S["wu"][li, j, m].rearrange("p kc j -> p (kc j)"),
                          R=[self.wr((("wu", li, j), m))], W=[wb.R()])
                else:
                    c.dma("sp", wb.t[:, :], S["wd"][li, j, m].rearrange("p kc j -> p (kc j)"),
                          R=[self.wr((("wd", li, j), m))], W=[wb.R()])
                state["issued"] = k + 1

            used = {"n": 0}

            def nextw():
                k = used["n"]
                used["n"] += 1
                while state["issued"] < min(k + NB, len(stream)):
                    issue()
                return wbuf[k % NB]

            def load_norm(idx):
                s, i = tiles[idx]
                b = idx % 2
                c.dma("sp", xt[b].t[:], self.XT[s][:, i * NT:(i + 1) * NT].rearrange("(kc p) t -> p kc t", p=128),
                      R=self.xres_range(s, i * NT, NT), W=[xt[b].R()])
                self.rmsnorm(es, xt[b], ht[b], lambda kc: K["norm_g"].t[:, gbase + kc:gbase + kc + 1], NT, tmp=tmp)

            def norm_pieces(idx):
                s_, i_ = tiles[idx]
                b_ = idx % 2
                x_, h_ = xt[b_], ht[b_]
                sq, ln, rs = tmp
                pn = self.pacc[0]
                P = []
                P.append(lambda: c.dma("sp", x_.t[:], self.XT[s_][:, i_ * NT:(i_ + 1) * NT].rearrange("(kc p) t -> p kc t", p=128),
                                       R=self.xres_range(s_, i_ * NT, NT), W=[x_.R()]))
                for st in range(nst):
                    sl = slice(st * 512, (st + 1) * 512)
                    for kc in range(8):
                        def psq(kc=kc, sl=sl):
                            c.op("act", lambda e: e.activation(out=sq[kc % 2].t[:], in_=x_.t[:, kc, sl], func=AF.Square), R=[x_.R()], W=[sq[kc % 2].R()])
                            c.op("pe", lambda e: e.matmul(pn.t[:], K["onesb"].t[:], sq[kc % 2].t[:], start=(kc == 0), stop=(kc == 7)),
                                 R=[K["onesb"].R(), sq[kc % 2].R()], W=[pn.R()])
                        P.append(psq)

                    def pln():
                        c.op("act", lambda e: e.activation(out=ln.t[:], in_=pn.t[:], func=AF.Ln, scale=1.0 / 1024, bias=self.eps_col()), R=[pn.R(), K["eps"].R()], W=[ln.R()])
                        c.op("act", lambda e: e.activation(out=rs.t[:], in_=ln.t[:], func=AF.Exp, scale=-0.5), R=[ln.R()], W=[rs.R()])
                    P.append(pln)
                    for half in range(2):
                        def pst(half=half, sl=sl):
                            for kc in range(half * 4, half * 4 + 4):
                                c.op("dve", lambda e: e.scalar_tensor_tensor(out=h_.t[:, kc, sl], in0=x_.t[:, kc, sl], scalar=K["norm_g"].t[:, gbase + kc:gbase + kc + 1],
                                                                             in1=rs.t[:], op0=ALU.mult, op1=ALU.mult),
                                     R=[x_.R(), rs.R(), K["norm_g"].R()], W=[h_.R()])
                        P.append(pst)
                return P

            for p in norm_pieces(0):
                p()
            for idx, (s, i) in enumerate(tiles):
                b = idx % 2
                h = ht[b]
                pend = norm_pieces(idx + 1) if idx + 1 < len(tiles) else []
                for m in range(22):
                    if pend:
                        pend.pop(0)()
                    wb = nextw()
                    for st in range(nst):
                        sl = slice(st * 512, (st + 1) * 512)
                        pg = c.psum()
                        pu = c.psum()
                        for kc in range(8):
                            c.op("pe", lambda e: e.matmul(pg.t[:], wb.t[:, kc * 128:(kc + 1) * 128], h.t[:, kc, sl], start=(kc == 0), stop=(kc == 7)),
                                 R=[wb.R(), h.R()], W=[pg.R()], inc=(kc == 7))
                        for kc in range(8):
                            c.op("pe", lambda e: e.matmul(pu.t[:], wb.t[:, 1024 + kc * 128:1024 + (kc + 1) * 128], h.t[:, kc, sl], start=(kc == 0), stop=(kc == 7)),
                                 R=[wb.R(), h.R()], W=[pu.R()], inc=(kc == 7))
                        sgt = sg[(m * nst + st) % 2]
                        c.op("act", lambda e: e.activation(out=sgt.t[:], in_=pg.t[:], func=AF.Silu), R=[pg.R()], W=[sgt.R()])
                        c.op("dve", lambda e: e.tensor_tensor(out=act.t[:, m, sl], in0=sgt.t[:], in1=pu.t[:], op=ALU.mult),
                             R=[sgt.R(), pu.R()], W=[act.R(("w", st))])
                while pend:
                    pend.pop(0)()
                for m in range(8):
                    wb = nextw()
                    for st in range(nst):
                        sl = slice(st * 512, (st + 1) * 512)
                        pd = c.psum()
                        for kc in range(22):
                            c.op("pe", lambda e: e.matmul(pd.t[:], wb.t[:, kc * 128:(kc + 1) * 128], act.t[:, kc, sl], start=(kc == 0), stop=(kc == 21)),
                                 R=[wb.R(), act.R(("w", st))], W=[pd.R()], inc=(kc == 21))
                        c.op("dve", lambda e: e.scalar_tensor_tensor(out=xt[b].t[:, m, sl], in0=pd.t[:], scalar=0.5, in1=xt[b].t[:, m, sl],
                                                                     op0=ALU.mult, op1=ALU.add),
                             R=[pd.R()], W=[xt[b].R()])
                c.dma("pool", self.XT[s][:, i * NT:(i + 1) * NT].rearrange("(kc p) t -> p kc t", p=128), xt[b].t[:],
                      R=[xt[b].R()], W=self.xres_range(s, i * NT, NT))

    def ssm(self, li, si):
        c, K, S, I = self.c, self.K, self.S, self.I
        T = self.T
        NT = 512
        ntile = T // NT
        nch = T // 128
        gbase = (li * 3 + 1) * 8
        gcol = lambda kc: K["norm_g"].t[:, gbase + kc:gbase + kc + 1]
        c.barrier()
        with ExitStack() as es:
            DT = c.sb(es, [128, nch, 64], F32, "DT")
            cw = c.sb(es, [128, 160], F32, "cw")
            cb = c.sb(es, [128, 32], F32, "cb")
            dtb = c.sb(es, [128, 64], F32, "dtb")
            Arow = c.sb(es, [128, 64], F32, "Arow")
            Dcol = c.sb(es, [128, 16], F32, "Dcol")
            ngc = c.sb(es, [128, 16], F32, "ngc")
            tri = [c.sb(es, [128, 128], F32, "tri") for _ in range(2)]
            one1 = c.sb(es, [128, 1], F32, "one1")
            c.dma("sp", cw.t[:], I["ssm_conv_w"][:, si * 160:(si + 1) * 160], W=[cw.R()])
            c.dma("sp", cb.t[:], I["ssm_conv_b"][:, si * 32:(si + 1) * 32], W=[cb.R()])
            c.dma("sp", dtb.t[:], I["ssm_dtb_row"][si:si + 1, :].to_broadcast([128, 64]), W=[dtb.R()])
            c.dma("sp", Arow.t[:], I["ssm_alog_row"][si:si + 1, :].to_broadcast([128, 64]), W=[Arow.R()])
            c.dma("sp", Dcol.t[:], I["ssm_D_col"][:, si * 16:(si + 1) * 16], W=[Dcol.R()])
            c.dma("sp", ngc.t[:], I["ssm_ng_col"][:, si * 16:(si + 1) * 16], W=[ngc.R()])
            c.dma("sp", tri[0].t[:], I["tri"][:, :], W=[tri[0].R()])
            c.dma("sp", tri[1].t[:], I["triT"][:, :], W=[tri[1].R()])
            c.op("dve", lambda e: e.memset(one1.t[:], 1.0), W=[one1.R()])
            c.op("act", lambda e: e.activation(out=Arow.t[:], in_=Arow.t[:], func=AF.Exp), R=[], W=[Arow.R()])
            c.op("dve", lambda e: e.tensor_scalar(out=Arow.t[:], in0=Arow.t[:], scalar1=-1.0, scalar2=None, op0=ALU.mult), W=[Arow.R()])
            for s in range(self.NSEQ):
                if 1 not in SSM_SKIP:
                    self.ssm_s1(es, s, si, gcol, DT, dtb, one1)
                c.barrier()
                if 2 not in SSM_SKIP:
                    self.ssm_s2(s, si, cw, cb)
                c.barrier()
                if 3 not in SSM_SKIP:
                    self.ssm_s3(s, li, si, DT, Arow, Dcol, ngc, tri)
                c.barrier()

    def ssm_s1(self, es0, s, si, gcol, DT, dtb, one1):
        c, K, S = self.c, self.K, self.S
        T = self.T
        NT = 512
        ntile = T // NT
        with ExitStack() as es:
            Win = c.sb(es, [128, 48, 1024], BF16, "Win")
            xt = c.sb(es, [128, 8, NT], F32, "sx")
            hv = [c.sb(es, [128, 8, NT], BF16, "shv") for _ in range(2)]
            ntmp = self.norm_tmp(es)
            wdt = c.sb(es, [128, 8, 64], BF16, "wdt")
            zo = [c.sb(es, [128, 4, 512], BF16, "zo") for _ in range(2)]
            xo = [c.sb(es, [128, 4, 512], F32, "xo") for _ in range(2)]
            t64 = [c.sb(es, [128, 64], F32, "t64") for _ in range(2)]
            c.dma("sp", wdt.t[:], S["w_dt"][si], R=[self.wr(("w_dt", si))], W=[wdt.R()])

            def load_norm(tt):
                c.dma("sp", xt.t[:], self.XT[s][:, tt * NT:(tt + 1) * NT].rearrange("(kc p) t -> p kc t", p=128),
                      R=self.xres_range(s, tt * NT, NT), W=[xt.R()])
                self.rmsnorm(es, xt, hv[tt % 2], gcol, NT, tmp=ntmp)

            load_norm(0)
            for q6 in range(6):
                c.dma("sp", Win.t[:, q6 * 8:(q6 + 1) * 8, :], S["w_in"][si, q6 * 8:(q6 + 1) * 8].rearrange("m p kc j -> p m (kc j)"),
                      R=[self.wr((("w_in", si), mm)) for mm in range(q6 * 8, (q6 + 1) * 8)], W=[Win.R(q6)])
            n = 0
            for tt in range(ntile):
                h = hv[tt % 2]
                if tt + 1 < ntile:
                    load_norm(tt + 1)
                for a in range(4):
                    ch = tt * 4 + a
                    ps = c.psum()
                    for kc in range(8):
                        c.op("pe", lambda e: e.matmul(ps.t[:, 0:64], h.t[:, kc, a * 128:(a + 1) * 128], wdt.t[:, kc, :], start=(kc == 0), stop=(kc == 7)),
                             R=[h.R(), wdt.R()], W=[ps.R()], inc=(kc == 7))
                    t = t64[ch % 2]
                    c.op("dve", lambda e: e.tensor_tensor(out=t.t[:], in0=ps.t[:, 0:64], in1=dtb.t[:], op=ALU.add), R=[ps.R(), dtb.R()], W=[t.R()])
                    c.op("act", lambda e: e.activation(out=t.t[:], in_=t.t[:], func=AF.Exp), W=[t.R()])
                    c.op("act", lambda e: e.activation(out=DT.t[:, ch, :], in_=t.t[:], func=AF.Ln, bias=one1.t[:, 0:1]), R=[t.R(), one1.R()], W=[DT.R()])
                for m in range(48):
                    ps = c.psum()
                    for kc in range(8):
                        c.op("pe", lambda e: e.matmul(ps.t[:], Win.t[:, m, kc * 128:(kc + 1) * 128], h.t[:, kc, :], start=(kc == 0), stop=(kc == 7)),
                             R=[Win.R(m // 8), h.R()], W=[ps.R()], inc=(kc == 7))
                    q, grp = m % 4, m // 4
                    if m < 16:
                        o = zo[grp % 2]
                        c.op("act", lambda e: e.activation(out=o.t[:, q, :], in_=ps.t[:], func=AF.Silu), R=[ps.R()], W=[o.R()])
                        if q == 3:
                            c.dma("pool", S["zs"][s, grp * 512:(grp + 1) * 512, tt * NT:(tt + 1) * NT].rearrange("(q p) t -> p q t", p=128), o.t[:],
                                  R=[o.R()], W=[self.wr(("zs", s, tt))])
                    else:
                        o = xo[grp % 2]
                        c.op("dve", lambda e: e.tensor_copy(out=o.t[:, q, :], in_=ps.t[:]), R=[ps.R()], W=[o.R()])
                        if q == 3:
                            c.dma("pool", S["xbc"][s, (grp - 4) * 512:(grp - 3) * 512, tt * NT:(tt + 1) * NT].rearrange("(q p) t -> p q t", p=128), o.t[:],
                                  R=[o.R()], W=[self.wr(("xbc", s))])

    def ssm_s2(self, s, si, cw, cb):
        c, K, S = self.c, self.K, self.S
        T = self.T
        NT = 512
        ntile = T // NT
        NBUF = 4
        with ExitStack() as es:
            cin = [c.sb(es, [128, NT + 4], F32, "cin") for _ in range(NBUF)]
            acc = [c.sb(es, [128, NT], F32, "cacc") for _ in range(NBUF)]
            ptm = [c.sb(es, [128, NT], F32, "cptm") for _ in range(NBUF)]
            fm = c.sb(es, [128, 32, NT], BF16, "fm")
            xtm = c.sb(es, [128, 4, 2048], BF16, "cxtm")
            btm = c.sb(es, [128, 4, 1024], BF16, "cbtm")
            its = [(tt, cc) for tt in range(ntile) for cc in range(32)]

            def load(n):
                tt, cc = its[n]
                t0 = tt * NT
                ci = cin[n % NBUF]
                lo = 2 if tt == 0 else 0
                hi = NT + 2 if tt == ntile - 1 else NT + 4
                if tt == 0:
                    c.op("pool", lambda e: e.memset(ci.t[:, 0:2], 0.0), W=[ci.R()])
                if tt == ntile - 1:
                    c.op("pool", lambda e: e.memset(ci.t[:, NT + 2:NT + 4], 0.0), W=[ci.R()])
                c.dma("sp", ci.t[:, lo:hi], S["xbc"][s, cc * 128:(cc + 1) * 128, t0 - 2 + lo:t0 - 2 + hi], R=[self.wr(("xbc", s))], W=[ci.R()])

            def ident(n):
                tt_, cc_ = its[n]
                ci_ = cin[n % NBUF]
                ac_ = acc[n % NBUF]
                c.op("act", lambda e: e.activation(out=ac_.t[:], in_=ci_.t[:, 2:NT + 2], func=AF.Identity, scale=cw.t[:, cc_ * 5 + 2:cc_ * 5 + 3], bias=cb.t[:, cc_:cc_ + 1]),
                     R=[ci_.R(), cw.R(), cb.R()], W=[ac_.R()])

            for n in range(min(NBUF - 1, len(its))):
                load(n)
            ident(0)
            for n, (tt, cc) in enumerate(its):
                t0 = tt * NT
                if n + NBUF - 1 < len(its):
                    load(n + NBUF - 1)
                if n + 1 < len(its):
                    ident(n + 1)
                ci = cin[n % NBUF]
                ac = acc[n % NBUF]
                pt = ptm[n % NBUF]
                wcol = lambda j: cw.t[:, cc * 5 + j:cc * 5 + j + 1]
                for jn, j in enumerate((0, 1, 3, 4)):
                    src = ac if jn == 0 else pt
                    c.op("dve", lambda e: e.scalar_tensor_tensor(out=pt.t[:], in0=ci.t[:, j:j + NT], scalar=wcol(j), in1=src.t[:], op0=ALU.mult, op1=ALU.add),
                         R=[ci.R(), cw.R(), src.R()], W=[pt.R()])
                c.op("act", lambda e: e.activation(out=fm.t[:, cc, :], in_=pt.t[:], func=AF.Silu), R=[pt.R()], W=[fm.R(cc)])
                if cc < 24:
                    half = cc % 2
                    pb = self.psb
                    for a in range(4):
                        c.op("pe", lambda e: e.transpose(pb.t[:, half * 512 + a * 128:half * 512 + (a + 1) * 128], fm.t[:, cc, a * 128:(a + 1) * 128], K["identb"].t[:]),
                             R=[fm.R(cc), K["identb"].R()], W=[pb.R(half)], inc=(a == 3))
                    if cc < 16:
                        dst, dr = xtm.t[:, :, cc * 128:(cc + 1) * 128], xtm.R()
                    else:
                        dst, dr = btm.t[:, :, (cc - 16) * 128:(cc - 15) * 128], btm.R()
                    c.op("act", lambda e: e.copy(out=dst, in_=pb.t[:, half * 512:(half + 1) * 512].rearrange("p (a f) -> p a f", a=4)),
                         R=[pb.R(half)], W=[dr])
                if cc == 31:
                    allfm = [fm.R(q) for q in range(32)]
                    c.dma("pool", S["xsT"][s][:, t0:t0 + NT].rearrange("(cc p) t -> p cc t", p=128), fm.t[:, 0:16, :], R=allfm[0:16], W=[self.wr(("xsT", s, tt))])
                    c.dma("pool", S["BT"][s][:, t0:t0 + NT].rearrange("(cc p) t -> p cc t", p=128), fm.t[:, 16:24, :], R=allfm[16:24], W=[self.wr(("BT", s, tt))])
                    c.dma("pool", S["CT"][s][:, t0:t0 + NT].rearrange("(cc p) t -> p cc t", p=128), fm.t[:, 24:32, :], R=allfm[24:32], W=[self.wr(("CT", s, tt))])
                    c.dma("pool", S["xs_tm"][s, t0:t0 + NT, :].rearrange("(a p) f -> p a f", p=128), xtm.t[:], R=[xtm.R()], W=[self.wr(("xs_tm", s, tt))])
                    c.dma("pool", S["B_tm"][s, t0:t0 + NT, :].rearrange("(a p) f -> p a f", p=128), btm.t[:], R=[btm.R()], W=[self.wr(("B_tm", s, tt))])

    def ssm_s3(self, s, li, si, DT, Arow, Dcol, ngc, tri):
        c = self.c
        base_banks = c.psum_banks
        c.psum_banks = base_banks + (self.pacc[0:1] if PCS_DEDICATED else self.pacc)
        c.psum_next = 0
        try:
            self._ssm_s3(s, li, si, DT, Arow, Dcol, ngc, tri)
        finally:
            c.psum_banks = base_banks
            c.psum_next = 0

    def _ssm_s3(self, s, li, si, DT, Arow, Dcol, ngc, tri):
        c, K, S = self.c, self.K, self.S
        T = self.T
        NT = 512
        ntile = T // NT
        with ExitStack() as es:
            St = c.sb(es, [128, 8, 256], F32, "St")
            Sb = c.sb(es, [128, 8, 256], BF16, "Sb")
            xtm = c.sb(es, [128, 4, 2048], BF16, "xtm")
            btm = c.sb(es, [128, 4, 1024], BF16, "btm")
            BTt = c.sb(es, [128, 8, NT], BF16, "BTt")
            CTt = c.sb(es, [128, 8, NT], BF16, "CTt")
            yacc = c.sb(es, [128, 16, NT], F32, "yacc")
            CH = []
            for _pb in range(NCH):
                CH.append((c.sb(es, [128, 32], F32, "atm"), c.sb(es, [128, 32], F32, "ncs"), c.sb(es, [128, 32], F32, "d1"),
                           c.sb(es, [128, 32], F32, "wend"), c.sb(es, [128, 32], F32, "dec"), c.sb(es, [96, 128], BF16, "cs3"),
                           c.sb(es, [128, 32], F32, "nb"), None))
            r1 = c.sb(es, [32, 128], F32, "r1")
            r2 = c.sb(es, [32, 128], F32, "r2")
            midt = c.sb(es, [32, 128], BF16, "midt")
            lot = c.sb(es, [32, 128], BF16, "lot")
            lndt = c.sb(es, [128, 32], F32, "lndt")
            XW = [c.sb(es, [128, 256], BF16, "xwg") for _ in range(2)]
            Sel3 = c.sb(es, [96, 32 * 128], BF16, "sel3")
            c.dma("pool", Sel3.t[:], self.I["sel3"][:, :], W=[Sel3.R()])
            cbm = [c.sb(es, [128, 128], F32, "cbm") for _ in range(2)]
            ECS = [c.sb(es, [128, 512], F32, "ECS") for _ in range(2)]
            Lt = [c.sb(es, [128, 512], F32, "Lt") for _ in range(2)]
            Mt = [c.sb(es, [128, 512], BF16, "Mt") for _ in range(2)]
            Cs = [c.sb(es, [128, 512], BF16, "Cs") for _ in range(2)]
            stmp = [c.sb(es, [128, 256], F32, "stmp") for _ in range(2)]
            zt = c.sb(es, [128, 16, NT], BF16, "zt")
            xf = c.sb(es, [128, 16, NT], BF16, "xf")
            xr = c.sb(es, [128, 8, NT], F32, "xr")
            wso = [c.sb(es, [128, 2048], BF16, "wso") for _ in range(2)]
            ntmp = self.norm_tmp(es)
            rsb = c.sb(es, [128, 512], F32, "rsb")

            def prologue_pieces(d, tt, a, pb):
                ch = tt * 4 + a
                dt = DT.t[:, ch, d * 32:(d + 1) * 32]
                trm = tri[d]
                atm, ncs, d1, wend, dec, cs3, nb, _ = CH[pb]
                hold = {}

                def p0():
                    c.op("dve", lambda e: e.tensor_tensor(out=atm.t[:], in0=dt, in1=Arow.t[:, d * 32:(d + 1) * 32], op=ALU.mult),
                         R=[DT.R(), Arow.R()], W=[atm.R()])
                    pcs = hold["pcs"] = self.pacc[1] if PCS_DEDICATED else c.psum()
                    c.op("pe", lambda e: e.matmul(pcs.t[:, 0:32], trm.t[:], atm.t[:], start=True, stop=True), R=[trm.R(), atm.R()], W=[pcs.R()], inc=False)
                    c.op("pe", lambda e: e.matmul(pcs.t[:, 32:64], K["ones"].t[:], atm.t[:], start=True, stop=True), R=[K["ones"].R(), atm.R()], W=[pcs.R()], inc=False)
                    c.op("pe", lambda e: e.matmul(pcs.t[0:32, 64:192], atm.t[:], trm.t[:], start=True, stop=True), R=[trm.R(), atm.R()], W=[pcs.R()])

                def p1():
                    pcs = hold["pcs"]
                    c.op("dve", lambda e: e.tensor_scalar(out=ncs.t[:], in0=pcs.t[:, 0:32], scalar1=-1.0, scalar2=None, op0=ALU.mult), R=[pcs.R()], W=[ncs.R()])
                    c.op("dve", lambda e: e.tensor_tensor(out=d1.t[:], in0=pcs.t[:, 32:64], in1=ncs.t[:], op=ALU.add), R=[pcs.R(), ncs.R()], W=[d1.R()])
                    c.op("act", lambda e: e.activation(out=lndt.t[:], in_=dt, func=AF.Ln), R=[DT.R()], W=[lndt.R()])

                def p2():
                    pcs = hold["pcs"]
                    c.op("act", lambda e: e.activation(out=d1.t[:], in_=d1.t[:], func=AF.Exp), W=[d1.R()])
                    c.op("act", lambda e: e.activation(out=dec.t[:], in_=pcs.t[:, 32:64], func=AF.Exp), R=[pcs.R()], W=[dec.R()])
                    c.op("act", lambda e: e.copy(out=cs3.t[0:32, :], in_=pcs.t[0:32, 64:192]), R=[pcs.R()], W=[cs3.R()])
                    c.op("dve", lambda e: e.tensor_tensor(out=nb.t[:], in0=lndt.t[:], in1=ncs.t[:], op=ALU.add), R=[lndt.R(), ncs.R()], W=[nb.R()])

                def p3():
                    pcs = hold["pcs"]
                    c.op("dve", lambda e: e.tensor_tensor(out=wend.t[:], in0=d1.t[:], in1=dt, op=ALU.mult), R=[d1.R(), DT.R()], W=[wend.R()])
                    c.op("dve", lambda e: e.tensor_tensor(out=r1.t[:], in0=pcs.t[0:32, 64:192], in1=cs3.t[0:32, :], op=ALU.subtract), R=[pcs.R(), cs3.R()], W=[r1.R()])

                def p4():
                    c.op("act", lambda e: e.copy(out=midt.t[:], in_=r1.t[:]), R=[r1.R()], W=[midt.R()])

                def p5():
                    c.op("dve", lambda e: e.tensor_tensor(out=r2.t[:], in0=r1.t[:], in1=midt.t[:], op=ALU.subtract), R=[r1.R(), midt.R()], W=[r2.R()])
                    c.op("dve", lambda e: e.tensor_copy(out=cs3.t[32:64, :], in_=midt.t[:]), R=[midt.R()], W=[cs3.R()])

                def p6():
                    c.op("act", lambda e: e.copy(out=lot.t[:], in_=r2.t[:]), R=[r2.R()], W=[lot.R()])

                def p7():
                    c.op("dve", lambda e: e.tensor_copy(out=cs3.t[64:96, :], in_=lot.t[:]), R=[lot.R()], W=[cs3.R()])

                return [p0, p1, p2, p3, p4, p5, p6, p7]

            PS = {}

            def st1(d, tt, a, pb, g):
                asl = slice(a * 128, (a + 1) * 128)
                cs3 = CH[pb][5]
                pcb = c.psum()
                c.op("pe", lambda e: e.matmul(pcb.t[:, 0:128], BTt.t[:, g, asl], CTt.t[:, g, asl], start=True, stop=True), R=[BTt.R(a), CTt.R(a)], W=[pcb.R()])
                pcr = c.psum()
                for rr in range(4):
                    r = g * 4 + rr
                    c.op("pe", lambda e: e.matmul(pcr.t[:, rr * 128:(rr + 1) * 128], Sel3.t[:, r * 128:(r + 1) * 128], cs3.t[:], start=True, stop=True),
                         R=[Sel3.R(), cs3.R()], W=[pcr.R()], inc=(rr == 3))
                PS[(a, g)] = [pcb, pcr, None]

            def st2(d, tt, a, pb, g):
                trm = tri[d]
                ncs = CH[pb][6]
                k = g % 2
                pcb, pcr, _ = PS[(a, g)]
                c.op("act", lambda e: e.activation(out=ECS[k].t[:], in_=pcr.t[:], func=AF.Exp), R=[pcr.R()], W=[ECS[k].R()])
                for rr in range(4):
                    r = g * 4 + rr
                    c.op("act", lambda e: e.activation(out=Lt[k].t[:, rr * 128:(rr + 1) * 128], in_=pcr.t[:, rr * 128:(rr + 1) * 128], func=AF.Exp, bias=ncs.t[:, r:r + 1]),
                         R=[pcr.R(), ncs.R()], W=[Lt[k].R()])
                c.op("dve", lambda e: e.tensor_tensor(out=cbm[k].t[:], in0=pcb.t[:, 0:128], in1=trm.t[:], op=ALU.mult), R=[pcb.R(), trm.R()], W=[cbm[k].R()])

            def st3(d, tt, a, pb, g):
                asl = slice(a * 128, (a + 1) * 128)
                k = g % 2
                c.op("dve", lambda e: e.scalar_tensor_tensor(out=Mt[k].t[:].rearrange("p (r i) -> p r i", r=4), in0=Lt[k].t[:].rearrange("p (r i) -> p r i", r=4), scalar=1e30,
                                                             in1=cbm[k].t[:].unsqueeze(1).to_broadcast([128, 4, 128]), op0=ALU.min, op1=ALU.mult),
                     R=[Lt[k].R(), cbm[k].R()], W=[Mt[k].R()])
                c.op("pool", lambda e: e.tensor_tensor(out=Cs[k].t[:].rearrange("p (r i) -> p r i", r=4), in0=ECS[k].t[:].rearrange("p (r i) -> p r i", r=4),
                                                      in1=CTt.t[:, g, asl].unsqueeze(1).to_broadcast([128, 4, 128]), op=ALU.mult),
                     R=[ECS[k].R(), CTt.R(a)], W=[Cs[k].R()])
                wend = CH[pb][3]
                c.op("pool", lambda e: e.tensor_tensor(out=XW[k].t[:].rearrange("p (r d) -> p r d", r=4), in0=xtm.t[:, a, g * 256:(g + 1) * 256].rearrange("p (r d) -> p r d", r=4),
                                                      in1=wend.t[:, g * 4:(g + 1) * 4].unsqueeze(2).to_broadcast([128, 4, 64]), op=ALU.mult),
                     R=[xtm.R(a), wend.R()], W=[XW[k].R()])

            def st4(d, tt, a, pb, g):
                k = g % 2
                py = c.psum()
                PS[(a, g)][2] = py
                for rr in range(4):
                    r = g * 4 + rr
                    half = rr % 2
                    osl = py.t[half * 64:(half + 1) * 64, (rr // 2) * 128:(rr // 2 + 1) * 128]
                    c.op("pe", lambda e: e.matmul(osl, xtm.t[:, a, r * 64:(r + 1) * 64], Mt[k].t[:, rr * 128:(rr + 1) * 128], start=True, stop=False),
                         R=[xtm.R(a), Mt[k].R()], W=[py.R()], inc=False)
                    c.op("pe", lambda e: e.matmul(osl, Sb.t[:, g, rr * 64:(rr + 1) * 64], Cs[k].t[:, rr * 128:(rr + 1) * 128], start=False, stop=True),
                         R=[Sb.R(g), Cs[k].R()], W=[py.R()], inc=(rr == 3))
                pst = c.psum()
                PS[(a, g)].append(pst)
                c.op("pe", lambda e: e.matmul(pst.t[:, 0:256], btm.t[:, a, g * 128:(g + 1) * 128], XW[k].t[:], start=True, stop=True),
                     R=[btm.R(a), XW[k].R()], W=[pst.R()])

            def st5(d, tt, a, pb, g):
                asl = slice(a * 128, (a + 1) * 128)
                dec = CH[pb][4]
                k = g % 2
                _ps = PS.pop((a, g))
                py, pst = _ps[2], _ps[3]
                ydst = yacc.t[:, 2 * g:2 * g + 2, asl]
                ysrc = py.t[:, 0:256].rearrange("p (c i) -> p c i", c=2)
                if d == 0:
                    c.op("act", lambda e: e.copy(out=ydst, in_=ysrc), R=[py.R()], W=[yacc.R(g)])
                else:
                    c.op("dve", lambda e: e.tensor_tensor(out=ydst, in0=ysrc, in1=ydst, op=ALU.add), R=[py.R()], W=[yacc.R(g)])
                st = stmp[k]
                c.op("dve", lambda e: e.tensor_tensor(out=st.t[:].rearrange("p (r d) -> p r d", r=4), in0=St.t[:, g, :].rearrange("p (r d) -> p r d", r=4),
                                                     in1=dec.t[:, g * 4:(g + 1) * 4].unsqueeze(2).to_broadcast([128, 4, 64]), op=ALU.mult),
                     R=[St.R(g), dec.R()], W=[st.R()])
                c.op("dve", lambda e: e.tensor_tensor(out=St.t[:, g, :], in0=pst.t[:, 0:256], in1=st.t[:], op=ALU.add), R=[pst.R(), st.R()], W=[St.R(g)])
                if d == 0:
                    c.op("dve", lambda e: e.tensor_copy(out=Sb.t[:, g, :], in_=St.t[:, g, :]), R=[St.R(g)], W=[Sb.R(g)])
                else:
                    c.op("act", lambda e: e.copy(out=Sb.t[:, g, :], in_=St.t[:, g, :]), R=[St.R(g)], W=[Sb.R(g)])

            def run_tile(d, tt):
                aorder = list(range(4)) if d == 0 else [3, 2, 1, 0]
                items = [(ai_, a, g) for ai_, a in enumerate(aorder) for g in range(8)]
                n = len(items)
                base = self._chunk_ctr
                self._chunk_ctr += 4
                for p in prologue_pieces(d, tt, aorder[0], base % NCH):
                    p()
                stages = [st1, st2, st3, st4, st5]
                pend = []
                for t in range(n + 4):
                    if t < n:
                        ai_, a, g = items[t]
                        if g == PRO_SLOT and ai_ + 1 < 4:
                            pend = prologue_pieces(d, tt, aorder[ai_ + 1], (base + ai_ + 1) % NCH)
                        for _ in range(PRO_RATE):
                            if pend:
                                pend.pop(0)()
                    for si_, fn in enumerate(stages):
                        j = t - si_
                        if 0 <= j < n:
                            ai_, a, g = items[j]
                            fn(d, tt, a, (base + ai_) % NCH, g)
                assert not pend

            def load_tile(tt, aorder):
                t0 = tt * NT
                for a in aorder:
                    ta = t0 + a * 128
                    c.dma("sp", xtm.t[:, a, :], S["xs_tm"][s, ta:ta + 128, :], R=[self.wr(("xs_tm", s, tt))], W=[xtm.R(a)])
                    c.dma("sp", btm.t[:, a, :], S["B_tm"][s, ta:ta + 128, :], R=[self.wr(("B_tm", s, tt))], W=[btm.R(a)])
                    c.dma("sp", BTt.t[:, :, a * 128:(a + 1) * 128], S["BT"][s][:, ta:ta + 128].rearrange("(cc p) t -> p cc t", p=128), R=[self.wr(("BT", s, tt))], W=[BTt.R(a)])
                    c.dma("sp", CTt.t[:, :, a * 128:(a + 1) * 128], S["CT"][s][:, ta:ta + 128].rearrange("(cc p) t -> p cc t", p=128), R=[self.wr(("CT", s, tt))], W=[CTt.R(a)])

            self._chunk_ctr = 0
            for d in range(2):
                c.op("dve", lambda e: e.memset(St.t[:], 0.0), W=[St.R(g) for g in range(8)])
                c.op("dve", lambda e: e.memset(Sb.t[:], 0.0), W=[Sb.R(g) for g in range(8)])
                order = range(ntile) if d == 0 else range(ntile - 1, -1, -1)
                order = list(order)
                aord = list(range(4)) if d == 0 else [3, 2, 1, 0]
                load_tile(order[0], aord)
                for oi, tt in enumerate(order):
                    t0 = tt * NT
                    if d == 1:
                        c.dma("sp", yacc.t[:], S["yf"][s][:, t0:t0 + NT].rearrange("(cc p) t -> p cc t", p=128), R=[self.wr(("yf", s, tt))], W=[yacc.R(g) for g in range(8)])
                        c.dma("sp", xf.t[:], S["xsT"][s][:, t0:t0 + NT].rearrange("(cc p) t -> p cc t", p=128), R=[self.wr(("xsT", s, tt))], W=[xf.R(q_) for q_ in range(16)])
                        c.dma("sp", zt.t[:], S["zs"][s][:, t0:t0 + NT].rearrange("(cc p) t -> p cc t", p=128), R=[self.wr(("zs", s, tt))], W=[zt.R()])
                        c.dma("sp", xr.t[:], self.XT[s][:, t0:t0 + NT].rearrange("(kc p) t -> p kc t", p=128), R=self.xres_range(s, t0, NT), W=[xr.R()])
                        for mo in range(2):
                            c.dma("sp", wso[mo].t[:], S["w_sout"][si, mo].rearrange("p kc j -> p (kc j)"), R=[self.wr((("w_sout", si), mo))], W=[wso[mo].R()])
                    run_tile(d, tt)
                    if d == 0:
                        c.dma("pool", S["yf"][s][:, t0:t0 + NT].rearrange("(cc p) t -> p cc t", p=128), yacc.t[:], R=[yacc.R(g) for g in range(8)], W=[self.wr(("yf", s, tt))])
                        if oi + 1 < len(order):
                            load_tile(order[oi + 1], aord)
                        continue
                    if oi + 1 < len(order):
                        load_tile(order[oi + 1], aord)
                    sq, ln, rs = ntmp
                    rs2 = [rs, rsb]

                    def epiA(g):
                        ps = c.psum()
                        for q in range(2):
                            cc = 2 * g + q
                            c.op("dve", lambda e: e.scalar_tensor_tensor(out=yacc.t[:, cc, :], in0=xf.t[:, cc, :], scalar=Dcol.t[:, cc:cc + 1], in1=yacc.t[:, cc, :], op0=ALU.mult, op1=ALU.add),
                                 R=[xf.R(cc), Dcol.R()], W=[yacc.R(g)])
                            c.op("pool", lambda e: e.tensor_tensor(out=yacc.t[:, cc, :], in0=yacc.t[:, cc, :], in1=zt.t[:, cc, :], op=ALU.mult), R=[zt.R()], W=[yacc.R(g)])
                            c.op("act", lambda e: e.activation(out=sq[q].t[:], in_=yacc.t[:, cc, :], func=AF.Square), R=[yacc.R(g)], W=[sq[q].R()])
                            c.op("pe", lambda e: e.matmul(ps.t[:], K["onesb"].t[:], sq[q].t[:], start=(q == 0), stop=(q == 1)), R=[K["onesb"].R(), sq[q].R()], W=[ps.R()])
                        c.op("act", lambda e: e.activation(out=ln.t[:], in_=ps.t[:], func=AF.Ln, scale=1.0 / 256, bias=self.eps_col()), R=[ps.R(), K["eps"].R()], W=[ln.R()])
                        c.op("act", lambda e: e.activation(out=rs2[g % 2].t[:], in_=ln.t[:], func=AF.Exp, scale=-0.5), R=[ln.R()], W=[rs2[g % 2].R()])

                    def epiB(g):
                        for q in range(2):
                            cc = 2 * g + q
                            c.op("dve", lambda e: e.scalar_tensor_tensor(out=xf.t[:, cc, :], in0=yacc.t[:, cc, :], scalar=ngc.t[:, cc:cc + 1], in1=rs2[g % 2].t[:], op0=ALU.mult, op1=ALU.mult),
                                 R=[yacc.R(g), ngc.R(), rs2[g % 2].R()], W=[xf.R(cc)])

                    epiA(0)
                    for g in range(8):
                        if g + 1 < 8:
                            epiA(g + 1)
                        epiB(g)
                    for mo in range(8):
                        wb = wso[mo % 2]
                        ps = c.psum()
                        for kc in range(16):
                            c.op("pe", lambda e: e.matmul(ps.t[:], wb.t[:, kc * 128:(kc + 1) * 128], xf.t[:, kc, :], start=(kc == 0), stop=(kc == 15)),
                                 R=[wb.R(), xf.R(kc)], W=[ps.R()], inc=(kc == 15))
                        if mo + 2 < 8:
                            c.dma("sp", wb.t[:], S["w_sout"][si, mo + 2].rearrange("p kc j -> p (kc j)"), R=[self.wr((("w_sout", si), mo + 2))], W=[wb.R()])
                        c.op("dve", lambda e: e.tensor_tensor(out=xr.t[:, mo, :], in0=ps.t[:], in1=xr.t[:, mo, :], op=ALU.add), R=[ps.R()], W=[xr.R()])
                    c.dma("pool", self.XT[s][:, t0:t0 + NT].rearrange("(kc p) t -> p kc t", p=128), xr.t[:], R=[xr.R()], W=self.xres_range(s, t0, NT))

    def qk_post(self, ps, gcol, out_ap, bias_col, cs, sn, tm):
        c, K = self.c, self.K
        qraw, sqf, ln, rs, qn, t1, t2 = tm
        c.op("act", lambda e: e.copy(out=qraw.t[:], in_=ps.t[:]), R=[ps.R()], W=[qraw.R()])
        c.op("act", lambda e: e.activation(out=sqf.t[:], in_=ps.t[:], func=AF.Square), R=[ps.R()], W=[sqf.R()])
        p2 = c.psum()
        c.op("pe", lambda e: e.matmul(p2.t[:], K["blk"].t[:], sqf.t[:], start=True, stop=True), R=[K["blk"].R(), sqf.R()], W=[p2.R()])
        c.op("act", lambda e: e.activation(out=ln.t[:], in_=p2.t[:], func=AF.Ln, scale=1.0 / 64, bias=self.eps_col()),
             R=[p2.R(), K["eps"].R()], W=[ln.R()])
        c.op("act", lambda e: e.activation(out=rs.t[:], in_=ln.t[:], func=AF.Exp, scale=-0.5, bias=bias_col),
             R=[ln.R(), K["lnq"].R()], W=[rs.R()])
        c.op("dve", lambda e: e.scalar_tensor_tensor(out=qn.t[:], in0=qraw.t[:], scalar=gcol, in1=rs.t[:], op0=ALU.mult, op1=ALU.mult),
             R=[qraw.R(), rs.R(), K["qkg"].R()], W=[qn.R()])
        p3 = c.psum()
        c.op("pe", lambda e: e.matmul(p3.t[:], K["Rm"].t[:], qn.t[:], start=True, stop=True), R=[K["Rm"].R(), qn.R()], W=[p3.R()])
        c.op("pool", lambda e: e.tensor_tensor(out=t1.t[:], in0=qn.t[:], in1=cs.t[:], op=ALU.mult), R=[qn.R(), cs.R()], W=[t1.R()])
        c.op("dve", lambda e: e.tensor_tensor(out=t2.t[:], in0=p3.t[:], in1=sn.t[:], op=ALU.mult), R=[p3.R(), sn.R()], W=[t2.R()])
        if callable(out_ap):
            out_ap(t1, t2)
        else:
            c.op("pool", lambda e: e.tensor_tensor(out=out_ap[0], in0=t1.t[:], in1=t2.t[:], op=ALU.add), R=[t1.R(), t2.R()], W=[out_ap[1]])

    def attn(self, li, ai):
        c, K, S, I = self.c, self.K, self.S, self.I
        T = self.T
        NT = 512
        ntile = T // NT
        nkc = T // 128
        gbase = (li * 3 + 1) * 8
        c.barrier()
        with ExitStack() as es:
            KT = c.sb(es, [128, 2, T], BF16, "KT")
            Vx = c.sb(es, [128, nkc, 4, 128], BF16, "Vx")
            xt = c.sb(es, [128, 8, NT], F32, "ax")
            ht = c.sb(es, [128, 8, NT], BF16, "ah")
            cs = c.sb(es, [128, NT], F32, "cos")
            sn = c.sb(es, [128, NT], F32, "sin")
            tms = [[c.sb(es, [128, 512], F32, "qk") for _ in range(7)] for _ in range(2)]
            tmi = [0]

            def next_tm():
                tmi[0] += 1
                return tms[tmi[0] % 2]
            ntmp = self.norm_tmp(es)
            K["Rm"] = c.sb(es, [128, 128], F32, "Rm")
            K["qkg"] = c.sb(es, [128, 2], F32, "qkg")
            qkall = c.sb(es, [128, 2 * max(self.n_att, 1)], F32, "qkall")
            K["lnq"] = c.sb(es, [128, 2], F32, "lnq")
            c.dma("sp", K["Rm"].t[:], I["Rm"][:, :], W=[K["Rm"].R()])
            na_ = max(self.n_att, 1)
            c.dma("sp", qkall.t[:, 0:na_], I["attn_qg_col"][:, :], W=[qkall.R()])
            c.dma("sp", qkall.t[:, na_:2 * na_], I["attn_kg_col"][:, :], W=[qkall.R()])
            c.op("dve", lambda e: e.tensor_copy(out=K["qkg"].t[:, 0:1], in_=qkall.t[:, ai:ai + 1]), R=[qkall.R()], W=[K["qkg"].R()])
            c.op("dve", lambda e: e.tensor_copy(out=K["qkg"].t[:, 1:2], in_=qkall.t[:, na_ + ai:na_ + ai + 1]), R=[qkall.R()], W=[K["qkg"].R()])
            c.op("dve", lambda e: e.memset(K["lnq"].t[:, 0:1], float(np.log(0.125))), W=[K["lnq"].R()])
            c.op("dve", lambda e: e.memset(K["lnq"].t[:, 1:2], 0.0), W=[K["lnq"].R()])
            c.op("dve", lambda e: e.memset(Vx.t[:, :, :, 64:128], 1.0), W=[Vx.R()])
            gcol = lambda kc: K["norm_g"].t[:, gbase + kc:gbase + kc + 1]

            def load_tile(s, tt):
                c.dma("sp", xt.t[:], self.XT[s][:, tt * NT:(tt + 1) * NT].rearrange("(kc p) t -> p kc t", p=128),
                      R=self.xres_range(s, tt * NT, NT), W=[xt.R()])
                c.dma("sp", cs.t[:], I["cosT"][:, tt * NT:(tt + 1) * NT], W=[cs.R()])
                c.dma("sp", sn.t[:], I["sinT"][:, tt * NT:(tt + 1) * NT], W=[sn.R()])
                self.rmsnorm(es, xt, ht, gcol, NT, tmp=ntmp)

            for s in range(self.NSEQ):
                with ExitStack() as e1:
                    wk = c.sb(e1, [128, 2, 1024], BF16, "wk")
                    wv = c.sb(e1, [128, 8, 256], BF16, "wv")
                    c.dma("sp", wk.t[:], S["w_k"][ai].rearrange("m p kc j -> p m (kc j)"),
                          R=[self.wr((("w_k", ai), m)) for m in range(2)], W=[wk.R()])
                    c.dma("sp", wv.t[:], S["w_v"][ai], R=[self.wr(("w_v", ai))], W=[wv.R()])
                    for tt in range(ntile):
                        load_tile(s, tt)
                        for kv in range(2):
                            ps = c.psum()
                            for kc in range(8):
                                c.op("pe", lambda e: e.matmul(ps.t[:], wk.t[:, kv, kc * 128:(kc + 1) * 128], ht.t[:, kc, :], start=(kc == 0), stop=(kc == 7)),
                                     R=[wk.R(), ht.R()], W=[ps.R()], inc=(kc == 7))
                            self.qk_post(ps, K["qkg"].t[:, 1:2], (KT.t[:, kv, tt * NT:(tt + 1) * NT], KT.R()), K["lnq"].t[:, 1:2], cs, sn, next_tm())
                        for a in range(NT // 128):
                            ps = c.psum()
                            for kc in range(8):
                                c.op("pe", lambda e: e.matmul(ps.t[:, 0:256], ht.t[:, kc, a * 128:(a + 1) * 128], wv.t[:, kc, :], start=(kc == 0), stop=(kc == 7)),
                                     R=[wv.R(), ht.R()], W=[ps.R()], inc=(kc == 7))
                            c.op("dve", lambda e: e.tensor_copy(out=Vx.t[:, tt * (NT // 128) + a, :, 0:64], in_=ps.t[:, 0:256].rearrange("p (g d) -> p g d", g=4)),
                                 R=[ps.R()], W=[Vx.R()])
                    c.barrier()
                with ExitStack() as e2:
                    wq = c.sb(e2, [128, 8, 1024], BF16, "wq")
                    wob = [c.sb(e2, [128, 1024], BF16, "wob") for _ in range(2)]
                    QT2 = [c.sb(e2, [128, 16, NT], BF16, "QP") for _ in range(2)]
                    for QT_ in QT2:
                        c.op("pool", lambda e: e.memset(QT_.t[:], 0.0), W=[QT_.R(h) for h in range(16)])
                    OT = c.sb(e2, [128, 8, NT], BF16, "OT")
                    PT = [c.sb(e2, [128, 512], BF16, "PT") for _ in range(3)]
                    rec = [c.sb(e2, [128, 512], F32, "rec") for _ in range(2)]
                    xr2 = [c.sb(e2, [128, 512], F32, "xr2") for _ in range(2)]
                    c.dma("sp", wq.t[:], S["w_q"][ai].rearrange("m p kc j -> p m (kc j)"),
                          R=[self.wr((("w_q", ai), m)) for m in range(8)], W=[wq.R()])
                    banks = c.psum_banks
                    loop_banks = banks[0:3]
                    prep_banks = banks[3:5]
                    rot = {"loop": 0, "prep": 0}

                    def use(which):
                        c.psum_banks = loop_banks if which == "loop" else prep_banks
                        c.psum_next = rot[which]

                    def save(which):
                        rot[which] = c.psum_next

                    def prep_pieces(tt, QTt):
                        P = []

                        def p_load():
                            c.dma("sp", xt.t[:], self.XT[s][:, tt * NT:(tt + 1) * NT].rearrange("(kc p) t -> p kc t", p=128), W=[xt.R()])
                            c.dma("sp", cs.t[:], I["cosT"][:, tt * NT:(tt + 1) * NT], W=[cs.R()])
                            c.dma("sp", sn.t[:], I["sinT"][:, tt * NT:(tt + 1) * NT], W=[sn.R()])
                        P.append(p_load)
                        P.append(lambda: self.rmsnorm(es, xt, ht, gcol, NT, tmp=ntmp))
                        for m in range(8):
                            H = {}
                            tm = tms[m % 2]
                            qraw, sqf, ln, rs, qn, t1, t2 = tm

                            def pa(m=m, H=H):
                                ps = H["ps"] = c.psum()
                                for kc in range(8):
                                    c.op("pe", lambda e: e.matmul(ps.t[:], wq.t[:, m, kc * 128:(kc + 1) * 128], ht.t[:, kc, :], start=(kc == 0), stop=(kc == 7)),
                                         R=[wq.R(), ht.R()], W=[ps.R()], inc=(kc == 7))

                            def pa2(H=H, qraw=qraw, sqf=sqf):
                                ps = H["ps"]
                                c.op("act", lambda e: e.copy(out=qraw.t[:], in_=ps.t[:]), R=[ps.R()], W=[qraw.R()])
                                c.op("act", lambda e: e.activation(out=sqf.t[:], in_=ps.t[:], func=AF.Square), R=[ps.R()], W=[sqf.R()])

                            def pb(H=H, sqf=sqf):
                                p2 = H["p2"] = c.psum()
                                c.op("pe", lambda e: e.matmul(p2.t[:], K["blk"].t[:], sqf.t[:], start=True, stop=True), R=[K["blk"].R(), sqf.R()], W=[p2.R()])

                            def pc(H=H, ln=ln, rs=rs, qn=qn, qraw=qraw):
                                p2 = H["p2"]
                                c.op("act", lambda e: e.activation(out=ln.t[:], in_=p2.t[:], func=AF.Ln, scale=1.0 / 64, bias=self.eps_col()),
                                     R=[p2.R(), K["eps"].R()], W=[ln.R()])
                                c.op("act", lambda e: e.activation(out=rs.t[:], in_=ln.t[:], func=AF.Exp, scale=-0.5, bias=K["lnq"].t[:, 0:1]),
                                     R=[ln.R(), K["lnq"].R()], W=[rs.R()])
                                c.op("dve", lambda e: e.scalar_tensor_tensor(out=qn.t[:], in0=qraw.t[:], scalar=K["qkg"].t[:, 0:1], in1=rs.t[:], op0=ALU.mult, op1=ALU.mult),
                                     R=[qraw.R(), rs.R(), K["qkg"].R()], W=[qn.R()])

                            def pd(H=H, qn=qn, t1=t1):
                                p3 = H["p3"] = c.psum()
                                c.op("pe", lambda e: e.matmul(p3.t[:], K["Rm"].t[:], qn.t[:], start=True, stop=True), R=[K["Rm"].R(), qn.R()], W=[p3.R()])
                                c.op("pool", lambda e: e.tensor_tensor(out=t1.t[:], in0=qn.t[:], in1=cs.t[:], op=ALU.mult), R=[qn.R(), cs.R()], W=[t1.R()])

                            def pe_(m=m, H=H, t1=t1, t2=t2):
                                p3 = H["p3"]
                                c.op("dve", lambda e: e.tensor_tensor(out=t2.t[:], in0=p3.t[:], in1=sn.t[:], op=ALU.mult), R=[p3.R(), sn.R()], W=[t2.R()])
                                for hh in range(2):
                                    h = 2 * m + hh
                                    kh = (h // 4) % 2
                                    c.op("pool", lambda e: e.tensor_tensor(out=QTt.t[kh * 64:(kh + 1) * 64, h, :], in0=t1.t[hh * 64:(hh + 1) * 64, :],
                                                                          in1=t2.t[hh * 64:(hh + 1) * 64, :], op=ALU.add),
                                         R=[t1.R(), t2.R()], W=[QTt.R(h)])
                            P += [pa, pa2, pb, pc, pd, pe_]
                        return P

                    def run_piece(p):
                        save("loop")
                        use("prep")
                        p()
                        save("prep")
                        use("loop")

                    use("loop")
                    for p in prep_pieces(0, QT2[0]):
                        run_piece(p)
                    for tt in range(ntile):
                        QT = QT2[tt % 2]
                        pend = prep_pieces(tt + 1, QT2[(tt + 1) % 2]) if tt + 1 < ntile else []
                        every = max(1, (16 * nkc) // (len(pend) + 1)) if pend else 0
                        for mo in range(2):
                            c.dma("sp", wob[mo].t[:], S["w_aout"][ai, mo].rearrange("p kc j -> p (kc j)"), R=[self.wr((("w_aout", ai), mo))], W=[wob[mo].R()])
                        c.dma("sp", xr2[0].t[:], self.XT[s][0:128, tt * NT:(tt + 1) * NT], W=[xr2[0].R()])
                        it = 0
                        for h in range(16):
                            m, half, kv = h // 2, h % 2, h // 4
                            hs = slice(half * 64, (half + 1) * 64)
                            po = self.pacc[h % 2]
                            pS = {}

                            def qk(kc):
                                p = c.psum()
                                pS[kc] = p
                                c.op("pe", lambda e: e.matmul(p.t[:], KT.t[:, kv // 2, kc * 128:(kc + 1) * 128], QT.t[:, h, :], start=True, stop=True),
                                     R=[KT.R(), QT.R(h)], W=[p.R()])

                            qk(0)
                            if nkc > 1:
                                qk(1)
                            for kc in range(nkc):
                                p = pS.pop(kc)
                                pt = PT[kc % 3]
                                c.op("act", lambda e: e.activation(out=pt.t[:], in_=p.t[:], func=AF.Exp), R=[p.R()], W=[pt.R()])
                                if kc + 2 < nkc:
                                    qk(kc + 2)
                                c.op("pe", lambda e: e.matmul(po.t[:], Vx.t[:, kc, kv, :], pt.t[:], start=(kc == 0), stop=(kc == nkc - 1)),
                                     R=[Vx.R(), pt.R()], W=[po.R()])
                                it += 1
                                if pend and it % every == 0:
                                    run_piece(pend.pop(0))
                            rc = rec[h % 2]
                            c.op("dve", lambda e: e.reciprocal(out=rc.t[0:64, :], in_=po.t[64:128, :]), R=[po.R()], W=[rc.R()])
                            c.op("dve", lambda e: e.tensor_tensor(out=OT.t[hs, m, :], in0=po.t[0:64, :], in1=rc.t[0:64, :], op=ALU.mult),
                                 R=[po.R(), rc.R()], W=[OT.R()])
                        while pend:
                            run_piece(pend.pop(0))
                        for mo in range(8):
                            wb = wob[mo % 2]
                            xr = xr2[mo % 2]
                            if mo + 1 < 8:
                                c.dma("sp", xr2[(mo + 1) % 2].t[:], self.XT[s][(mo + 1) * 128:(mo + 2) * 128, tt * NT:(tt + 1) * NT], W=[xr2[(mo + 1) % 2].R()])
                            ps = c.psum()
                            for kc in range(8):
                                c.op("pe", lambda e: e.matmul(ps.t[:], wb.t[:, kc * 128:(kc + 1) * 128], OT.t[:, kc, :], start=(kc == 0), stop=(kc == 7)),
                                     R=[wb.R(), OT.R()], W=[ps.R()], inc=(kc == 7))
                            if mo + 2 < 8:
                                c.dma("sp", wb.t[:], S["w_aout"][ai, mo + 2].rearrange("p kc j -> p (kc j)"), R=[self.wr((("w_aout", ai), mo + 2))], W=[wb.R()])
                            c.op("dve", lambda e: e.tensor_tensor(out=xr.t[:], in0=ps.t[:], in1=xr.t[:], op=ALU.add), R=[ps.R()], W=[xr.R()])
                            c.dma("pool", self.XT[s][mo * 128:(mo + 1) * 128, tt * NT:(tt + 1) * NT], xr.t[:], R=[xr.R()], W=[self.wr(("XTc", s, tt, mo))])
                    save("loop")
                    c.psum_banks = banks
                    c.psum_next = 0
                    c.barrier()


def host_inputs(T, NSEQ, layers, x_slots, inp):
    L = len(layers)
    f32 = np.float32
    m = {}
    m["x"] = np.ascontiguousarray(x_slots, dtype=f32)
    ng = np.asarray(inp["norm_g"], f32)[:L]
    m["norm_g"] = np.ascontiguousarray(ng.reshape(L, 3, 8, 128).transpose(3, 0, 1, 2).reshape(128, L * 24))
    for k in ("ffn_w_gate", "ffn_w_up", "ffn_w_down"):
        m[k] = np.ascontiguousarray(np.asarray(inp[k], f32)[:L])
    ns = max(sum(1 for l in layers if l[1] == "ssm"), 1)
    na = max(sum(1 for l in layers if l[1] == "attn"), 1)
    m["ssm_w_in"] = np.ascontiguousarray(np.asarray(inp["ssm_w_in"], f32)[:ns])
    cw = np.asarray(inp["ssm_conv_w"], f32)[:ns]
    m["ssm_conv_w"] = np.ascontiguousarray(cw.reshape(ns, 5, 32, 128).transpose(3, 0, 2, 1).reshape(128, ns * 32 * 5))
    cb = np.asarray(inp["ssm_conv_b"], f32)[:ns]
    m["ssm_conv_b"] = np.ascontiguousarray(cb.reshape(ns, 32, 128).transpose(2, 0, 1).reshape(128, ns * 32))
    dtb = np.asarray(inp["ssm_dt_bias"], f32)[:ns].reshape(ns, 64)
    m["ssm_dtb_col"] = np.ascontiguousarray(dtb.T)
    m["ssm_dtb_row"] = np.ascontiguousarray(dtb)
    m["ssm_alog_row"] = np.ascontiguousarray(np.asarray(inp["ssm_A_log"], f32)[:ns].reshape(ns, 64))
    Dv = np.asarray(inp["ssm_D"], f32)[:ns]
    m["ssm_D_col"] = np.ascontiguousarray(np.repeat(Dv, 64, axis=1).reshape(ns, 16, 128).transpose(2, 0, 1).reshape(128, ns * 16))
    ngm = np.asarray(inp["ssm_norm_g"], f32)[:ns]
    m["ssm_ng_col"] = np.ascontiguousarray(ngm.reshape(ns, 16, 128).transpose(2, 0, 1).reshape(128, ns * 16))
    m["ssm_w_out"] = np.ascontiguousarray(np.asarray(inp["ssm_w_out"], f32)[:ns])
    m["attn_w_qkv"] = np.ascontiguousarray(np.asarray(inp["attn_w_qkv"], f32)[:na])
    qg = np.asarray(inp["attn_q_norm"], f32)[:na]
    kg = np.asarray(inp["attn_k_norm"], f32)[:na]
    m["attn_qg_col"] = np.ascontiguousarray(np.concatenate([qg, qg], 1).T)
    m["attn_kg_col"] = np.ascontiguousarray(np.concatenate([kg, kg], 1).T)
    m["attn_w_out"] = np.ascontiguousarray(np.asarray(inp["attn_w_out"], f32)[:na])
    m["final_g"] = np.ascontiguousarray(np.asarray(inp["final_norm"], f32).reshape(8, 128).T)
    m["ident"] = np.eye(128, dtype=f32)
    cosT, sinT, Rm = rope_tables(T)
    m["cosT"], m["sinT"], m["Rm"] = cosT, sinT, Rm
    jj, ii = np.meshgrid(np.arange(128), np.arange(128), indexing="ij")
    m["tri"] = (jj <= ii).astype(f32)
    m["triT"] = (jj >= ii).astype(f32)
    sel = np.zeros((32, 32, 128), f32)
    for r in range(32):
        sel[r, r, :] = 1.0
    m["sel"] = sel.reshape(32, 32 * 128)
    m["sel3"] = np.ascontiguousarray(np.concatenate([sel, sel, sel], 0).reshape(96, 32 * 128))
    return m


FULL_LAYERS = [(True, "ssm", True), (True, "attn", True), (True, "ssm", True), (True, "attn", True)]
_CACHE = {}


def run(T, NSEQ, layers, slots_per_core, inp, final=True):
    key = (T, NSEQ, tuple(layers), final)
    if key not in _CACHE:
        _CACHE[key] = Builder(T, NSEQ, layers, final).build()
    nc = _CACHE[key]
    in_maps = [host_inputs(T, NSEQ, layers, xs, inp) for xs in slots_per_core]
    res = run_bass_kernel_spmd(nc, in_maps, core_ids=list(range(len(slots_per_core))))
    return [r["y"] for r in res.results]


def kernel(**inputs):
    xp = np.asarray(inputs["x_prompt"], np.float32)
    xs = np.asarray(inputs["x_sample"], np.float32)
    seqs = [xp[i] for i in range(xp.shape[0])] + [xs[i] for i in range(xs.shape[0])]
    T = xp.shape[1]
    slots = []
    for cidx in range(8):
        a = seqs[cidx]
        b = seqs[8 + cidx] if 8 + cidx < len(seqs) else seqs[cidx]
        slots.append(np.stack([a, b], 0))
    ys = run(T, 2, FULL_LAYERS, slots, inputs)
    out = [None] * len(seqs)
    for cidx in range(8):
        out[cidx] = ys[cidx][0]
        if 8 + cidx < len(seqs):
            out[8 + cidx] = ys[cidx][1]
    nP = xp.shape[0]
    y_prompt = np.stack(out[:nP], 0).astype(np.float32)
    y_sample = np.stack(out[nP:], 0).astype(np.float32)
    return (y_prompt, y_sample)
```
